# Optimizing a Trainium2 kernel written in Bass

```python
import jax, jax.numpy as jnp
from jax import lax
import numpy as np

D_MODEL = 1024
BATCH = 2
SEQ = 16384
DEPTH = 1

CHUNK = 64
Q_BLOCK = 2 * CHUNK
ATTN_WIDTH = D_MODEL // 2
CONV_WIDTH = D_MODEL - ATTN_WIDTH
HEAD_DIM = 64
N_HEADS = ATTN_WIDTH // HEAD_DIM
CONV_KERNEL = 31
D_FF = 4 * D_MODEL
IN_WIDTH = 3 * ATTN_WIDTH + N_HEADS + 2 * CONV_WIDTH
EPS = 1e-6

kernel_name = "hybrid_fox_conformer_conv_adaln_block"


def _rmsnorm(x, g):
    xf = x.astype(jnp.float32)
    y = xf * lax.rsqrt(jnp.mean(xf * xf, axis=-1, keepdims=True) + EPS)
    return (y * g.astype(jnp.float32)).astype(x.dtype)


def _layernorm(x, g, b):
    xf = x.astype(jnp.float32)
    mu = jnp.mean(xf, axis=-1, keepdims=True)
    var = jnp.mean(jnp.square(xf - mu), axis=-1, keepdims=True)
    y = (xf - mu) * lax.rsqrt(var + EPS)
    return (y * g.astype(jnp.float32) + b.astype(jnp.float32)).astype(x.dtype)


def _fox_attention(q, k, v, log_f):
    B, S, H, Dh = q.shape
    nb = S // Q_BLOCK
    F = jnp.cumsum(log_f, axis=1).transpose(0, 2, 1)
    qh = q.transpose(0, 2, 1, 3)
    kh = k.transpose(0, 2, 1, 3)
    vh = v.transpose(0, 2, 1, 3)
    q_blocks = qh.reshape(B, H, nb, Q_BLOCK, Dh).transpose(2, 0, 1, 3, 4)
    F_blocks = F.reshape(B, H, nb, Q_BLOCK).transpose(2, 0, 1, 3)
    k_pos = jnp.arange(S)
    scale = HEAD_DIM ** -0.5

    def block(args):
        qb, Fb, i = args
        logits = jnp.einsum('bhqd,bhkd->bhqk', qb, kh).astype(jnp.float32) * scale
        logits = logits + Fb[..., None] - F[:, :, None, :]
        q_pos = i * Q_BLOCK + jnp.arange(Q_BLOCK)
        mask = k_pos[None, :] <= q_pos[:, None]
        logits = jnp.where(mask[None, None], logits, -jnp.inf)
        p = jax.nn.softmax(logits, axis=-1)
        return jnp.einsum('bhqk,bhkd->bhqd', p.astype(vh.dtype), vh)

    out = lax.map(block, (q_blocks, F_blocks, jnp.arange(nb)))
    return out.transpose(1, 0, 3, 2, 4).reshape(B, S, H * Dh)


def _causal_depthwise_conv(u, w, b):
    K, C = w.shape
    u_pad = jnp.pad(u, ((0, 0), (K - 1, 0), (0, 0)))
    y = lax.conv_general_dilated(u_pad, w[:, None, :], window_strides=(1,), padding='VALID',
                                 dimension_numbers=('NWC', 'WIO', 'NWC'),
                                 feature_group_count=C)
    return y + b


def setup_inputs(seed: int = 0) -> dict:
    key = jax.random.key(seed)
    ks = jax.random.split(key, 20)
    f32 = jnp.float32
    L, D, A, Cw, H = DEPTH, D_MODEL, ATTN_WIDTH, CONV_WIDTH, N_HEADS
    nrm = lambda k, shape, s: jax.random.normal(k, shape, f32) * s
    return {
        "x": jax.random.normal(ks[0], (BATCH, SEQ, D), f32),
        "c": jax.random.normal(ks[1], (BATCH, D), f32),
        "w_ada": nrm(ks[2], (L, D, 6 * D), 0.5 * D ** -0.5),
        "b_ada": nrm(ks[3], (L, 6 * D), 0.02),
        "norm1_g": 1.0 + nrm(ks[4], (L, D), 0.02),
        "w_in": nrm(ks[5], (L, D, IN_WIDTH), D ** -0.5),
        "q_norm_g": 1.0 + nrm(ks[6], (L, HEAD_DIM), 0.02),
        "k_norm_g": 1.0 + nrm(ks[7], (L, HEAD_DIM), 0.02),
        "b_f": jax.random.uniform(ks[8], (L, H), f32, minval=1.0, maxval=6.0),
        "conv_w": nrm(ks[9], (L, CONV_KERNEL, Cw), CONV_KERNEL ** -0.5),
        "conv_b": nrm(ks[10], (L, Cw), 0.02),
        "conv_ln_g": 1.0 + nrm(ks[11], (L, Cw), 0.02),
        "conv_ln_b": nrm(ks[12], (L, Cw), 0.02),
        "beta_attn": 1.0 + nrm(ks[13], (L, A), 0.02),
        "beta_conv": 1.0 + nrm(ks[14], (L, Cw), 0.02),
        "w_out": nrm(ks[15], (L, D, D), D ** -0.5),
        "norm2_g": 1.0 + nrm(ks[16], (L, D), 0.02),
        "w_ff1": nrm(ks[17], (L, D, D_FF), D ** -0.5),
        "w_ff2": nrm(ks[18], (L, D_FF, D), D_FF ** -0.5),
    }


def reference(x, c, w_ada, b_ada, norm1_g, w_in, q_norm_g, k_norm_g, b_f, conv_w, conv_b,
              conv_ln_g, conv_ln_b, beta_attn, beta_conv, w_out, norm2_g, w_ff1, w_ff2):
    B, S, D = x.shape
    A, H = ATTN_WIDTH, N_HEADS
    split_pts = [A, 2 * A, 3 * A, 3 * A + H]
    for l in range(DEPTH):
        mod = jax.nn.silu(c) @ w_ada[l] + b_ada[l]
        sh1, sc1, g1, sh2, sc2, g2 = [m[:, None, :] for m in jnp.split(mod, 6, axis=-1)]

        h = _rmsnorm(x, norm1_g[l]) * (1 + sc1) + sh1
        z = h @ w_in[l]
        q, k, v, fg, conv_in = jnp.split(z, split_pts, axis=-1)

        q = _rmsnorm(q.reshape(B, S, H, HEAD_DIM), q_norm_g[l])
        k = _rmsnorm(k.reshape(B, S, H, HEAD_DIM), k_norm_g[l])
        v = v.reshape(B, S, H, HEAD_DIM)
        log_f = jax.nn.log_sigmoid(fg.astype(jnp.float32) + b_f[l].astype(jnp.float32))
        attn = _fox_attention(q, k, v, log_f)

        a_lin, a_gate = jnp.split(conv_in, 2, axis=-1)
        u = a_lin * jax.nn.sigmoid(a_gate)
        u = _causal_depthwise_conv(u, conv_w[l], conv_b[l])
        u = jax.nn.silu(_layernorm(u, conv_ln_g[l], conv_ln_b[l]))

        merged = jnp.concatenate([_rmsnorm(attn, beta_attn[l]), _rmsnorm(u, beta_conv[l])], axis=-1)
        x = x + g1 * (merged @ w_out[l])

        h = _rmsnorm(x, norm2_g[l]) * (1 + sc2) + sh2
        x = x + g2 * (jnp.square(jax.nn.relu(h @ w_ff1[l])) @ w_ff2[l])
    return x
```

```python
import numpy as np
import ml_dtypes
from contextlib import ExitStack
import concourse.bass as bass
import concourse.mybir as mybir
from concourse.bass_utils import run_bass_kernel_spmd

F32 = mybir.dt.float32
BF16 = mybir.dt.bfloat16
AF = mybir.ActivationFunctionType
ALU = mybir.AluOpType
AX = mybir.AxisListType

D = 1024
S = 16384
NT = 32
TN = 512
NOWN = 8
HALO = 30
TNH = TN + HALO
H = 8
DH = 64
DFF = 4096
EPS = 1e-6
NEG = -30000.0
KW = 31


def own_tiles(j):
    out = []
    for m in range(4):
        out += [8 * m + j, 8 * m + 7 - j]
    return out


def kmax_of(i):
    m = i // 2
    return 8 * m + 3 if i % 2 == 0 else 8 * m + 7


def rbase_of(i):
    m = i // 2
    return 8 * m if i % 2 == 0 else 8 * m + 4


class Tok:
    __slots__ = ("sem", "val")

    def __init__(self, sem, val):
        self.sem = sem
        self.val = val


class Res:
    def __init__(self, name):
        self.name = name
        self.w = []
        self.r = []
        self.dsem = None
        self.dcnt = 0

    @staticmethod
    def _add(lst, tok):
        for k, t in enumerate(lst):
            if t.sem is tok.sem:
                if tok.val > t.val:
                    lst[k] = tok
                return
        lst.append(tok)

    def add_reader(self, tok):
        Res._add(self.r, tok)

    def add_writer(self, tok):
        Res._add(self.w, tok)


class Eng:
    def __init__(self, name, sem, same_sync):
        self.name = name
        self.sem = sem
        self.same = same_sync
        self.cnt = 0
        self.q = []
        self.seen = {}
        self.relax = False

    def _wait(self, t):
        if t.sem is self.sem and (not self.same or (self.relax and self.cnt - t.val >= 3)):
            return
        key = id(t.sem)
        if self.seen.get(key, 0) >= t.val:
            return
        self.seen[key] = t.val
        self.q.append(lambda e, s=t.sem, v=t.val: e.wait_ge(s, v))

    def wait_deps(self, reads, writes, adds):
        for r in reads:
            for t in r.w:
                self._wait(t)
        for w in writes:
            for t in w.w:
                self._wait(t)
            for t in w.r:
                self._wait(t)
        for a in adds:
            for t in a.r:
                self._wait(t)

    def run(self, fn, reads=(), writes=(), adds=()):
        self.wait_deps(reads, writes, adds)
        self.cnt += 1
        sem = self.sem
        self.q.append(lambda e, fn=fn, sem=sem: fn(e).then_inc(sem, 1))
        tok = Tok(sem, self.cnt)
        for r in reads:
            r.add_reader(tok)
        for w in writes:
            w.w = [tok]
            w.r = []
        for a in adds:
            a.add_writer(tok)
        return tok


class KB:
    def __init__(self, nc, es):
        self.nc = nc
        self.es = es
        self.engs = {}
        for name, same in [("pe", False), ("act", True), ("dve", True), ("pool", True), ("sp", False)]:
            sem = es.enter_context(nc.semaphore("s_" + name))
            self.engs[name] = Eng(name, sem, same)
        self.pe = self.engs["pe"]
        self.act = self.engs["act"]
        self.dve = self.engs["dve"]
        self.pool = self.engs["pool"]
        self.sp = self.engs["sp"]
        self.dma_res = []
        self.nres = 0

    def res(self, name, dma=False):
        r = Res(name)
        if dma:
            r.dsem = self.es.enter_context(self.nc.semaphore("d%d_%s" % (self.nres, name)))
            self.dma_res.append(r)
        self.nres += 1
        return r

    def dma(self, qname, out, in_, sres, reads=(), writes=(), adds=()):
        q = self.engs[qname]
        q.wait_deps(reads, writes, adds)
        sres.dcnt += 16
        sem = sres.dsem
        tok = Tok(sem, sres.dcnt)
        q.q.append(lambda e, o=out, i=in_, sem=sem: e.dma_start(out=o, in_=i).then_inc(sem, 16))
        for r in reads:
            r.add_reader(tok)
        for w in writes:
            w.w = [tok]
            w.r = []
        for a in adds:
            a.add_writer(tok)
        return tok

    def barrier(self):
        toks = [Tok(e.sem, e.cnt) for e in self.engs.values() if e.cnt > 0]
        toks += [Tok(r.dsem, r.dcnt) for r in self.dma_res if r.dcnt > 0]
        for e in self.engs.values():
            for t in toks:
                if t.sem is e.sem:
                    continue
                e._wait(t)


class Arena:
    def __init__(self, ap, words):
        self.ap = ap
        self.words = words
        self.off = 0

    def mark(self):
        return self.off

    def reset(self, m):
        self.off = m

    def alloc(self, shape, dt):
        n = int(np.prod(shape))
        nw = n if dt == F32 else (n + 1) // 2
        assert self.off + nw <= self.words, ("arena overflow", self.off, nw, self.words)
        v = self.ap[:, self.off:self.off + nw]
        self.off += nw
        if dt != F32:
            v = v.bitcast(dt)
            if v.shape[1] != n:
                v = v[:, 0:n]
        if len(shape) == 2:
            return v.rearrange("p (a b) -> p a b", a=shape[0])
        if len(shape) == 3:
            return v.rearrange("p (a b c) -> p a b c", a=shape[0], b=shape[1])
        return v


def build_nc(debug=False):
    nc = bass.Bass("TRN2", target_bir_lowering=False)

    def DI(name, shape, dt=F32):
        return nc.dram_tensor(name, list(shape), dt, kind="ExternalInput").ap()

    def DS(name, shape, dt):
        return nc.dram_tensor(name, list(shape), dt, kind="ExternalOutput" if debug else "Internal").ap()

    xT = DI("xT", [D, S])
    xqT = DI("xqT", [D, NOWN * TNH])
    ccol_d = DI("ccol", [128, 8])
    wada_d = DI("wada", [D, 6 * D])
    bada_d = DI("badacol", [128, 48])
    n1g_d = DI("n1gcol", [128, 8])
    n2g_d = DI("n2gcol", [128, 8])
    wq_d = DI("wq", [D, 512])
    wk_d = DI("wk", [D, 512])
    wv_d = DI("wv", [D, 512])
    wfg_d = DI("wfg", [D, 8])
    wc_d = DI("wc", [D, 1024])
    qg_d = DI("qgcol", [128, 1])
    kg_d = DI("kgcol", [128, 1])
    bf_d = DI("bfrep", [128, 32])
    cw_d = DI("cwcol", [128, 4 * KW])
    cb_d = DI("cbcol", [128, 4])
    clg_d = DI("clgcol", [128, 4])
    clb_d = DI("clbcol", [128, 4])
    bc_d = DI("bccol", [128, 4])
    ba_d = DI("bapcol", [128, 4])
    wo_d = DI("wo", [D, D])
    w1_d = DI("w1", [D, DFF])
    w2_d = DI("w2", [DFF, D])
    ident_d = DI("ident", [128, 128])
    tri_d = DI("tri", [128, 128])
    mtri_d = DI("mtri", [128, 4 * TN], BF16)
    dsel_d = DI("diagsel", [128, 32 * 128], BF16)
    cmask_d = DI("cmask", [128, NOWN * NT])
    oh_d = DI("ohrep", [128, NOWN * H * NT])
    hs_d = DI("haloscale", [128, NOWN])
    outT = nc.dram_tensor("outT", [D, NOWN * TN], F32, kind="ExternalOutput").ap()

    KA = DS("KA", [H, 66, S], BF16)
    VA = DS("VA", [H, 128, 128, 65], BF16)
    QA = DS("QA", [H, 66, NOWN * TN], BF16)
    MC = DS("MC", [512, NOWN * TN], BF16)
    AT = DS("AT", [H, 65, NOWN * TN], F32)
    X1 = DS("X1", [D, NOWN * TN], F32)
    H2 = DS("H2", [D, NOWN * TN], BF16)

    es = ExitStack()
    E = es.enter_context
    AW = 52800
    arena_t = E(nc.sbuf_tensor("arena", [128, AW], F32))
    psum_t = E(nc.psum_tensor("psum", [128, 4096], F32))
    k = KB(nc, es)
    ar = Arena(arena_t[:, :], AW)

    def PS(bank, nb=1):
        return psum_t[:, bank * 512:(bank + nb) * 512]

    psres = [k.res("psb%d" % i) for i in range(8)]
    pe, act, dve, pool = k.pe, k.act, k.dve, k.pool

    cst = ar.alloc((8,), F32)
    r_cst = k.res("cst")
    pool.run(lambda e: e.memset(cst[:, 0:1], EPS), writes=[r_cst])
    pool.run(lambda e: e.memset(cst[:, 1:2], float(np.log(0.125))), adds=[r_cst])
    pool.run(lambda e: e.memset(cst[:, 2:3], 1.0), adds=[r_cst])
    pool.run(lambda e: e.memset(cst[:, 3:4], 0.0), adds=[r_cst])
    c_eps, c_ln8, c_one, c_zero = cst[:, 0:1], cst[:, 1:2], cst[:, 2:3], cst[:, 3:4]

    r_consts = k.res("consts", dma=True)

    def cload(shape, dt, src, q="sp"):
        v = ar.alloc(shape, dt)
        k.dma(q, v if len(shape) > 1 else v, src, r_consts, adds=[r_consts])
        return v

    ccol = ar.alloc((8,), F32); k.dma("sp", ccol, ccol_d, r_consts, adds=[r_consts])
    bada = ar.alloc((48,), F32); k.dma("sp", bada, bada_d, r_consts, adds=[r_consts])
    n1g = ar.alloc((8,), F32); k.dma("sp", n1g, n1g_d, r_consts, adds=[r_consts])
    n2g = ar.alloc((8,), F32); k.dma("sp", n2g, n2g_d, r_consts, adds=[r_consts])
    qg = ar.alloc((1,), F32); k.dma("sp", qg, qg_d, r_consts, adds=[r_consts])
    kg = ar.alloc((1,), F32); k.dma("sp", kg, kg_d, r_consts, adds=[r_consts])
    bfr = ar.alloc((32,), F32); k.dma("sp", bfr, bf_d, r_consts, adds=[r_consts])
    cwc = ar.alloc((4 * KW,), F32); k.dma("sp", cwc, cw_d, r_consts, adds=[r_consts])
    cbc = ar.alloc((4,), F32); k.dma("sp", cbc, cb_d, r_consts, adds=[r_consts])
    clg = ar.alloc((4,), F32); k.dma("sp", clg, clg_d, r_consts, adds=[r_consts])
    clb = ar.alloc((4,), F32); k.dma("sp", clb, clb_d, r_consts, adds=[r_consts])
    bcc = ar.alloc((4,), F32); k.dma("sp", bcc, bc_d, r_consts, adds=[r_consts])
    bac = ar.alloc((4,), F32); k.dma("sp", bac, ba_d, r_consts, adds=[r_consts])
    ident = ar.alloc((128,), F32); k.dma("sp", ident, ident_d, r_consts, adds=[r_consts])
    tri = ar.alloc((128,), F32); k.dma("sp", tri, tri_d, r_consts, adds=[r_consts])
    cmask = ar.alloc((NOWN, NT), F32); k.dma("sp", cmask, cmask_d.rearrange("p (a b) -> p a b", a=NOWN), r_consts, adds=[r_consts])
    hsc = ar.alloc((NOWN,), F32); k.dma("sp", hsc, hs_d, r_consts, adds=[r_consts])

    onesf = ar.alloc((128,), F32)
    onesb = ar.alloc((128,), BF16)
    blk1 = ar.alloc((128,), BF16)
    r_ones = k.res("ones")
    pool.run(lambda e: e.memset(onesf, 1.0), writes=[r_ones])
    pool.run(lambda e: e.memset(onesb, 1.0), adds=[r_ones])
    pool.run(lambda e: e.memset(blk1, 0.0), adds=[r_ones])
    pool.run(lambda e: e.memset(blk1[0:64, 0:64], 1.0), reads=[r_ones], adds=[r_ones])
    pool.run(lambda e: e.memset(blk1[64:128, 64:128], 1.0), adds=[r_ones])

    modc = ar.alloc((48,), F32)
    gm1 = ar.alloc((8,), F32)
    gm2 = ar.alloc((8,), F32)
    r_mod = k.res("mod")
    sh1, g1c, sh2, g2c = modc[:, 0:8], modc[:, 16:24], modc[:, 24:32], modc[:, 40:48]
    GRt = ar.alloc((NT + 1, H), F32)
    r_GR = k.res("GR")
    pool.run(lambda e: e.memset(GRt[:, 0, :], 0.0), writes=[r_GR])

    pmark = ar.mark()
    xqT_v = xqT.rearrange("(kc p) (i n) -> p kc i n", p=128, n=TNH)
    AT_p = AT.rearrange("(pr two) d t -> two d pr t", two=2)
    MC_v = MC.rearrange("(c p) t -> p c t", p=128)
    X1_v = X1.rearrange("(kc p) t -> p kc t", p=128)
    H2_v = H2.rearrange("(kc p) t -> p kc t", p=128)
    r_KA = k.res("KA"); r_VA = k.res("VA"); r_QA = k.res("QA"); r_MC = k.res("MC"); r_AT = k.res("AT"); r_X1 = k.res("X1"); r_H2 = k.res("H2")

    def phase0():
        sccol = ar.alloc((8,), F32)
        tmp8 = ar.alloc((8,), F32)
        r_sc = k.res("sc")
        act.run(lambda e: e.activation(out=tmp8, in_=ccol, func=AF.Exp, bias=c_zero, scale=-1.0), reads=[r_consts, r_cst], writes=[r_sc])
        dve.run(lambda e: e.tensor_scalar(out=tmp8, in0=tmp8, scalar1=1.0, scalar2=None, op0=ALU.add), writes=[r_sc])
        dve.run(lambda e: e.reciprocal(out=tmp8, in_=tmp8), writes=[r_sc])
        dve.run(lambda e: e.tensor_tensor(out=sccol, in0=ccol, in1=tmp8, op=ALU.mult), writes=[r_sc])
        wsec = [ar.alloc((8, 1024), BF16) for _ in range(2)]
        scb = ar.alloc((8,), BF16)
        dve.run(lambda e: e.tensor_copy(out=scb, in_=sccol), writes=[r_sc])
        r_wsec = [k.res("wsec%d" % i, dma=True) for i in range(2)]
        wada_v = wada_d.rearrange("(kc p) n -> p kc n", p=128)
        psm = PS(0)
        for sec in range(6):
            b = sec % 2
            for half in range(2):
                k.dma("pool", wsec[b][:, half * 4:(half + 1) * 4, :], wada_v[:, half * 4:(half + 1) * 4, sec * 1024:(sec + 1) * 1024], r_wsec[b],
                      writes=[r_wsec[b]] if half == 0 else (), adds=() if half == 0 else [r_wsec[b]])

            def mm_sec(e, sec=sec, b=b):
                last = None
                for oc in range(8):
                    col = sec * 8 + oc
                    for kc in range(8):
                        last = e.matmul(psm[:, col:col + 1], lhsT=wsec[b][:, kc, oc * 128:(oc + 1) * 128], rhs=scb[:, kc:kc + 1],
                                        start=(kc == 0), stop=(kc == 7))
                return last
            pe.run(mm_sec, reads=[r_wsec[b], r_sc], writes=[psres[0]] if sec == 0 else (), adds=() if sec == 0 else [psres[0]])
        dve.run(lambda e: e.tensor_tensor(out=modc, in0=psm[:, 0:48], in1=bada, op=ALU.add), reads=[psres[0], r_consts], writes=[r_mod])
        dve.run(lambda e: e.tensor_scalar(out=gm1, in0=modc[:, 8:16], scalar1=1.0, scalar2=None, op0=ALU.add), reads=[r_mod], writes=[r_sc])
        dve.run(lambda e: e.tensor_tensor(out=gm1, in0=gm1, in1=n1g, op=ALU.mult), writes=[r_sc])
        dve.run(lambda e: e.tensor_scalar(out=gm2, in0=modc[:, 32:40], scalar1=1.0, scalar2=None, op0=ALU.add), writes=[r_sc])
        dve.run(lambda e: e.tensor_tensor(out=gm2, in0=gm2, in1=n2g, op=ALU.mult), writes=[r_sc, r_mod])
        k.barrier()
        ar.reset(pmark)

    phase0()
    def norm_mod(xt, r_x, N, gm, sh, sq, r_sq, rstd, r_rstd, tmpf, r_tmpf, hT, r_hT, ps_bank):
        pss = PS(ps_bank, 2)
        r_ps = [psres[ps_bank], psres[ps_bank + 1]]
        dve.run(lambda e: e.tensor_tensor(out=sq, in0=xt, in1=xt, op=ALU.mult), reads=[r_x], writes=[r_sq])

        def mm(e):
            last = None
            for kc in range(8):
                last = e.matmul(pss[:, 0:min(N, 512)], lhsT=onesb, rhs=sq[:, kc, 0:min(N, 512)], start=(kc == 0), stop=(kc == 7))
            if N > 512:
                for kc in range(8):
                    last = e.matmul(pss[:, 512:N], lhsT=onesb, rhs=sq[:, kc, 512:N], start=(kc == 0), stop=(kc == 7))
            return last
        pe.run(mm, reads=[r_sq, r_ones], writes=r_ps)
        act.run(lambda e: e.activation(out=rstd, in_=pss[:, 0:N], func=AF.Ln, bias=c_eps, scale=1.0 / D), reads=r_ps, writes=[r_rstd])
        act.run(lambda e: e.activation(out=rstd, in_=rstd, func=AF.Exp, bias=c_zero, scale=-0.5), writes=[r_rstd])
        for kc in range(8):
            tb = kc % 2
            dve.run(lambda e, kc=kc, tb=tb: e.tensor_tensor(out=tmpf[tb], in0=xt[:, kc, :], in1=rstd, op=ALU.mult),
                    reads=[r_x, r_rstd], writes=[r_tmpf[tb]])
            act.run(lambda e, kc=kc, tb=tb: e.activation(out=hT[:, kc, :], in_=tmpf[tb], func=AF.Identity, bias=sh[:, kc:kc + 1], scale=gm[:, kc:kc + 1]),
                    reads=[r_tmpf[tb], r_mod], writes=[r_hT] if kc == 0 else (), adds=() if kc == 0 else [r_hT])

    def pair_norm(ps_raw, r_raw, gcol, lnbias, raw, r_rawsb, sqb, r_sqb, ps_ss, r_ss, rs, r_rs, out, r_out, first, steps=("sq", "rest")):
        if "sq" in steps:
            act.run(lambda e: e.activation(out=sqb, in_=ps_raw, func=AF.Square, bias=c_zero, scale=1.0), reads=[r_raw, r_cst], writes=[r_sqb])
        if "rest" in steps:
            pe.run(lambda e: e.matmul(ps_ss, lhsT=blk1, rhs=sqb, start=True, stop=True), reads=[r_sqb, r_ones], writes=[r_ss])
            act.run(lambda e: e.activation(out=rs, in_=ps_ss, func=AF.Ln, bias=c_eps, scale=1.0 / DH), reads=[r_ss, r_cst], writes=[r_rs])
            act.run(lambda e: e.activation(out=rs, in_=rs, func=AF.Exp, bias=lnbias, scale=-0.5), writes=[r_rs])
            dve.run(lambda e: e.scalar_tensor_tensor(out=out, in0=ps_raw, scalar=gcol, in1=rs, op0=ALU.mult, op1=ALU.mult),
                    reads=[r_raw, r_rs, r_consts], writes=[r_out] if first else (), adds=() if first else [r_out])

    def fgate(hT, r_hT, wfg, r_w, ps_fg, r_psfg, zt, r_zt, spt, r_spt, ps_gk, r_psgk, ps_tot, r_pstot, want_tot, part=0):
        def mm(e):
            last = None
            for b in range(4):
                for kc in range(8):
                    last = e.matmul(ps_fg[:, b * 8:(b + 1) * 8], lhsT=hT[:, kc, b * 128:(b + 1) * 128], rhs=wfg[:, kc, :],
                                    start=(kc == 0), stop=(kc == 7))
            return last
        if part in (0, 1):
            pe.run(mm, reads=[r_hT, r_w], writes=[r_psfg])
            dve.run(lambda e: e.tensor_tensor(out=zt, in0=ps_fg[:, 0:32], in1=bfr, op=ALU.add), reads=[r_psfg, r_consts], writes=[r_zt])
            act.run(lambda e: e.activation(out=zt, in_=zt, func=AF.Exp, bias=c_zero, scale=-1.0), reads=[r_cst], writes=[r_zt])
            act.run(lambda e: e.activation(out=spt, in_=zt, func=AF.Ln, bias=c_one, scale=1.0), reads=[r_zt], writes=[r_spt])
        if part == 1:
            return

        def mm2(e):
            last = None
            for b in range(4):
                last = e.matmul(ps_gk[0:8, b * 128:(b + 1) * 128], lhsT=spt[:, b * 8:(b + 1) * 8], rhs=tri, start=True, stop=(b == 0))
                for b2 in range(b):
                    last = e.matmul(ps_gk[0:8, b * 128:(b + 1) * 128], lhsT=spt[:, b2 * 8:(b2 + 1) * 8], rhs=onesf, start=False, stop=(b2 == b - 1))
            return last
        pe.run(mm2, reads=[r_spt, r_consts, r_ones], writes=[r_psgk])
        if want_tot:
            def mm3(e):
                last = None
                for b in range(4):
                    last = e.matmul(ps_tot[:, 0:8], lhsT=onesf, rhs=spt[:, b * 8:(b + 1) * 8], start=(b == 0), stop=(b == 3))
                return last
            pe.run(mm3, reads=[r_spt, r_ones], writes=[r_pstot])

    def phase1a():
        wk = ar.alloc((8, 512), BF16)
        wv = ar.alloc((8, 512), BF16)
        wfg = ar.alloc((8, 8), BF16)
        r_w1a = k.res("w1a", dma=True)
        k.dma("pool", wk, wk_d.rearrange("(kc p) n -> p kc n", p=128), r_w1a, adds=[r_w1a])
        k.dma("pool", wv, wv_d.rearrange("(kc p) n -> p kc n", p=128), r_w1a, adds=[r_w1a])
        k.dma("pool", wfg, wfg_d.rearrange("(kc p) n -> p kc n", p=128), r_w1a, adds=[r_w1a])
        xt = [ar.alloc((8, TN), F32) for _ in range(2)]
        r_xt = [k.res("xt%d" % i, dma=True) for i in range(2)]
        sq = ar.alloc((8, TN), BF16); r_sq = k.res("sq")
        rstd = ar.alloc((TN,), F32); r_rstd = k.res("rstd")
        tmpf = [ar.alloc((TN,), F32) for _ in range(2)]
        r_tmpf = [k.res("tmpf%d" % i) for i in range(2)]
        hT = [ar.alloc((8, TN), BF16) for _ in range(2)]
        r_hT = [k.res("hT%d" % i) for i in range(2)]
        raw = [ar.alloc((TN,), F32) for _ in range(2)]
        r_raw = [k.res("raw%d" % i) for i in range(2)]
        sqb = [ar.alloc((TN,), BF16) for _ in range(2)]
        r_sqb = [k.res("sqb%d" % i) for i in range(2)]
        rs = [ar.alloc((TN,), F32) for _ in range(2)]
        r_rs = [k.res("rs%d" % i) for i in range(2)]
        kn = [ar.alloc((4, TN), BF16) for _ in range(2)]
        r_kn = [k.res("kn%d" % i, dma=True) for i in range(2)]
        vaug = [ar.alloc((H, 4, 65), BF16) for _ in range(2)]
        r_vaug = [k.res("vaug%d" % i, dma=True) for i in range(2)]
        zt = ar.alloc((32,), F32); r_zt = k.res("zt")
        spt = ar.alloc((32,), F32); r_spt = k.res("spt")
        tot32 = ar.alloc((32,), F32); r_tot8 = k.res("tot8")
        ghl = [ar.alloc((2, TN), BF16) for _ in range(2)]
        r_ghl = [k.res("ghl%d" % i, dma=True) for i in range(2)]
        for b in range(2):
            pool.run(lambda e, b=b: e.memset(vaug[b], 1.0), writes=[r_vaug[b]])

        xT_v = xT.rearrange("(kc p) t -> p kc t", p=128)
        def load_xt(t):
            b = t % 2
            for half in range(2):
                k.dma("sp", xt[b][:, half * 4:(half + 1) * 4, :], xT_v[:, half * 4:(half + 1) * 4, t * TN:(t + 1) * TN], r_xt[b],
                      writes=[r_xt[b]] if half == 0 else (), adds=() if half == 0 else [r_xt[b]])
        def A_sq(t):
            b = t % 2
            dve.run(lambda e, b=b: e.tensor_tensor(out=sq, in0=xt[b], in1=xt[b], op=ALU.mult), reads=[r_xt[b]], writes=[r_sq])

        def A_ss(t):
            def mm(e):
                last = None
                for kc in range(8):
                    last = e.matmul(PS(0), lhsT=onesb, rhs=sq[:, kc, :], start=(kc == 0), stop=(kc == 7))
                return last
            pe.run(mm, reads=[r_sq, r_ones], writes=[psres[0]])

        def A_rstd(t):
            act.run(lambda e: e.activation(out=rstd, in_=PS(0), func=AF.Ln, bias=c_eps, scale=1.0 / D), reads=[psres[0], r_cst], writes=[r_rstd])
            act.run(lambda e: e.activation(out=rstd, in_=rstd, func=AF.Exp, bias=c_zero, scale=-0.5), writes=[r_rstd])

        def A_u(t, kcs):
            b = t % 2
            for kc in kcs:
                tb = kc % 2
                dve.run(lambda e, kc=kc, tb=tb, b=b: e.tensor_tensor(out=tmpf[tb], in0=xt[b][:, kc, :], in1=rstd, op=ALU.mult),
                        reads=[r_xt[b], r_rstd], writes=[r_tmpf[tb]])
                act.run(lambda e, kc=kc, tb=tb, b=b: e.activation(out=hT[b][:, kc, :], in_=tmpf[tb], func=AF.Identity, bias=sh1[:, kc:kc + 1], scale=gm1[:, kc:kc + 1]),
                        reads=[r_tmpf[tb], r_mod], writes=[r_hT[b]] if kc == 0 else (), adds=() if kc == 0 else [r_hT[b]])

        load_xt(0)
        load_xt(1)
        A_sq(0); A_ss(0); A_rstd(0); A_u(0, range(8))
        psvb = [5, 1]
        for t in range(NT):
            b = t % 2
            nxt = t + 1 < NT
            if t + 2 < NT:
                load_xt(t + 2)

            def mmk(pr, b=b):
                pb = pr % 2
                psk = PS(2 + pb)

                def f(e, pr=pr, psk=psk, b=b):
                    last = None
                    for kc in range(8):
                        last = e.matmul(psk, lhsT=wk[:, kc, pr * 128:(pr + 1) * 128], rhs=hT[b][:, kc, :], start=(kc == 0), stop=(kc == 7))
                    return last
                pe.run(f, reads=[r_hT[b], r_w1a], writes=[psres[2 + pb]])

            def ksq(pr):
                pb = pr % 2
                act.run(lambda e, pb=pb: e.activation(out=sqb[pb], in_=PS(2 + pb), func=AF.Square, bias=c_zero, scale=1.0),
                        reads=[psres[2 + pb], r_cst], writes=[r_sqb[pb]])

            def kss(pr):
                pb = pr % 2
                pe.run(lambda e, pb=pb: e.matmul(PS(4), lhsT=blk1, rhs=sqb[pb], start=True, stop=True), reads=[r_sqb[pb], r_ones], writes=[psres[4]])

            def klnexp(pr):
                pb = pr % 2
                act.run(lambda e, pb=pb: e.activation(out=rs[pb], in_=PS(4), func=AF.Ln, bias=c_eps, scale=1.0 / DH), reads=[psres[4], r_cst], writes=[r_rs[pb]])
                act.run(lambda e, pb=pb: e.activation(out=rs[pb], in_=rs[pb], func=AF.Exp, bias=c_zero, scale=-0.5), writes=[r_rs[pb]])

            def kn_(pr, b=b):
                pb = pr % 2
                dve.run(lambda e, pr=pr, pb=pb, b=b: e.scalar_tensor_tensor(out=kn[b][:, pr, :], in0=PS(2 + pb), scalar=kg[:, 0:1], in1=rs[pb], op0=ALU.mult, op1=ALU.mult),
                        reads=[psres[2 + pb], r_rs[pb], r_consts], writes=[r_kn[b]] if pr == 0 else (), adds=() if pr == 0 else [r_kn[b]])

            def mmv(blk, b=b):
                bank = psvb[blk % 2]
                psv = PS(bank)

                def f(e, blk=blk, b=b, psv=psv):
                    last = None
                    for kc in range(8):
                        last = e.matmul(psv, lhsT=hT[b][:, kc, blk * 128:(blk + 1) * 128], rhs=wv[:, kc, :], start=(kc == 0), stop=(kc == 7))
                    return last
                pe.run(f, reads=[r_hT[b], r_w1a], writes=[psres[bank]])

            def vcast(blk, b=b):
                bank = psvb[blk % 2]
                dve.run(lambda e, blk=blk, b=b, bank=bank: e.tensor_copy(out=vaug[b][:, :, blk, 0:64], in_=PS(bank).rearrange("p (h d) -> p h d", h=H)),
                        reads=[psres[bank]], writes=[r_vaug[b]] if blk == 0 else (), adds=() if blk == 0 else [r_vaug[b]])

            if nxt:
                A_sq(t + 1)
            ps_gk = PS(7)

            def fg_a(b=b):
                def mm(e, b=b):
                    last = None
                    for blk in range(4):
                        for kc in range(8):
                            last = e.matmul(PS(6)[:, blk * 8:(blk + 1) * 8], lhsT=hT[b][:, kc, blk * 128:(blk + 1) * 128], rhs=wfg[:, kc, :],
                                            start=(kc == 0), stop=(kc == 7))
                    return last
                pe.run(mm, reads=[r_hT[b], r_w1a], writes=[psres[6]])
                dve.run(lambda e: e.tensor_tensor(out=zt, in0=PS(6)[:, 0:32], in1=bfr, op=ALU.add), reads=[psres[6], r_consts], writes=[r_zt])
                act.run(lambda e: e.activation(out=zt, in_=zt, func=AF.Exp, bias=c_zero, scale=-1.0), reads=[r_cst], writes=[r_zt])
                act.run(lambda e: e.activation(out=spt, in_=zt, func=AF.Ln, bias=c_one, scale=1.0), reads=[r_zt], writes=[r_spt])

            def fg_b():
                def mm2(e):
                    last = None
                    for blk in range(4):
                        last = e.matmul(ps_gk[0:8, blk * 128:(blk + 1) * 128], lhsT=spt[:, blk * 8:(blk + 1) * 8], rhs=tri, start=True, stop=(blk == 0))
                        for b2 in range(blk):
                            last = e.matmul(ps_gk[0:8, blk * 128:(blk + 1) * 128], lhsT=spt[:, b2 * 8:(b2 + 1) * 8], rhs=onesf, start=False, stop=(b2 == blk - 1))
                    return last
                pe.run(mm2, reads=[r_spt, r_consts, r_ones], writes=[psres[7]])

            mmk(0); mmk(1); fg_a(); ksq(0); mmv(0); ksq(1); kss(0)
            if nxt:
                A_ss(t + 1)
            klnexp(0); vcast(0); mmv(1); kn_(0); kss(1); klnexp(1)
            if nxt:
                A_rstd(t + 1)
            vcast(1); mmk(2); kn_(1); mmk(3); ksq(2); mmv(2); ksq(3)
            if nxt:
                A_u(t + 1, range(0, 4))
            kss(2); klnexp(2); vcast(2); mmv(3); kn_(2); kss(3); klnexp(3); vcast(3); kn_(3)
            if nxt:
                A_u(t + 1, range(4, 8))
            for pr in range(4):
                for hh in range(2):
                    k.dma("sp", KA[2 * pr + hh, 0:64, t * TN:(t + 1) * TN], kn[b][hh * 64:(hh + 1) * 64, pr, :], r_kn[b], reads=[r_kn[b]], adds=[r_KA])
            k.dma("sp", VA[:, :, t * 4:(t + 1) * 4, :].rearrange("h p k e -> p h k e"), vaug[b], r_vaug[b], reads=[r_vaug[b]], adds=[r_VA])
            fg_b()
            pe.run(lambda e: e.matmul(PS(4)[:, 0:32], lhsT=onesf, rhs=spt[:, 0:32], start=True, stop=True), reads=[r_spt, r_ones], writes=[psres[4]])
            dve.run(lambda e: e.tensor_copy(out=tot32, in_=PS(4)[:, 0:32]), reads=[psres[4]], writes=[r_tot8])
            dve.run(lambda e: e.tensor_tensor(out=tot32[:, 0:16], in0=tot32[:, 0:16], in1=tot32[:, 16:32], op=ALU.add), writes=[r_tot8])
            dve.run(lambda e: e.tensor_tensor(out=tot32[:, 0:8], in0=tot32[:, 0:8], in1=tot32[:, 8:16], op=ALU.add), writes=[r_tot8])
            dve.run(lambda e, t=t: e.tensor_tensor(out=GRt[:, t + 1, :], in0=tot32[:, 0:8], in1=GRt[:, t, :], op=ALU.add),
                    reads=[r_tot8], writes=[r_GR])
            dve.run(lambda e, b=b: e.tensor_copy(out=ghl[b][0:8, 0, :], in_=ps_gk[0:8, :]), reads=[psres[7]], writes=[r_ghl[b]])
            dve.run(lambda e, b=b: e.tensor_tensor(out=ghl[b][0:8, 1, :], in0=ps_gk[0:8, :], in1=ghl[b][0:8, 0, :], op=ALU.subtract),
                    reads=[psres[7]], writes=[r_ghl[b]])
            k.dma("sp", KA[:, 64:66, t * TN:(t + 1) * TN], ghl[b][0:8, :, :], r_ghl[b], reads=[r_ghl[b]], adds=[r_KA])
        k.barrier()
        ar.reset(pmark)

    phase1a()
    def phase1b():
        wq = ar.alloc((8, 512), BF16)
        wc = ar.alloc((8, 1024), BF16)
        wfg = ar.alloc((8, 8), BF16)
        r_w1b = k.res("w1b", dma=True)
        k.dma("pool", wq, wq_d.rearrange("(kc p) n -> p kc n", p=128), r_w1b, adds=[r_w1b])
        k.dma("pool", wc, wc_d.rearrange("(kc p) n -> p kc n", p=128), r_w1b, adds=[r_w1b])
        k.dma("pool", wfg, wfg_d.rearrange("(kc p) n -> p kc n", p=128), r_w1b, adds=[r_w1b])
        Dg = ar.alloc((4, KW, 128), BF16)
        r_Dg = k.res("Dg")
        first = True
        for c in range(4):
            for kk in range(KW):
                eng = dve
                eng.run(lambda e, c=c, kk=kk: e.tensor_scalar(out=Dg[:, c, kk, :], in0=ident, scalar1=cwc[:, c * KW + kk:c * KW + kk + 1], scalar2=None, op0=ALU.mult),
                        reads=[r_consts], writes=[r_Dg] if first else (), adds=() if first else [r_Dg])
                first = False
        xq = [ar.alloc((8, TNH), F32) for _ in range(2)]
        r_xq = [k.res("xq%d" % i, dma=True) for i in range(2)]
        sq = ar.alloc((8, TNH), BF16); r_sq = k.res("sqb")
        rstd = ar.alloc((TNH,), F32); r_rstd = k.res("rstdb")
        tmpf = [ar.alloc((TNH,), F32) for _ in range(2)]
        r_tmpf = [k.res("tmpfb%d" % i) for i in range(2)]
        hq = [ar.alloc((8, TNH), BF16) for _ in range(2)]
        r_hq = [k.res("hq%d" % i) for i in range(2)]
        raw = [ar.alloc((TN,), F32) for _ in range(2)]
        r_raw = [k.res("rawb%d" % i) for i in range(2)]
        sqb = [ar.alloc((TN,), BF16) for _ in range(2)]
        r_sqb = [k.res("sqbb%d" % i) for i in range(2)]
        rs = [ar.alloc((TN,), F32) for _ in range(2)]
        r_rs = [k.res("rsb%d" % i) for i in range(2)]
        qn = [ar.alloc((4, TN), BF16) for _ in range(2)]
        r_qn = [k.res("qn%d" % i, dma=True) for i in range(2)]
        zt = ar.alloc((32,), F32); r_zt = k.res("ztb")
        spt = ar.alloc((32,), F32); r_spt = k.res("sptb")
        dhl = [ar.alloc((2, TN), BF16) for _ in range(2)]
        r_dhl = [k.res("dhl%d" % i, dma=True) for i in range(2)]
        eg = ar.alloc((TNH,), F32); r_eg = k.res("eg")
        ubuf = ar.alloc((4, TNH), BF16); r_ub = k.res("ubuf")
        ybuf = ar.alloc((4, TN), F32); r_yb = k.res("ybuf")
        ybf = ar.alloc((4, TN), BF16); r_ybf = k.res("ybf")
        ysq = ar.alloc((4, TN), BF16); r_ysq = k.res("ysq")
        mean = ar.alloc((TN,), F32); r_mean = k.res("mean")
        var = ar.alloc((TN,), F32); r_var = k.res("var")
        tcv = ar.alloc((TN,), F32); r_tcv = k.res("tcv")
        u2 = ar.alloc((4, TN), F32); r_u2 = k.res("u2")
        u2s = ar.alloc((4, TN), BF16); r_u2s = k.res("u2s")
        rs3 = ar.alloc((TN,), F32); r_rs3 = k.res("rs3")
        mcb = [ar.alloc((4, TN), BF16) for _ in range(2)]
        r_mcb = [k.res("mcb%d" % i, dma=True) for i in range(2)]

        def load_xq(i):
            b = i % 2
            for half in range(2):
                k.dma("sp", xq[b][:, half * 4:(half + 1) * 4, :], xqT_v[:, half * 4:(half + 1) * 4, i, :], r_xq[b],
                      writes=[r_xq[b]] if half == 0 else (), adds=() if half == 0 else [r_xq[b]])
        def A_sq(i):
            b = i % 2
            dve.run(lambda e, b=b: e.tensor_tensor(out=sq, in0=xq[b], in1=xq[b], op=ALU.mult), reads=[r_xq[b]], writes=[r_sq])

        def A_ss(i):
            def mm(e):
                last = None
                for kc in range(8):
                    last = e.matmul(PS(0), lhsT=onesb, rhs=sq[:, kc, 0:TN], start=(kc == 0), stop=(kc == 7))
                for kc in range(8):
                    last = e.matmul(PS(1)[:, 0:HALO], lhsT=onesb, rhs=sq[:, kc, TN:TNH], start=(kc == 0), stop=(kc == 7))
                return last
            pe.run(mm, reads=[r_sq, r_ones], writes=[psres[0], psres[1]])

        def A_rstd(i):
            act.run(lambda e: e.activation(out=rstd, in_=PS(0, 2)[:, 0:TNH], func=AF.Ln, bias=c_eps, scale=1.0 / D), reads=[psres[0], psres[1], r_cst], writes=[r_rstd])
            act.run(lambda e: e.activation(out=rstd, in_=rstd, func=AF.Exp, bias=c_zero, scale=-0.5), writes=[r_rstd])

        def A_u(i, kcs):
            b = i % 2
            for kc in kcs:
                tb = kc % 2
                dve.run(lambda e, kc=kc, tb=tb, b=b: e.tensor_tensor(out=tmpf[tb], in0=xq[b][:, kc, :], in1=rstd, op=ALU.mult),
                        reads=[r_xq[b], r_rstd], writes=[r_tmpf[tb]])
                act.run(lambda e, kc=kc, tb=tb, b=b: e.activation(out=hq[b][:, kc, :], in_=tmpf[tb], func=AF.Identity, bias=sh1[:, kc:kc + 1], scale=gm1[:, kc:kc + 1]),
                        reads=[r_tmpf[tb], r_mod], writes=[r_hq[b]] if kc == 0 else (), adds=() if kc == 0 else [r_hq[b]])
        load_xq(0)
        load_xq(1)
        A_sq(0); A_ss(0); A_rstd(0); A_u(0, range(8))
        for i in range(NOWN):
            b = i % 2
            nxt = i + 1 < NOWN
            if i + 2 < NOWN:
                load_xq(i + 2)
            if nxt:
                A_sq(i + 1)
            fgate(hq[b], r_hq[b], wfg, r_w1b, PS(6), psres[6], zt, r_zt, spt, r_spt, PS(7), psres[7], None, None, False, part=1)
            def q_mm(pr, b=b):
                pb = pr % 2
                psq = PS(2 + pb)

                def mmq(e, pr=pr, psq=psq, hqb=hq[b]):
                    last = None
                    for kc in range(8):
                        last = e.matmul(psq, lhsT=wq[:, kc, pr * 128:(pr + 1) * 128], rhs=hqb[:, kc, 0:TN], start=(kc == 0), stop=(kc == 7))
                    return last
                pe.run(mmq, reads=[r_hq[b], r_w1b], writes=[psres[2 + pb]])

            def q_norm(pr, steps, b=b):
                pb = pr % 2
                pair_norm(PS(2 + pb), psres[2 + pb], qg[:, 0:1], c_ln8, raw[pb], r_raw[pb], sqb[pb], r_sqb[pb], PS(4), psres[4], rs[pb], r_rs[pb],
                          qn[b][:, pr, :], r_qn[b], pr == 0, steps=steps)
            for p0 in (0, 2):
                q_mm(p0); q_mm(p0 + 1)
                q_norm(p0, ("sq",)); q_norm(p0 + 1, ("sq",))
                q_norm(p0, ("rest",)); q_norm(p0 + 1, ("rest",))
            for pr in range(4):
                for hh in range(2):
                    k.dma("sp", QA[2 * pr + hh, 0:64, i * TN:(i + 1) * TN], qn[b][hh * 64:(hh + 1) * 64, pr, :], r_qn[b], reads=[r_qn[b]], adds=[r_QA])
            if nxt:
                A_ss(i + 1)
            ps_gk = PS(7)
            fgate(hq[b], r_hq[b], wfg, r_w1b, PS(6), psres[6], zt, r_zt, spt, r_spt, ps_gk, psres[7], None, None, False, part=2)
            dve.run(lambda e, b=b: e.tensor_scalar(out=dhl[b][0:8, 0, :], in0=ps_gk[0:8, :], scalar1=-1.0, scalar2=None, op0=ALU.mult),
                    reads=[psres[7]], writes=[r_dhl[b]])
            dve.run(lambda e, b=b: e.scalar_tensor_tensor(out=dhl[b][0:8, 1, :], in0=ps_gk[0:8, :], scalar=-1.0, in1=dhl[b][0:8, 0, :],
                                                          op0=ALU.mult, op1=ALU.subtract),
                    reads=[psres[7]], writes=[r_dhl[b]])
            k.dma("sp", QA[:, 64:66, i * TN:(i + 1) * TN], dhl[b][0:8, :, :], r_dhl[b], reads=[r_dhl[b]], adds=[r_QA])
            if nxt:
                A_rstd(i + 1)
            for c in range(4):
                psl = PS(2, 2)
                psg = PS(4, 2)

                def mmc(e, c=c, psl=psl, psg=psg, hqb=hq[b]):
                    last = None
                    for (ps_, col0) in ((psl, c * 128), (psg, 512 + c * 128)):
                        for kc in range(8):
                            last = e.matmul(ps_[:, 0:TN], lhsT=wc[:, kc, col0:col0 + 128], rhs=hqb[:, kc, 0:TN], start=(kc == 0), stop=(kc == 7))
                        for kc in range(8):
                            last = e.matmul(ps_[:, TN:TNH], lhsT=wc[:, kc, col0:col0 + 128], rhs=hqb[:, kc, TN:TNH], start=(kc == 0), stop=(kc == 7))
                    return last
                pe.run(mmc, reads=[r_hq[b], r_w1b], writes=[psres[2], psres[3], psres[4], psres[5]])
                act.run(lambda e, psg=psg: e.activation(out=eg, in_=psg[:, 0:TNH], func=AF.Sigmoid, bias=c_zero, scale=1.0),
                        reads=[psres[4], psres[5], r_cst], writes=[r_eg])
                dve.run(lambda e, c=c, psl=psl: e.tensor_tensor(out=ubuf[:, c, HALO:TNH], in0=psl[:, 0:TN], in1=eg[:, 0:TN], op=ALU.mult),
                        reads=[psres[2], psres[3], r_eg], writes=[r_ub] if c == 0 else (), adds=() if c == 0 else [r_ub])
                dve.run(lambda e, c=c, psl=psl, i=i: e.scalar_tensor_tensor(out=ubuf[:, c, 0:HALO], in0=psl[:, TN:TNH], scalar=hsc[:, i:i + 1], in1=eg[:, TN:TNH],
                                                                            op0=ALU.mult, op1=ALU.mult),
                        reads=[psres[2], psres[3], r_eg, r_consts], adds=[r_ub])
            if nxt:
                A_u(i + 1, range(8))
            for c in range(4):
                psy = PS(2 + (c % 2))

                def mmy(e, c=c, psy=psy):
                    last = None
                    for kk in range(KW):
                        last = e.matmul(psy, lhsT=Dg[:, c, kk, :], rhs=ubuf[:, c, kk:kk + TN], start=(kk == 0), stop=(kk == KW - 1))
                    return last
                pe.run(mmy, reads=[r_ub, r_Dg], writes=[psres[2 + (c % 2)]])
                act.run(lambda e, c=c, psy=psy: e.activation(out=ybuf[:, c, :], in_=psy, func=AF.Identity, bias=cbc[:, c:c + 1], scale=1.0),
                        reads=[psres[2 + (c % 2)], r_consts], writes=[r_yb] if c == 0 else (), adds=() if c == 0 else [r_yb])
                dve.run(lambda e, c=c: e.tensor_copy(out=ybf[:, c, :], in_=ybuf[:, c, :]), reads=[r_yb], writes=[r_ybf] if c == 0 else (), adds=() if c == 0 else [r_ybf])
                act.run(lambda e, c=c: e.activation(out=ysq[:, c, :], in_=ybuf[:, c, :], func=AF.Square, bias=c_zero, scale=1.0),
                        reads=[r_yb, r_cst], writes=[r_ysq] if c == 0 else (), adds=() if c == 0 else [r_ysq])

            def mmln(e):
                last = None
                for c in range(4):
                    last = e.matmul(PS(4), lhsT=onesb, rhs=ybf[:, c, :], start=(c == 0), stop=(c == 3))
                for c in range(4):
                    last = e.matmul(PS(5), lhsT=onesb, rhs=ysq[:, c, :], start=(c == 0), stop=(c == 3))
                return last
            pe.run(mmln, reads=[r_ybf, r_ysq, r_ones], writes=[psres[4], psres[5]])
            dve.run(lambda e: e.tensor_scalar(out=mean, in0=PS(4), scalar1=1.0 / 512, scalar2=None, op0=ALU.mult), reads=[psres[4]], writes=[r_mean])
            dve.run(lambda e: e.tensor_tensor(out=tcv, in0=mean, in1=mean, op=ALU.mult), reads=[r_mean], writes=[r_tcv])
            dve.run(lambda e: e.scalar_tensor_tensor(out=var, in0=PS(5), scalar=1.0 / 512, in1=tcv, op0=ALU.mult, op1=ALU.subtract),
                    reads=[psres[5], r_tcv], writes=[r_var])
            act.run(lambda e: e.activation(out=var, in_=var, func=AF.Ln, bias=c_eps, scale=1.0), reads=[r_cst], writes=[r_var])
            act.run(lambda e: e.activation(out=var, in_=var, func=AF.Exp, bias=c_zero, scale=-0.5), writes=[r_var])
            for c in range(4):
                tb_, r_tb_ = (tcv, r_tcv) if c % 2 == 0 else (eg[:, 0:TN], r_eg)
                dve.run(lambda e, c=c, tb_=tb_: e.tensor_tensor(out=tb_, in0=ybuf[:, c, :], in1=mean, op=ALU.subtract), reads=[r_yb, r_mean], writes=[r_tb_])
                dve.run(lambda e, tb_=tb_: e.tensor_tensor(out=tb_, in0=tb_, in1=var, op=ALU.mult), reads=[r_var], writes=[r_tb_])
                act.run(lambda e, c=c, tb_=tb_: e.activation(out=u2[:, c, :], in_=tb_, func=AF.Silu, bias=clb[:, c:c + 1], scale=clg[:, c:c + 1]),
                        reads=[r_tb_, r_consts], writes=[r_u2] if c == 0 else (), adds=() if c == 0 else [r_u2])
                act.run(lambda e, c=c: e.activation(out=u2s[:, c, :], in_=u2[:, c, :], func=AF.Square, bias=c_zero, scale=1.0),
                        reads=[r_u2, r_cst], writes=[r_u2s] if c == 0 else (), adds=() if c == 0 else [r_u2s])

            def mmr(e):
                last = None
                for c in range(4):
                    last = e.matmul(PS(6), lhsT=onesb, rhs=u2s[:, c, :], start=(c == 0), stop=(c == 3))
                return last
            pe.run(mmr, reads=[r_u2s, r_ones], writes=[psres[6]])
            act.run(lambda e: e.activation(out=rs3, in_=PS(6), func=AF.Ln, bias=c_eps, scale=1.0 / 512), reads=[psres[6], r_cst], writes=[r_rs3])
            act.run(lambda e: e.activation(out=rs3, in_=rs3, func=AF.Exp, bias=c_zero, scale=-0.5), writes=[r_rs3])
            for c in range(4):
                dve.run(lambda e, c=c, b=b: e.scalar_tensor_tensor(out=mcb[b][:, c, :], in0=u2[:, c, :], scalar=bcc[:, c:c + 1], in1=rs3, op0=ALU.mult, op1=ALU.mult),
                        reads=[r_u2, r_rs3, r_consts], writes=[r_mcb[b]] if c == 0 else (), adds=() if c == 0 else [r_mcb[b]])
            k.dma("sp", MC.rearrange("(c p) t -> p c t", p=128)[:, :, i * TN:(i + 1) * TN], mcb[b], r_mcb[b], reads=[r_mcb[b]], adds=[r_MC])
        k.barrier()
        ar.reset(pmark)

    phase1b()
    def phase2():
        mtri = ar.alloc((4, TN), BF16)
        dsel = ar.alloc((32, 128), BF16)
        ohrep = ar.alloc((NOWN * H, NT), F32)
        r_c2 = k.res("c2", dma=True)
        k.dma("sp", mtri, mtri_d.rearrange("p (a b) -> p a b", a=4), r_c2, adds=[r_c2])
        k.dma("sp", dsel, dsel_d.rearrange("p (a b) -> p a b", a=32), r_c2, adds=[r_c2])
        k.dma("sp", ohrep, oh_d.rearrange("p (a b) -> p a b", b=NT), r_c2, adds=[r_c2])
        CB = ar.alloc((NOWN * H, NT), F32); r_CB = k.res("CB")
        seltmp = ar.alloc((H, NT), F32); r_seltmp = k.res("seltmp")
        sel = ar.alloc((H,), F32); r_sel = k.res("sel")
        GRv = GRt[:, 0:NT, :].rearrange("p t h -> p h t")
        for i in range(NOWN):
            dve.run(lambda e, i=i: e.tensor_tensor(out=seltmp, in0=GRv, in1=ohrep[:, i * H:(i + 1) * H, :], op=ALU.mult),
                    reads=[r_GR, r_c2], writes=[r_seltmp])
            dve.run(lambda e: e.tensor_reduce(out=sel, in_=seltmp, axis=AX.X, op=ALU.add), reads=[r_seltmp], writes=[r_sel])
            dve.run(lambda e: e.tensor_scalar(out=sel, in0=sel, scalar1=-1.0, scalar2=None, op0=ALU.mult), writes=[r_sel])
            for h in range(H):
                dve.run(lambda e, i=i, h=h: e.scalar_tensor_tensor(out=CB[:, i * H + h, :], in0=GRt[:, 0:NT, h], scalar=sel[:, h:h + 1], in1=cmask[:, i, :],
                                                                   op0=ALU.add, op1=ALU.add),
                        reads=[r_sel, r_GR, r_consts], writes=[r_CB] if (i == 0 and h == 0) else (), adds=() if (i == 0 and h == 0) else [r_CB])
        if debug:
            DBG = nc.dram_tensor("DBG", [128, NOWN * H * NT + (NT + 1) * H], F32, kind="ExternalOutput").ap()
            r_dbg = k.res("dbg", dma=True)
            k.dma("sp", DBG[:, 0:NOWN * H * NT].rearrange("p (a b) -> p a b", b=NT), CB, r_dbg, reads=[r_CB], adds=[r_dbg])
            k.dma("sp", DBG[:, NOWN * H * NT:].rearrange("p (a b) -> p a b", b=H), GRt, r_dbg, reads=[r_GR], adds=[r_dbg])
        kaug = [ar.alloc((S,), BF16) for _ in range(2)]
        r_kaug = [k.res("kaug%d" % i, dma=True) for i in range(2)]
        vh = [ar.alloc((128, 65), BF16) for _ in range(2)]
        r_vh = [k.res("vh%d" % i, dma=True) for i in range(2)]
        qaug = [ar.alloc((TN,), BF16) for _ in range(2)]
        r_qaug = [k.res("qaug%d" % i, dma=True) for i in range(2)]
        NPT = 3
        pT = [ar.alloc((1024,), BF16) for _ in range(NPT)]
        r_pT = [k.res("pT%d" % i) for i in range(NPT)]
        osb = [ar.alloc((TN,), F32) for _ in range(2)]
        r_osb = [k.res("osb%d" % i, dma=True) for i in range(2)]
        for b in range(2):
            pool.run(lambda e, b=b: e.memset(kaug[b][64:68, :], 1.0), writes=[r_kaug[b]])
            pool.run(lambda e, b=b: e.memset(qaug[b][64:68, :], 1.0), writes=[r_qaug[b]])

        def load_head(h, qn="sp"):
            hb = h % 2
            for ch in range(4):
                k.dma(qn, kaug[hb][0:66, ch * 4096:(ch + 1) * 4096], KA[h, :, ch * 4096:(ch + 1) * 4096], r_kaug[hb], reads=[r_KA],
                      writes=[r_kaug[hb]] if ch == 0 else (), adds=() if ch == 0 else [r_kaug[hb]])
            for ch in range(2):
                k.dma(qn, vh[hb][:, ch * 64:(ch + 1) * 64, :], VA[h, :, ch * 64:(ch + 1) * 64, :], r_vh[hb], reads=[r_VA],
                      writes=[r_vh[hb]] if ch == 0 else (), adds=() if ch == 0 else [r_vh[hb]])

        def load_head_part(h, part):
            hb = h % 2
            if part < 4:
                ch = part
                k.dma("pool", kaug[hb][0:66, ch * 4096:(ch + 1) * 4096], KA[h, :, ch * 4096:(ch + 1) * 4096], r_kaug[hb], reads=[r_KA],
                      writes=[r_kaug[hb]] if ch == 0 else (), adds=() if ch == 0 else [r_kaug[hb]])
            else:
                ch = part - 4
                k.dma("pool", vh[hb][:, ch * 64:(ch + 1) * 64, :], VA[h, :, ch * 64:(ch + 1) * 64, :], r_vh[hb], reads=[r_VA],
                      writes=[r_vh[hb]] if ch == 0 else (), adds=() if ch == 0 else [r_vh[hb]])

        def load_q(h, i, qb):
            k.dma("sp", qaug[qb][0:64, :], QA[h, 0:64, i * TN:(i + 1) * TN], r_qaug[qb], reads=[r_QA], writes=[r_qaug[qb]])
            k.dma("sp", qaug[qb][66:68, :], QA[h, 64:66, i * TN:(i + 1) * TN], r_qaug[qb], adds=[r_qaug[qb]])

        NST = 3
        units = [(h, i) for h in range(H) for i in range(NOWN)]
        load_head(0)
        load_q(0, 0, 0)
        act.relax = True
        stream = []
        for ui, (h, i) in enumerate(units):
            nonrag = [(T, hp, None) for T in range(rbase_of(i)) for hp in range(2)]
            rag = [(T, hp, i * 4 + (T - rbase_of(i))) for T in range(rbase_of(i), kmax_of(i) + 1) for hp in range(2)]
            order = []
            ni = ri = 0
            while ni < len(nonrag) or ri < len(rag):
                if ri < len(rag) and (ni >= len(nonrag) or ri * len(nonrag) <= ni * len(rag)):
                    order.append(rag[ri]); ri += 1
                else:
                    order.append(nonrag[ni]); ni += 1
            for pi, (T, hp, slot) in enumerate(order):
                stream.append(dict(ui=ui, h=h, i=i, T=T, hp=hp, slot=slot, first=(pi == 0), last=(pi == len(order) - 1)))

        def emit_pv(ent):
            ui, h, i = ent["ui"], ent["h"], ent["i"]
            hb, ob = h % 2, ui % 2
            ot = PS(6 + ob)
            r_ot = psres[6 + ob]
            first_, last_ = ent["first"], ent["last"]

            def mmpv(e, pb_=ent["pb"], T_=ent["T"], hp_=ent["hp"], first_=first_, last_=last_, ot=ot, hb=hb):
                last = None
                for x in range(2):
                    kb = T_ * 4 + hp_ * 2 + x
                    last = e.matmul(ot[0:65, :], lhsT=vh[hb][:, kb, :], rhs=pT[pb_][:, x * 512:(x + 1) * 512],
                                    start=(first_ and x == 0), stop=(last_ and x == 1))
                return last
            pe.run(mmpv, reads=[r_pT[ent["pb"]], r_vh[hb]], writes=[r_ot] if first_ else (), adds=() if first_ else [r_ot])
            if last_:
                dve.run(lambda e, ob=ob, ot=ot: e.tensor_copy(out=osb[ob][0:65, :], in_=ot[0:65, :]), reads=[r_ot], writes=[r_osb[ob]])
                k.dma("sp", AT[h, :, i * TN:(i + 1) * TN], osb[ob][0:65, :], r_osb[ob], reads=[r_osb[ob]], adds=[r_AT])

        for n, ent in enumerate(stream):
            ui, h, i, T, hp, slot = ent["ui"], ent["h"], ent["i"], ent["T"], ent["hp"], ent["slot"]
            hb, qb = h % 2, ui % 2
            sb = n % NST
            pb = n % NPT
            ent["pb"] = pb
            st = PS(2 * sb, 2)
            r_st = [psres[2 * sb], psres[2 * sb + 1]]

            def mmqk(e, T=T, hp=hp, slot=slot, st=st, hb=hb, qb=qb):
                last = None
                for x in range(2):
                    kb = T * 4 + hp * 2 + x
                    last = e.matmul(st[:, x * 512:(x + 1) * 512], lhsT=kaug[hb][0:68, kb * 128:(kb + 1) * 128], rhs=qaug[qb][0:68, :],
                                    start=True, stop=(slot is None))
                    if slot is not None:
                        last = e.matmul(st[:, x * 512:(x + 1) * 512], lhsT=dsel[:, slot, :], rhs=mtri[:, hp * 2 + x, :], start=False, stop=True)
                return last
            pe.run(mmqk, reads=[r_kaug[hb], r_qaug[qb], r_c2], writes=r_st)
            act.run(lambda e, st=st, pb=pb, i=i, h=h, T=T: e.activation(out=pT[pb], in_=st, func=AF.Exp, bias=CB[:, i * H + h, T:T + 1], scale=1.0),
                    reads=r_st + [r_CB], writes=[r_pT[pb]])
            if n > 0:
                emit_pv(stream[n - 1])
            if ent["first"]:
                if ui + 1 < len(units):
                    load_q(units[ui + 1][0], units[ui + 1][1], (ui + 1) % 2)
                if h + 1 < H and 1 <= i <= 6:
                    load_head_part(h + 1, i - 1)
        emit_pv(stream[-1])
        act.relax = False
        k.barrier()
        ar.reset(pmark)

    phase2()
    def phase3a():
        woa = ar.alloc((4, D), BF16)
        woc = ar.alloc((4, D), BF16)
        r_w3a = k.res("w3a", dma=True)
        k.dma("pool", woa, wo_d[0:512, :].rearrange("(c p) n -> p c n", p=128), r_w3a, adds=[r_w3a])
        k.dma("pool", woc, wo_d[512:1024, :].rearrange("(c p) n -> p c n", p=128), r_w3a, adds=[r_w3a])
        num = [ar.alloc((4, TN), F32) for _ in range(2)]
        r_num = [k.res("num%d" % i, dma=True) for i in range(2)]
        den1 = ar.alloc((4, TN), F32)
        den = [den1, den1]
        r_den1 = k.res("den", dma=True)
        r_den = [r_den1, r_den1]
        mcl = [ar.alloc((4, TN), BF16) for _ in range(2)]
        r_mcl = [k.res("mcl%d" % i, dma=True) for i in range(2)]
        xr = [ar.alloc((8, TN), F32) for _ in range(2)]
        r_xr = [k.res("xr%d" % i, dma=True) for i in range(2)]
        atsq = ar.alloc((4, TN), BF16); r_atsq = k.res("atsq")
        rsa = ar.alloc((TN,), F32); r_rsa = k.res("rsa")
        ma = [ar.alloc((4, TN), BF16) for _ in range(2)]
        r_ma = [k.res("ma%d" % i) for i in range(2)]
        x1 = [ar.alloc((8, TN), F32) for _ in range(2)]
        r_x1 = [k.res("x1%d" % i, dma=True) for i in range(2)]
        sq = ar.alloc((8, TN), BF16); r_sq = k.res("sq3")
        rstd = ar.alloc((TN,), F32); r_rstd = k.res("rstd3")
        tmpf = [ar.alloc((TN,), F32) for _ in range(2)]
        r_tmpf = [k.res("tmpf3%d" % i) for i in range(2)]
        h2 = [ar.alloc((8, TN), BF16) for _ in range(2)]
        r_h2 = [k.res("h2%d" % i, dma=True) for i in range(2)]
        def load_3a(i):
            b = i % 2
            cols = slice(i * TN, (i + 1) * TN)
            for hh in range(2):
                k.dma("sp", num[b][hh * 64:(hh + 1) * 64, :, :], AT_p[hh, 0:64, :, cols], r_num[b], reads=[r_AT],
                      writes=[r_num[b]] if hh == 0 else (), adds=() if hh == 0 else [r_num[b]])
            k.dma("sp", mcl[b], MC_v[:, :, cols], r_mcl[b], reads=[r_MC], writes=[r_mcl[b]])
            for half in range(2):
                k.dma("sp", xr[b][:, half * 4:(half + 1) * 4, :], xqT_v[:, half * 4:(half + 1) * 4, i, 0:TN], r_xr[b],
                      writes=[r_xr[b]] if half == 0 else (), adds=() if half == 0 else [r_xr[b]])
        def a_load_den(i):
            b = i % 2
            cols = slice(i * TN, (i + 1) * TN)
            for h in range(H):
                k.dma("sp", den[b][(h % 2) * 64:(h % 2 + 1) * 64, h // 2, :], AT[h, 64:65, cols].partition_broadcast(64), r_den[b], reads=[r_AT],
                      writes=[r_den[b]] if h == 0 else (), adds=() if h == 0 else [r_den[b]])

        def a_rec(i):
            b = i % 2
            dve.run(lambda e, b=b: e.reciprocal(out=den[b], in_=den[b]), writes=[r_den[b]])

        def a_mul(i):
            b = i % 2
            dve.run(lambda e, b=b: e.tensor_tensor(out=num[b], in0=num[b], in1=den[b], op=ALU.mult),
                    reads=[r_den[b]], writes=[r_num[b]])

        def a_sq(i):
            b = i % 2
            act.run(lambda e, b=b: e.activation(out=atsq, in_=num[b], func=AF.Square, bias=c_zero, scale=1.0),
                    reads=[r_num[b], r_cst], writes=[r_atsq])

        def a_mma(i):
            def mma(e):
                last = None
                for h in range(4):
                    last = e.matmul(PS(0), lhsT=onesb, rhs=atsq[:, h, :], start=(h == 0), stop=(h == 3))
                return last
            pe.run(mma, reads=[r_atsq, r_ones], writes=[psres[0]])

        def a_ln(i):
            act.run(lambda e: e.activation(out=rsa, in_=PS(0), func=AF.Ln, bias=c_eps, scale=1.0 / 512),
                    reads=[psres[0], r_cst], writes=[r_rsa])
            act.run(lambda e: e.activation(out=rsa, in_=rsa, func=AF.Exp, bias=c_zero, scale=-0.5), writes=[r_rsa])

        def a_ma(i, hs):
            b = i % 2
            for h in hs:
                dve.run(lambda e, h=h, b=b: e.scalar_tensor_tensor(out=ma[b][:, h, :], in0=num[b][:, h, :], scalar=bac[:, h:h + 1], in1=rsa,
                                                                   op0=ALU.mult, op1=ALU.mult),
                        reads=[r_num[b], r_rsa, r_consts], writes=[r_ma[b]] if h == 0 else (), adds=() if h == 0 else [r_ma[b]])

        def b_mm(i, oc):
            b = i % 2
            pso = PS(2 + (oc % 2))

            def mmo(e, oc=oc, pso=pso, b=b):
                last = None
                for h in range(4):
                    last = e.matmul(pso, lhsT=woa[:, h, oc * 128:(oc + 1) * 128], rhs=ma[b][:, h, :], start=(h == 0), stop=False)
                for c in range(4):
                    last = e.matmul(pso, lhsT=woc[:, c, oc * 128:(oc + 1) * 128], rhs=mcl[b][:, c, :], start=False, stop=(c == 3))
                return last
            pe.run(mmo, reads=[r_ma[b], r_mcl[b], r_w3a], writes=[psres[2 + (oc % 2)]])

        def b_x1(i, oc):
            b = i % 2
            pso = PS(2 + (oc % 2))
            dve.run(lambda e, oc=oc, pso=pso, b=b: e.scalar_tensor_tensor(out=x1[b][:, oc, :], in0=pso, scalar=g1c[:, oc:oc + 1], in1=xr[b][:, oc, :],
                                                                          op0=ALU.mult, op1=ALU.add),
                    reads=[psres[2 + (oc % 2)], r_xr[b], r_mod], writes=[r_x1[b]] if oc == 0 else (), adds=() if oc == 0 else [r_x1[b]])

        def c_sq(i):
            b = i % 2
            act.run(lambda e, b=b: e.activation(out=sq, in_=x1[b], func=AF.Square, bias=c_zero, scale=1.0), reads=[r_x1[b], r_cst], writes=[r_sq])

        def c_ss(i):
            def mm(e):
                last = None
                for kc in range(8):
                    last = e.matmul(PS(4), lhsT=onesb, rhs=sq[:, kc, :], start=(kc == 0), stop=(kc == 7))
                return last
            pe.run(mm, reads=[r_sq, r_ones], writes=[psres[4]])

        def c_rstd(i):
            act.run(lambda e: e.activation(out=rstd, in_=PS(4), func=AF.Ln, bias=c_eps, scale=1.0 / D), reads=[psres[4], r_cst], writes=[r_rstd])
            act.run(lambda e: e.activation(out=rstd, in_=rstd, func=AF.Exp, bias=c_zero, scale=-0.5), writes=[r_rstd])

        def c_u(i, kcs):
            b = i % 2
            for kc in kcs:
                tb = kc % 2
                dve.run(lambda e, kc=kc, tb=tb, b=b: e.tensor_tensor(out=tmpf[tb], in0=x1[b][:, kc, :], in1=rstd, op=ALU.mult),
                        reads=[r_x1[b], r_rstd], writes=[r_tmpf[tb]])
                act.run(lambda e, kc=kc, tb=tb, b=b: e.activation(out=h2[b][:, kc, :], in_=tmpf[tb], func=AF.Identity, bias=sh2[:, kc:kc + 1], scale=gm2[:, kc:kc + 1]),
                        reads=[r_tmpf[tb], r_mod], writes=[r_h2[b]] if kc == 0 else (), adds=() if kc == 0 else [r_h2[b]])

        load_3a(0)
        a_load_den(0); a_rec(0); a_mul(0); a_sq(0); a_mma(0); a_ln(0); a_ma(0, range(4))
        for i in range(NOWN):
            b = i % 2
            cols = slice(i * TN, (i + 1) * TN)
            n = i + 1
            nxt = n < NOWN
            if nxt:
                load_3a(n)
                a_load_den(n)
            b_mm(i, 0); b_mm(i, 1)
            if nxt:
                a_rec(n)
            b_x1(i, 0); b_mm(i, 2)
            if nxt:
                a_mul(n)
            b_x1(i, 1); b_mm(i, 3)
            if nxt:
                a_sq(n)
            b_x1(i, 2); b_mm(i, 4)
            if nxt:
                a_mma(n)
            b_x1(i, 3); b_mm(i, 5)
            if nxt:
                a_ln(n)
            b_x1(i, 4); b_mm(i, 6); b_x1(i, 5); b_mm(i, 7)
            if nxt:
                a_ma(n, range(0, 2))
            b_x1(i, 6); b_x1(i, 7)
            c_sq(i)
            if nxt:
                a_ma(n, range(2, 4))
            c_ss(i); c_rstd(i); c_u(i, range(8))
            k.dma("sp", X1_v[:, :, cols], x1[b], r_x1[b], reads=[r_x1[b]], adds=[r_X1])
            k.dma("sp", H2_v[:, :, cols], h2[b], r_h2[b], reads=[r_h2[b]], adds=[r_H2])
        k.barrier()
        ar.reset(pmark)

    phase3a()
    def phase3b():
        w1 = ar.alloc((8, DFF), BF16)
        w2 = ar.alloc((32, D), BF16)
        r_w3b = k.res("w3b", dma=True)
        w1_v = w1_d.rearrange("(kc p) n -> p kc n", p=128)
        w2_v = w2_d.rearrange("(fc p) n -> p fc n", p=128)
        for kc in range(8):
            k.dma("pool", w1[:, kc, :], w1_v[:, kc, :], r_w3b, adds=[r_w3b])
        for f4 in range(8):
            k.dma("pool", w2[:, f4 * 4:(f4 + 1) * 4, :], w2_v[:, f4 * 4:(f4 + 1) * 4, :], r_w3b, adds=[r_w3b])
        h2l = [ar.alloc((8, TN), BF16) for _ in range(2)]
        r_h2l = [k.res("h2l%d" % i, dma=True) for i in range(2)]
        aT = ar.alloc((32, TN), BF16)
        r_aT = [k.res("aT%d" % i) for i in range(32)]
        xo = ar.alloc((8, TN), F32); r_xo = k.res("xo", dma=True)
        rbuf = [ar.alloc((TN,), F32) for _ in range(4)]
        r_rbuf = [k.res("rbuf%d" % i) for i in range(4)]
        outT_v = outT.rearrange("(kc p) t -> p kc t", p=128)
        k.dma("sp", h2l[0], H2_v[:, :, 0:TN], r_h2l[0], reads=[r_H2], writes=[r_h2l[0]])
        for i in range(NOWN):
            b = i % 2
            cols = slice(i * TN, (i + 1) * TN)
            k.dma("sp", xo, X1_v[:, :, cols], r_xo, reads=[r_X1], writes=[r_xo])
            if i + 1 < NOWN:
                k.dma("sp", h2l[1 - b], H2_v[:, :, (i + 1) * TN:(i + 2) * TN], r_h2l[1 - b], reads=[r_H2], writes=[r_h2l[1 - b]])
            for fc in range(32):
                psf = PS(fc % 4)

                def mmf(e, fc=fc, psf=psf, b=b):
                    last = None
                    for kc in range(8):
                        last = e.matmul(psf, lhsT=w1[:, kc, fc * 128:(fc + 1) * 128], rhs=h2l[b][:, kc, :], start=(kc == 0), stop=(kc == 7))
                    return last
                pe.run(mmf, reads=[r_h2l[b], r_w3b], writes=[psres[fc % 4]])
                rb = fc % 4
                act.run(lambda e, psf=psf, rb=rb: e.activation(out=rbuf[rb], in_=psf, func=AF.Relu, bias=c_zero, scale=1.0),
                        reads=[psres[fc % 4], r_cst], writes=[r_rbuf[rb]])
                dve.run(lambda e, fc=fc, rb=rb: e.tensor_tensor(out=aT[:, fc, :], in0=rbuf[rb], in1=rbuf[rb], op=ALU.mult),
                        reads=[r_rbuf[rb]], writes=[r_aT[fc]])
            for oc in range(8):
                ps2 = PS(4 + (oc % 4))

                def mm2(e, oc=oc, ps2=ps2):
                    last = None
                    for fc in range(32):
                        last = e.matmul(ps2, lhsT=w2[:, fc, oc * 128:(oc + 1) * 128], rhs=aT[:, fc, :], start=(fc == 0), stop=(fc == 31))
                    return last
                pe.run(mm2, reads=r_aT + [r_w3b], writes=[psres[4 + (oc % 4)]])
                dve.run(lambda e, oc=oc, ps2=ps2: e.scalar_tensor_tensor(out=xo[:, oc, :], in0=ps2, scalar=g2c[:, oc:oc + 1], in1=xo[:, oc, :],
                                                                         op0=ALU.mult, op1=ALU.add),
                        reads=[psres[4 + (oc % 4)], r_mod], writes=[r_xo])
            k.dma("sp", outT_v[:, :, cols], xo, r_xo, reads=[r_xo])
    phase3b()
    k.barrier()

    block = E(nc.Block())

    @block.tensor
    def _(e):
        for f in k.pe.q:
            f(e)

    @block.scalar
    def _(e):
        for f in k.act.q:
            f(e)

    @block.vector
    def _(e):
        for f in k.dve.q:
            f(e)

    @block.gpsimd
    def _(e):
        for f in k.pool.q:
            f(e)

    @block.sync
    def _(e):
        for f in k.sp.q:
            f(e)

    es.close()
    return nc


def _col(v, p=128):
    v = np.asarray(v, np.float32).reshape(-1)
    return np.ascontiguousarray(v.reshape(-1, p).T)


def prep_inputs(x, c, w_ada, b_ada, norm1_g, w_in, q_norm_g, k_norm_g, b_f, conv_w, conv_b, conv_ln_g, conv_ln_b,
                beta_attn, beta_conv, w_out, norm2_g, w_ff1, w_ff2):
    f32 = np.float32
    x = np.asarray(x, f32)
    w_in0 = np.asarray(w_in, f32)[0]
    shared = {
        "wada": np.ascontiguousarray(np.asarray(w_ada, f32)[0]),
        "badacol": _col(np.asarray(b_ada)[0]),
        "n1gcol": _col(np.asarray(norm1_g)[0]),
        "n2gcol": _col(np.asarray(norm2_g)[0]),
        "wq": np.ascontiguousarray(w_in0[:, 0:512]),
        "wk": np.ascontiguousarray(w_in0[:, 512:1024]),
        "wv": np.ascontiguousarray(w_in0[:, 1024:1536]),
        "wfg": np.ascontiguousarray(w_in0[:, 1536:1544]),
        "wc": np.ascontiguousarray(w_in0[:, 1544:2568]),
        "qgcol": np.ascontiguousarray(np.tile(np.asarray(q_norm_g, f32)[0], 2).reshape(128, 1)),
        "kgcol": np.ascontiguousarray(np.tile(np.asarray(k_norm_g, f32)[0], 2).reshape(128, 1)),
        "bfrep": np.ascontiguousarray(np.tile(np.asarray(b_f, f32)[0].reshape(1, 8), (128, 4))),
        "cwcol": np.ascontiguousarray(np.asarray(conv_w, f32)[0].reshape(KW, 4, 128).transpose(2, 1, 0).reshape(128, 4 * KW)),
        "cbcol": _col(np.asarray(conv_b)[0]),
        "clgcol": _col(np.asarray(conv_ln_g)[0]),
        "clbcol": _col(np.asarray(conv_ln_b)[0]),
        "bccol": _col(np.asarray(beta_conv)[0]),
        "bapcol": _col(np.asarray(beta_attn)[0]),
        "wo": np.ascontiguousarray(np.asarray(w_out, f32)[0]),
        "w1": np.ascontiguousarray(np.asarray(w_ff1, f32)[0]),
        "w2": np.ascontiguousarray(np.asarray(w_ff2, f32)[0]),
        "ident": np.eye(128, dtype=f32),
        "tri": np.triu(np.ones((128, 128), f32)),
    }
    kk = np.arange(128)[:, None, None] + 128 * np.arange(4)[None, :, None]
    qq = np.arange(TN)[None, None, :]
    shared["mtri"] = np.where(kk > qq, NEG, 0.0).astype(f32).reshape(128, 4 * TN).astype(ml_dtypes.bfloat16)
    xTb = [np.ascontiguousarray(x[b].T) for b in range(2)]
    ccols = [_col(np.asarray(c, f32)[b]) for b in range(2)]
    in_maps = []
    for core in range(8):
        b, j = core // 4, core % 4
        tiles = own_tiles(j)
        m = dict(shared)
        m["xT"] = xTb[b]
        m["ccol"] = ccols[b]
        xq = np.zeros((D, NOWN, TNH), f32)
        dsel = np.zeros((128, 32, 128), f32)
        cmask = np.zeros((128, NOWN, NT), f32)
        oh = np.zeros((128, NOWN, H, NT), f32)
        hs = np.ones((128, NOWN), f32)
        for i, t in enumerate(tiles):
            xq[:, i, 0:TN] = xTb[b][:, t * TN:(t + 1) * TN]
            if t > 0:
                xq[:, i, TN:TNH] = xTb[b][:, t * TN - HALO:t * TN]
            else:
                hs[:, i] = 0.0
            cmask[:, i, t + 1:] = NEG
            oh[:, i, :, t] = 1.0
            for r in range(4):
                if rbase_of(i) + r == t:
                    dsel[:, i * 4 + r, :] = np.eye(128, dtype=f32)
        m["xqT"] = np.ascontiguousarray(xq.reshape(D, NOWN * TNH))
        m["diagsel"] = dsel.reshape(128, 32 * 128).astype(ml_dtypes.bfloat16)
        m["cmask"] = cmask.reshape(128, NOWN * NT)
        m["ohrep"] = oh.reshape(128, NOWN * H * NT)
        m["haloscale"] = hs
        in_maps.append(m)
    return in_maps


_NC_CACHE = {}


def kernel(**inputs):
    in_maps = prep_inputs(**inputs)
    if "nc" not in _NC_CACHE:
        _NC_CACHE["nc"] = build_nc()
    nc = _NC_CACHE["nc"]
    res = run_bass_kernel_spmd(nc, in_maps, core_ids=list(range(8)))
    out = np.zeros((2, S, D), np.float32)
    for core in range(8):
        b, j = core // 4, core % 4
        oT = np.asarray(res.results[core]["outT"])
        for i, t in enumerate(own_tiles(j)):
            out[b, t * TN:(t + 1) * TN, :] = oT[:, i * TN:(i + 1) * TN].T
    return out
```

```python
import numpy as np
import ml_dtypes
from contextlib import ExitStack
import concourse.bass as bass
import concourse.mybir as mybir
from concourse.bass_utils import run_bass_kernel_spmd

F32 = mybir.dt.float32
BF16 = mybir.dt.bfloat16
AF = mybir.ActivationFunctionType
ALU = mybir.AluOpType
AX = mybir.AxisListType

D = 1024
S = 16384
NT = 32
TN = 512
NOWN = 8
HALO = 30
TNH = TN + HALO
H = 8
DH = 64
DFF = 4096
EPS = 1e-6
NEG = -30000.0
KW = 31


def own_tiles(j):
    out = []
    for m in range(4):
        out += [8 * m + j, 8 * m + 7 - j]
    return out


def kmax_of(i):
    m = i // 2
    return 8 * m + 3 if i % 2 == 0 else 8 * m + 7


def rbase_of(i):
    m = i // 2
    return 8 * m if i % 2 == 0 else 8 * m + 4


class Tok:
    __slots__ = ("sem", "val")

    def __init__(self, sem, val):
        self.sem = sem
        self.val = val


class Res:
    def __init__(self, name):
        self.name = name
        self.w = []
        self.r = []
        self.dsem = None
        self.dcnt = 0

    @staticmethod
    def _add(lst, tok):
        for k, t in enumerate(lst):
            if t.sem is tok.sem:
                if tok.val > t.val:
                    lst[k] = tok
                return
        lst.append(tok)

    def add_reader(self, tok):
        Res._add(self.r, tok)

    def add_writer(self, tok):
        Res._add(self.w, tok)


class Eng:
    def __init__(self, name, sem, same_sync):
        self.name = name
        self.sem = sem
        self.same = same_sync
        self.cnt = 0
        self.q = []
        self.seen = {}
        self.relax = False

    def _wait(self, t):
        if t.sem is self.sem and (not self.same or (self.relax and self.cnt - t.val >= 3)):
            return
        key = id(t.sem)
        if self.seen.get(key, 0) >= t.val:
            return
        self.seen[key] = t.val
        self.q.append(lambda e, s=t.sem, v=t.val: e.wait_ge(s, v))

    def wait_deps(self, reads, writes, adds):
        for r in reads:
            for t in r.w:
                self._wait(t)
        for w in writes:
            for t in w.w:
                self._wait(t)
            for t in w.r:
                self._wait(t)
        for a in adds:
            for t in a.r:
                self._wait(t)

    def run(self, fn, reads=(), writes=(), adds=()):
        self.wait_deps(reads, writes, adds)
        self.cnt += 1
        sem = self.sem
        self.q.append(lambda e, fn=fn, sem=sem: fn(e).then_inc(sem, 1))
        tok = Tok(sem, self.cnt)
        for r in reads:
            r.add_reader(tok)
        for w in writes:
            w.w = [tok]
            w.r = []
        for a in adds:
            a.add_writer(tok)
        return tok


class KB:
    def __init__(self, nc, es):
        self.nc = nc
        self.es = es
        self.engs = {}
        for name, same in [("pe", False), ("act", True), ("dve", True), ("pool", True), ("sp", False)]:
            sem = es.enter_context(nc.semaphore("s_" + name))
            self.engs[name] = Eng(name, sem, same)
        self.pe = self.engs["pe"]
        self.act = self.engs["act"]
        self.dve = self.engs["dve"]
        self.pool = self.engs["pool"]
        self.sp = self.engs["sp"]
        self.dma_res = []
        self.nres = 0

    def res(self, name, dma=False):
        r = Res(name)
        if dma:
            r.dsem = self.es.enter_context(self.nc.semaphore("d%d_%s" % (self.nres, name)))
            self.dma_res.append(r)
        self.nres += 1
        return r

    def dma(self, qname, out, in_, sres, reads=(), writes=(), adds=()):
        q = self.engs[qname]
        q.wait_deps(reads, writes, adds)
        sres.dcnt += 16
        sem = sres.dsem
        tok = Tok(sem, sres.dcnt)
        q.q.append(lambda e, o=out, i=in_, sem=sem: e.dma_start(out=o, in_=i).then_inc(sem, 16))
        for r in reads:
            r.add_reader(tok)
        for w in writes:
            w.w = [tok]
            w.r = []
        for a in adds:
            a.add_writer(tok)
        return tok

    def barrier(self):
        toks = [Tok(e.sem, e.cnt) for e in self.engs.values() if e.cnt > 0]
        toks += [Tok(r.dsem, r.dcnt) for r in self.dma_res if r.dcnt > 0]
        for e in self.engs.values():
            for t in toks:
                if t.sem is e.sem:
                    continue
                e._wait(t)


class Arena:
    def __init__(self, ap, words):
        self.ap = ap
        self.words = words
        self.off = 0

    def mark(self):
        return self.off

    def reset(self, m):
        self.off = m

    def alloc(self, shape, dt):
        n = int(np.prod(shape))
        nw = n if dt == F32 else (n + 1) // 2
        assert self.off + nw <= self.words, ("arena overflow", self.off, nw, self.words)
        v = self.ap[:, self.off:self.off + nw]
        self.off += nw
        if dt != F32:
            v = v.bitcast(dt)
            if v.shape[1] != n:
                v = v[:, 0:n]
        if len(shape) == 2:
            return v.rearrange("p (a b) -> p a b", a=shape[0])
        if len(shape) == 3:
            return v.rearrange("p (a b c) -> p a b c", a=shape[0], b=shape[1])
        return v


def build_nc(debug=False):
    nc = bass.Bass("TRN2", target_bir_lowering=False)

    def DI(name, shape, dt=F32):
        return nc.dram_tensor(name, list(shape), dt, kind="ExternalInput").ap()

    def DS(name, shape, dt):
        return nc.dram_tensor(name, list(shape), dt, kind="ExternalOutput" if debug else "Internal").ap()

    xT = DI("xT", [D, S])
    xqT = DI("xqT", [D, NOWN * TNH])
    ccol_d = DI("ccol", [128, 8])
    wada_d = DI("wada", [D, 6 * D])
    bada_d = DI("badacol", [128, 48])
    n1g_d = DI("n1gcol", [128, 8])
    n2g_d = DI("n2gcol", [128, 8])
    wq_d = DI("wq", [D, 512])
    wk_d = DI("wk", [D, 512])
    wv_d = DI("wv", [D, 512])
    wfg_d = DI("wfg", [D, 8])
    wc_d = DI("wc", [D, 1024])
    qg_d = DI("qgcol", [128, 1])
    kg_d = DI("kgcol", [128, 1])
    bf_d = DI("bfrep", [128, 32])
    cw_d = DI("cwcol", [128, 4 * KW])
    cb_d = DI("cbcol", [128, 4])
    clg_d = DI("clgcol", [128, 4])
    clb_d = DI("clbcol", [128, 4])
    bc_d = DI("bccol", [128, 4])
    ba_d = DI("bapcol", [128, 4])
    wo_d = DI("wo", [D, D])
    w1_d = DI("w1", [D, DFF])
    w2_d = DI("w2", [DFF, D])
    ident_d = DI("ident", [128, 128])
    tri_d = DI("tri", [128, 128])
    mtri_d = DI("mtri", [128, 4 * TN], BF16)
    dsel_d = DI("diagsel", [128, 32 * 128], BF16)
    cmask_d = DI("cmask", [128, NOWN * NT])
    oh_d = DI("ohrep", [128, NOWN * H * NT])
    hs_d = DI("haloscale", [128, NOWN])
    outT = nc.dram_tensor("outT", [D, NOWN * TN], F32, kind="ExternalOutput").ap()

    KA = DS("KA", [H, 66, S], BF16)
    VA = DS("VA", [H, 128, 128, 65], BF16)
    QA = DS("QA", [H, 66, NOWN * TN], BF16)
    MC = DS("MC", [512, NOWN * TN], BF16)
    AT = DS("AT", [H, 65, NOWN * TN], F32)
    X1 = DS("X1", [D, NOWN * TN], F32)
    H2 = DS("H2", [D, NOWN * TN], BF16)

    es = ExitStack()
    E = es.enter_context
    AW = 52800
    arena_t = E(nc.sbuf_tensor("arena", [128, AW], F32))
    psum_t = E(nc.psum_tensor("psum", [128, 4096], F32))
    k = KB(nc, es)
    ar = Arena(arena_t[:, :], AW)

    def PS(bank, nb=1):
        return psum_t[:, bank * 512:(bank + nb) * 512]

    psres = [k.res("psb%d" % i) for i in range(8)]
    pe, act, dve, pool = k.pe, k.act, k.dve, k.pool

    cst = ar.alloc((8,), F32)
    r_cst = k.res("cst")
    pool.run(lambda e: e.memset(cst[:, 0:1], EPS), writes=[r_cst])
    pool.run(lambda e: e.memset(cst[:, 1:2], float(np.log(0.125))), adds=[r_cst])
    pool.run(lambda e: e.memset(cst[:, 2:3], 1.0), adds=[r_cst])
    pool.run(lambda e: e.memset(cst[:, 3:4], 0.0), adds=[r_cst])
    c_eps, c_ln8, c_one, c_zero = cst[:, 0:1], cst[:, 1:2], cst[:, 2:3], cst[:, 3:4]

    r_consts = k.res("consts", dma=True)

    def cload(shape, dt, src, q="sp"):
        v = ar.alloc(shape, dt)
        k.dma(q, v if len(shape) > 1 else v, src, r_consts, adds=[r_consts])
        return v

    ccol = ar.alloc((8,), F32); k.dma("sp", ccol, ccol_d, r_consts, adds=[r_consts])
    bada = ar.alloc((48,), F32); k.dma("sp", bada, bada_d, r_consts, adds=[r_consts])
    n1g = ar.alloc((8,), F32); k.dma("sp", n1g, n1g_d, r_consts, adds=[r_consts])
    n2g = ar.alloc((8,), F32); k.dma("sp", n2g, n2g_d, r_consts, adds=[r_consts])
    qg = ar.alloc((1,), F32); k.dma("sp", qg, qg_d, r_consts, adds=[r_consts])
    kg = ar.alloc((1,), F32); k.dma("sp", kg, kg_d, r_consts, adds=[r_consts])
    bfr = ar.alloc((32,), F32); k.dma("sp", bfr, bf_d, r_consts, adds=[r_consts])
    cwc = ar.alloc((4 * KW,), F32); k.dma("sp", cwc, cw_d, r_consts, adds=[r_consts])
    cbc = ar.alloc((4,), F32); k.dma("sp", cbc, cb_d, r_consts, adds=[r_consts])
    clg = ar.alloc((4,), F32); k.dma("sp", clg, clg_d, r_consts, adds=[r_consts])
    clb = ar.alloc((4,), F32); k.dma("sp", clb, clb_d, r_consts, adds=[r_consts])
    bcc = ar.alloc((4,), F32); k.dma("sp", bcc, bc_d, r_consts, adds=[r_consts])
    bac = ar.alloc((4,), F32); k.dma("sp", bac, ba_d, r_consts, adds=[r_consts])
    ident = ar.alloc((128,), F32); k.dma("sp", ident, ident_d, r_consts, adds=[r_consts])
    tri = ar.alloc((128,), F32); k.dma("sp", tri, tri_d, r_consts, adds=[r_consts])
    cmask = ar.alloc((NOWN, NT), F32); k.dma("sp", cmask, cmask_d.rearrange("p (a b) -> p a b", a=NOWN), r_consts, adds=[r_consts])
    hsc = ar.alloc((NOWN,), F32); k.dma("sp", hsc, hs_d, r_consts, adds=[r_consts])

    onesf = ar.alloc((128,), F32)
    onesb = ar.alloc((128,), BF16)
    blk1 = ar.alloc((128,), BF16)
    r_ones = k.res("ones")
    pool.run(lambda e: e.memset(onesf, 1.0), writes=[r_ones])
    pool.run(lambda e: e.memset(onesb, 1.0), adds=[r_ones])
    pool.run(lambda e: e.memset(blk1, 0.0), adds=[r_ones])
    pool.run(lambda e: e.memset(blk1[0:64, 0:64], 1.0), reads=[r_ones], adds=[r_ones])
    pool.run(lambda e: e.memset(blk1[64:128, 64:128], 1.0), adds=[r_ones])

    modc = ar.alloc((48,), F32)
    gm1 = ar.alloc((8,), F32)
    gm2 = ar.alloc((8,), F32)
    r_mod = k.res("mod")
    sh1, g1c, sh2, g2c = modc[:, 0:8], modc[:, 16:24], modc[:, 24:32], modc[:, 40:48]
    GRt = ar.alloc((NT + 1, H), F32)
    r_GR = k.res("GR")
    pool.run(lambda e: e.memset(GRt[:, 0, :], 0.0), writes=[r_GR])

    pmark = ar.mark()
    xqT_v = xqT.rearrange("(kc p) (i n) -> p kc i n", p=128, n=TNH)
    AT_p = AT.rearrange("(pr two) d t -> two d pr t", two=2)
    MC_v = MC.rearrange("(c p) t -> p c t", p=128)
    X1_v = X1.rearrange("(kc p) t -> p kc t", p=128)
    H2_v = H2.rearrange("(kc p) t -> p kc t", p=128)
    r_KA = k.res("KA"); r_VA = k.res("VA"); r_QA = k.res("QA"); r_MC = k.res("MC"); r_AT = k.res("AT"); r_X1 = k.res("X1"); r_H2 = k.res("H2")

    def phase0():
        sccol = ar.alloc((8,), F32)
        tmp8 = ar.alloc((8,), F32)
        r_sc = k.res("sc")
        act.run(lambda e: e.activation(out=tmp8, in_=ccol, func=AF.Exp, bias=c_zero, scale=-1.0), reads=[r_consts, r_cst], writes=[r_sc])
        dve.run(lambda e: e.tensor_scalar(out=tmp8, in0=tmp8, scalar1=1.0, scalar2=None, op0=ALU.add), writes=[r_sc])
        dve.run(lambda e: e.reciprocal(out=tmp8, in_=tmp8), writes=[r_sc])
        dve.run(lambda e: e.tensor_tensor(out=sccol, in0=ccol, in1=tmp8, op=ALU.mult), writes=[r_sc])
        wsec = [ar.alloc((8, 1024), BF16) for _ in range(2)]
        scb = ar.alloc((8,), BF16)
        dve.run(lambda e: e.tensor_copy(out=scb, in_=sccol), writes=[r_sc])
        r_wsec = [k.res("wsec%d" % i, dma=True) for i in range(2)]
        wada_v = wada_d.rearrange("(kc p) n -> p kc n", p=128)
        psm = PS(0)
        for sec in range(6):
            b = sec % 2
            for half in range(2):
                k.dma("pool", wsec[b][:, half * 4:(half + 1) * 4, :], wada_v[:, half * 4:(half + 1) * 4, sec * 1024:(sec + 1) * 1024], r_wsec[b],
                      writes=[r_wsec[b]] if half == 0 else (), adds=() if half == 0 else [r_wsec[b]])

            def mm_sec(e, sec=sec, b=b):
                last = None
                for oc in range(8):
                    col = sec * 8 + oc
                    for kc in range(8):
                        last = e.matmul(psm[:, col:col + 1], lhsT=wsec[b][:, kc, oc * 128:(oc + 1) * 128], rhs=scb[:, kc:kc + 1],
                                        start=(kc == 0), stop=(kc == 7))
                return last
            pe.run(mm_sec, reads=[r_wsec[b], r_sc], writes=[psres[0]] if sec == 0 else (), adds=() if sec == 0 else [psres[0]])
        dve.run(lambda e: e.tensor_tensor(out=modc, in0=psm[:, 0:48], in1=bada, op=ALU.add), reads=[psres[0], r_consts], writes=[r_mod])
        dve.run(lambda e: e.tensor_scalar(out=gm1, in0=modc[:, 8:16], scalar1=1.0, scalar2=None, op0=ALU.add), reads=[r_mod], writes=[r_sc])
        dve.run(lambda e: e.tensor_tensor(out=gm1, in0=gm1, in1=n1g, op=ALU.mult), writes=[r_sc])
        dve.run(lambda e: e.tensor_scalar(out=gm2, in0=modc[:, 32:40], scalar1=1.0, scalar2=None, op0=ALU.add), writes=[r_sc])
        dve.run(lambda e: e.tensor_tensor(out=gm2, in0=gm2, in1=n2g, op=ALU.mult), writes=[r_sc, r_mod])
        k.barrier()
        ar.reset(pmark)

    phase0()
    def norm_mod(xt, r_x, N, gm, sh, sq, r_sq, rstd, r_rstd, tmpf, r_tmpf, hT, r_hT, ps_bank):
        pss = PS(ps_bank, 2)
        r_ps = [psres[ps_bank], psres[ps_bank + 1]]
        dve.run(lambda e: e.tensor_tensor(out=sq, in0=xt, in1=xt, op=ALU.mult), reads=[r_x], writes=[r_sq])

        def mm(e):
            last = None
            for kc in range(8):
                last = e.matmul(pss[:, 0:min(N, 512)], lhsT=onesb, rhs=sq[:, kc, 0:min(N, 512)], start=(kc == 0), stop=(kc == 7))
            if N > 512:
                for kc in range(8):
                    last = e.matmul(pss[:, 512:N], lhsT=onesb, rhs=sq[:, kc, 512:N], start=(kc == 0), stop=(kc == 7))
            return last
        pe.run(mm, reads=[r_sq, r_ones], writes=r_ps)
        act.run(lambda e: e.activation(out=rstd, in_=pss[:, 0:N], func=AF.Ln, bias=c_eps, scale=1.0 / D), reads=r_ps, writes=[r_rstd])
        act.run(lambda e: e.activation(out=rstd, in_=rstd, func=AF.Exp, bias=c_zero, scale=-0.5), writes=[r_rstd])
        for kc in range(8):
            tb = kc % 2
            dve.run(lambda e, kc=kc, tb=tb: e.tensor_tensor(out=tmpf[tb], in0=xt[:, kc, :], in1=rstd, op=ALU.mult),
                    reads=[r_x, r_rstd], writes=[r_tmpf[tb]])
            act.run(lambda e, kc=kc, tb=tb: e.activation(out=hT[:, kc, :], in_=tmpf[tb], func=AF.Identity, bias=sh[:, kc:kc + 1], scale=gm[:, kc:kc + 1]),
                    reads=[r_tmpf[tb], r_mod], writes=[r_hT] if kc == 0 else (), adds=() if kc == 0 else [r_hT])

    def pair_norm(ps_raw, r_raw, gcol, lnbias, raw, r_rawsb, sqb, r_sqb, ps_ss, r_ss, rs, r_rs, out, r_out, first, steps=("sq", "rest")):
        if "sq" in steps:
            act.run(lambda e: e.activation(out=sqb, in_=ps_raw, func=AF.Square, bias=c_zero, scale=1.0), reads=[r_raw, r_cst], writes=[r_sqb])
        if "rest" in steps:
            pe.run(lambda e: e.matmul(ps_ss, lhsT=blk1, rhs=sqb, start=True, stop=True), reads=[r_sqb, r_ones], writes=[r_ss])
            act.run(lambda e: e.activation(out=rs, in_=ps_ss, func=AF.Ln, bias=c_eps, scale=1.0 / DH), reads=[r_ss, r_cst], writes=[r_rs])
            act.run(lambda e: e.activation(out=rs, in_=rs, func=AF.Exp, bias=lnbias, scale=-0.5), writes=[r_rs])
            dve.run(lambda e: e.scalar_tensor_tensor(out=out, in0=ps_raw, scalar=gcol, in1=rs, op0=ALU.mult, op1=ALU.mult),
                    reads=[r_raw, r_rs, r_consts], writes=[r_out] if first else (), adds=() if first else [r_out])

    def fgate(hT, r_hT, wfg, r_w, ps_fg, r_psfg, zt, r_zt, spt, r_spt, ps_gk, r_psgk, ps_tot, r_pstot, want_tot, part=0):
        def mm(e):
            last = None
            for b in range(4):
                for kc in range(8):
                    last = e.matmul(ps_fg[:, b * 8:(b + 1) * 8], lhsT=hT[:, kc, b * 128:(b + 1) * 128], rhs=wfg[:, kc, :],
                                    start=(kc == 0), stop=(kc == 7))
            return last
        if part in (0, 1):
            pe.run(mm, reads=[r_hT, r_w], writes=[r_psfg])
            dve.run(lambda e: e.tensor_tensor(out=zt, in0=ps_fg[:, 0:32], in1=bfr, op=ALU.add), reads=[r_psfg, r_consts], writes=[r_zt])
            act.run(lambda e: e.activation(out=zt, in_=zt, func=AF.Exp, bias=c_zero, scale=-1.0), reads=[r_cst], writes=[r_zt])
            act.run(lambda e: e.activation(out=spt, in_=zt, func=AF.Ln, bias=c_one, scale=1.0), reads=[r_zt], writes=[r_spt])
        if part == 1:
            return

        def mm2(e):
            last = None
            for b in range(4):
                last = e.matmul(ps_gk[0:8, b * 128:(b + 1) * 128], lhsT=spt[:, b * 8:(b + 1) * 8], rhs=tri, start=True, stop=(b == 0))
                for b2 in range(b):
                    last = e.matmul(ps_gk[0:8, b * 128:(b + 1) * 128], lhsT=spt[:, b2 * 8:(b2 + 1) * 8], rhs=onesf, start=False, stop=(b2 == b - 1))
            return last
        pe.run(mm2, reads=[r_spt, r_consts, r_ones], writes=[r_psgk])
        if want_tot:
            def mm3(e):
                last = None
                for b in range(4):
                    last = e.matmul(ps_tot[:, 0:8], lhsT=onesf, rhs=spt[:, b * 8:(b + 1) * 8], start=(b == 0), stop=(b == 3))
                return last
            pe.run(mm3, reads=[r_spt, r_ones], writes=[r_pstot])

    def phase1a():
        wk = ar.alloc((8, 512), BF16)
        wv = ar.alloc((8, 512), BF16)
        wfg = ar.alloc((8, 8), BF16)
        r_w1a = k.res("w1a", dma=True)
        k.dma("pool", wk, wk_d.rearrange("(kc p) n -> p kc n", p=128), r_w1a, adds=[r_w1a])
        k.dma("pool", wv, wv_d.rearrange("(kc p) n -> p kc n", p=128), r_w1a, adds=[r_w1a])
        k.dma("pool", wfg, wfg_d.rearrange("(kc p) n -> p kc n", p=128), r_w1a, adds=[r_w1a])
        xt = [ar.alloc((8, TN), F32) for _ in range(2)]
        r_xt = [k.res("xt%d" % i, dma=True) for i in range(2)]
        sq = ar.alloc((8, TN), BF16); r_sq = k.res("sq")
        rstd = ar.alloc((TN,), F32); r_rstd = k.res("rstd")
        tmpf = [ar.alloc((TN,), F32) for _ in range(2)]
        r_tmpf = [k.res("tmpf%d" % i) for i in range(2)]
        hT = [ar.alloc((8, TN), BF16) for _ in range(2)]
        r_hT = [k.res("hT%d" % i) for i in range(2)]
        raw = [ar.alloc((TN,), F32) for _ in range(2)]
        r_raw = [k.res("raw%d" % i) for i in range(2)]
        sqb = [ar.alloc((TN,), BF16) for _ in range(2)]
        r_sqb = [k.res("sqb%d" % i) for i in range(2)]
        rs = [ar.alloc((TN,), F32) for _ in range(2)]
        r_rs = [k.res("rs%d" % i) for i in range(2)]
        kn = [ar.alloc((4, TN), BF16) for _ in range(2)]
        r_kn = [k.res("kn%d" % i, dma=True) for i in range(2)]
        vaug = [ar.alloc((H, 4, 65), BF16) for _ in range(2)]
        r_vaug = [k.res("vaug%d" % i, dma=True) for i in range(2)]
        zt = ar.alloc((32,), F32); r_zt = k.res("zt")
        spt = ar.alloc((32,), F32); r_spt = k.res("spt")
        tot32 = ar.alloc((32,), F32); r_tot8 = k.res("tot8")
        ghl = [ar.alloc((2, TN), BF16) for _ in range(2)]
        r_ghl = [k.res("ghl%d" % i, dma=True) for i in range(2)]
        for b in range(2):
            pool.run(lambda e, b=b: e.memset(vaug[b], 1.0), writes=[r_vaug[b]])

        xT_v = xT.rearrange("(kc p) t -> p kc t", p=128)
        def load_xt(t):
            b = t % 2
            for half in range(2):
                k.dma("sp", xt[b][:, half * 4:(half + 1) * 4, :], xT_v[:, half * 4:(half + 1) * 4, t * TN:(t + 1) * TN], r_xt[b],
                      writes=[r_xt[b]] if half == 0 else (), adds=() if half == 0 else [r_xt[b]])
        def A_sq(t):
            b = t % 2
            dve.run(lambda e, b=b: e.tensor_tensor(out=sq, in0=xt[b], in1=xt[b], op=ALU.mult), reads=[r_xt[b]], writes=[r_sq])

        def A_ss(t):
            def mm(e):
                last = None
                for kc in range(8):
                    last = e.matmul(PS(0), lhsT=onesb, rhs=sq[:, kc, :], start=(kc == 0), stop=(kc == 7))
                return last
            pe.run(mm, reads=[r_sq, r_ones], writes=[psres[0]])

        def A_rstd(t):
            act.run(lambda e: e.activation(out=rstd, in_=PS(0), func=AF.Ln, bias=c_eps, scale=1.0 / D), reads=[psres[0], r_cst], writes=[r_rstd])
            act.run(lambda e: e.activation(out=rstd, in_=rstd, func=AF.Exp, bias=c_zero, scale=-0.5), writes=[r_rstd])

        def A_u(t, kcs):
            b = t % 2
            for kc in kcs:
                tb = kc % 2
                dve.run(lambda e, kc=kc, tb=tb, b=b: e.tensor_tensor(out=tmpf[tb], in0=xt[b][:, kc, :], in1=rstd, op=ALU.mult),
                        reads=[r_xt[b], r_rstd], writes=[r_tmpf[tb]])
                act.run(lambda e, kc=kc, tb=tb, b=b: e.activation(out=hT[b][:, kc, :], in_=tmpf[tb], func=AF.Identity, bias=sh1[:, kc:kc + 1], scale=gm1[:, kc:kc + 1]),
                        reads=[r_tmpf[tb], r_mod], writes=[r_hT[b]] if kc == 0 else (), adds=() if kc == 0 else [r_hT[b]])

        load_xt(0)
        load_xt(1)
        A_sq(0); A_ss(0); A_rstd(0); A_u(0, range(8))
        psvb = [5, 1]
        for t in range(NT):
            b = t % 2
            nxt = t + 1 < NT
            if t + 2 < NT:
                load_xt(t + 2)

            def mmk(pr, b=b):
                pb = pr % 2
                psk = PS(2 + pb)

                def f(e, pr=pr, psk=psk, b=b):
                    last = None
                    for kc in range(8):
                        last = e.matmul(psk, lhsT=wk[:, kc, pr * 128:(pr + 1) * 128], rhs=hT[b][:, kc, :], start=(kc == 0), stop=(kc == 7))
                    return last
                pe.run(f, reads=[r_hT[b], r_w1a], writes=[psres[2 + pb]])

            def ksq(pr):
                pb = pr % 2
                act.run(lambda e, pb=pb: e.activation(out=sqb[pb], in_=PS(2 + pb), func=AF.Square, bias=c_zero, scale=1.0),
                        reads=[psres[2 + pb], r_cst], writes=[r_sqb[pb]])

            def kss(pr):
                pb = pr % 2
                pe.run(lambda e, pb=pb: e.matmul(PS(4), lhsT=blk1, rhs=sqb[pb], start=True, stop=True), reads=[r_sqb[pb], r_ones], writes=[psres[4]])

            def klnexp(pr):
                pb = pr % 2
                act.run(lambda e, pb=pb: e.activation(out=rs[pb], in_=PS(4), func=AF.Ln, bias=c_eps, scale=1.0 / DH), reads=[psres[4], r_cst], writes=[r_rs[pb]])
                act.run(lambda e, pb=pb: e.activation(out=rs[pb], in_=rs[pb], func=AF.Exp, bias=c_zero, scale=-0.5), writes=[r_rs[pb]])

            def kn_(pr, b=b):
                pb = pr % 2
                dve.run(lambda e, pr=pr, pb=pb, b=b: e.scalar_tensor_tensor(out=kn[b][:, pr, :], in0=PS(2 + pb), scalar=kg[:, 0:1], in1=rs[pb], op0=ALU.mult, op1=ALU.mult),
                        reads=[psres[2 + pb], r_rs[pb], r_consts], writes=[r_kn[b]] if pr == 0 else (), adds=() if pr == 0 else [r_kn[b]])

            def mmv(blk, b=b):
                bank = psvb[blk % 2]
                psv = PS(bank)

                def f(e, blk=blk, b=b, psv=psv):
                    last = None
                    for kc in range(8):
                        last = e.matmul(psv, lhsT=hT[b][:, kc, blk * 128:(blk + 1) * 128], rhs=wv[:, kc, :], start=(kc == 0), stop=(kc == 7))
                    return last
                pe.run(f, reads=[r_hT[b], r_w1a], writes=[psres[bank]])

            def vcast(blk, b=b):
                bank = psvb[blk % 2]
                dve.run(lambda e, blk=blk, b=b, bank=bank: e.tensor_copy(out=vaug[b][:, :, blk, 0:64], in_=PS(bank).rearrange("p (h d) -> p h d", h=H)),
                        reads=[psres[bank]], writes=[r_vaug[b]] if blk == 0 else (), adds=() if blk == 0 else [r_vaug[b]])

            if nxt:
                A_sq(t + 1)
            ps_gk = PS(7)

            def fg_a(b=b):
                def mm(e, b=b):
                    last = None
                    for blk in range(4):
                        for kc in range(8):
                            last = e.matmul(PS(6)[:, blk * 8:(blk + 1) * 8], lhsT=hT[b][:, kc, blk * 128:(blk + 1) * 128], rhs=wfg[:, kc, :],
                                            start=(kc == 0), stop=(kc == 7))
                    return last
                pe.run(mm, reads=[r_hT[b], r_w1a], writes=[psres[6]])
                dve.run(lambda e: e.tensor_tensor(out=zt, in0=PS(6)[:, 0:32], in1=bfr, op=ALU.add), reads=[psres[6], r_consts], writes=[r_zt])
                act.run(lambda e: e.activation(out=zt, in_=zt, func=AF.Exp, bias=c_zero, scale=-1.0), reads=[r_cst], writes=[r_zt])
                act.run(lambda e: e.activation(out=spt, in_=zt, func=AF.Ln, bias=c_one, scale=1.0), reads=[r_zt], writes=[r_spt])

            def fg_b():
                def mm2(e):
                    last = None
                    for blk in range(4):
                        last = e.matmul(ps_gk[0:8, blk * 128:(blk + 1) * 128], lhsT=spt[:, blk * 8:(blk + 1) * 8], rhs=tri, start=True, stop=(blk == 0))
                        for b2 in range(blk):
                            last = e.matmul(ps_gk[0:8, blk * 128:(blk + 1) * 128], lhsT=spt[:, b2 * 8:(b2 + 1) * 8], rhs=onesf, start=False, stop=(b2 == blk - 1))
                    return last
                pe.run(mm2, reads=[r_spt, r_consts, r_ones], writes=[psres[7]])

            mmk(0); mmk(1); fg_a(); ksq(0); mmv(0); ksq(1); kss(0)
            if nxt:
                A_ss(t + 1)
            klnexp(0); vcast(0); mmv(1); kn_(0); kss(1); klnexp(1)
            if nxt:
                A_rstd(t + 1)
            vcast(1); mmk(2); kn_(1); mmk(3); ksq(2); mmv(2); ksq(3)
            if nxt:
                A_u(t + 1, range(0, 4))
            kss(2); klnexp(2); vcast(2); mmv(3); kn_(2); kss(3); klnexp(3); vcast(3); kn_(3)
            if nxt:
                A_u(t + 1, range(4, 8))
            for pr in range(4):
                for hh in range(2):
                    k.dma("sp", KA[2 * pr + hh, 0:64, t * TN:(t + 1) * TN], kn[b][hh * 64:(hh + 1) * 64, pr, :], r_kn[b], reads=[r_kn[b]], adds=[r_KA])
            k.dma("sp", VA[:, :, t * 4:(t + 1) * 4, :].rearrange("h p k e -> p h k e"), vaug[b], r_vaug[b], reads=[r_vaug[b]], adds=[r_VA])
            fg_b()
            pe.run(lambda e: e.matmul(PS(4)[:, 0:32], lhsT=onesf, rhs=spt[:, 0:32], start=True, stop=True), reads=[r_spt, r_ones], writes=[psres[4]])
            dve.run(lambda e: e.tensor_copy(out=tot32, in_=PS(4)[:, 0:32]), reads=[psres[4]], writes=[r_tot8])
            dve.run(lambda e: e.tensor_tensor(out=tot32[:, 0:16], in0=tot32[:, 0:16], in1=tot32[:, 16:32], op=ALU.add), writes=[r_tot8])
            dve.run(lambda e: e.tensor_tensor(out=tot32[:, 0:8], in0=tot32[:, 0:8], in1=tot32[:, 8:16], op=ALU.add), writes=[r_tot8])
            dve.run(lambda e, t=t: e.tensor_tensor(out=GRt[:, t + 1, :], in0=tot32[:, 0:8], in1=GRt[:, t, :], op=ALU.add),
                    reads=[r_tot8], writes=[r_GR])
            dve.run(lambda e, b=b: e.tensor_copy(out=ghl[b][0:8, 0, :], in_=ps_gk[0:8, :]), reads=[psres[7]], writes=[r_ghl[b]])
            dve.run(lambda e, b=b: e.tensor_tensor(out=ghl[b][0:8, 1, :], in0=ps_gk[0:8, :], in1=ghl[b][0:8, 0, :], op=ALU.subtract),
                    reads=[psres[7]], writes=[r_ghl[b]])
            k.dma("sp", KA[:, 64:66, t * TN:(t + 1) * TN], ghl[b][0:8, :, :], r_ghl[b], reads=[r_ghl[b]], adds=[r_KA])
        k.barrier()
        ar.reset(pmark)

    phase1a()
    def phase1b():
        wq = ar.alloc((8, 512), BF16)
        wc = ar.alloc((8, 1024), BF16)
        wfg = ar.alloc((8, 8), BF16)
        r_w1b = k.res("w1b", dma=True)
        k.dma("pool", wq, wq_d.rearrange("(kc p) n -> p kc n", p=128), r_w1b, adds=[r_w1b])
        k.dma("pool", wc, wc_d.rearrange("(kc p) n -> p kc n", p=128), r_w1b, adds=[r_w1b])
        k.dma("pool", wfg, wfg_d.rearrange("(kc p) n -> p kc n", p=128), r_w1b, adds=[r_w1b])
        Dg = ar.alloc((4, KW, 128), BF16)
        r_Dg = k.res("Dg")
        first = True
        for c in range(4):
            for kk in range(KW):
                eng = dve
                eng.run(lambda e, c=c, kk=kk: e.tensor_scalar(out=Dg[:, c, kk, :], in0=ident, scalar1=cwc[:, c * KW + kk:c * KW + kk + 1], scalar2=None, op0=ALU.mult),
                        reads=[r_consts], writes=[r_Dg] if first else (), adds=() if first else [r_Dg])
                first = False
        xq = [ar.alloc((8, TNH), F32) for _ in range(2)]
        r_xq = [k.res("xq%d" % i, dma=True) for i in range(2)]
        sq = ar.alloc((8, TNH), BF16); r_sq = k.res("sqb")
        rstd = ar.alloc((TNH,), F32); r_rstd = k.res("rstdb")
        tmpf = [ar.alloc((TNH,), F32) for _ in range(2)]
        r_tmpf = [k.res("tmpfb%d" % i) for i in range(2)]
        hq = [ar.alloc((8, TNH), BF16) for _ in range(2)]
        r_hq = [k.res("hq%d" % i) for i in range(2)]
        raw = [ar.alloc((TN,), F32) for _ in range(2)]
        r_raw = [k.res("rawb%d" % i) for i in range(2)]
        sqb = [ar.alloc((TN,), BF16) for _ in range(2)]
        r_sqb = [k.res("sqbb%d" % i) for i in range(2)]
        rs = [ar.alloc((TN,), F32) for _ in range(2)]
        r_rs = [k.res("rsb%d" % i) for i in range(2)]
        qn = [ar.alloc((4, TN), BF16) for _ in range(2)]
        r_qn = [k.res("qn%d" % i, dma=True) for i in range(2)]
        zt = ar.alloc((32,), F32); r_zt = k.res("ztb")
        spt = ar.alloc((32,), F32); r_spt = k.res("sptb")
        dhl = [ar.alloc((2, TN), BF16) for _ in range(2)]
        r_dhl = [k.res("dhl%d" % i, dma=True) for i in range(2)]
        eg = ar.alloc((TNH,), F32); r_eg = k.res("eg")
        ubuf = ar.alloc((4, TNH), BF16); r_ub = k.res("ubuf")
        ybuf = ar.alloc((4, TN), F32); r_yb = k.res("ybuf")
        ybf = ar.alloc((4, TN), BF16); r_ybf = k.res("ybf")
        ysq = ar.alloc((4, TN), BF16); r_ysq = k.res("ysq")
        mean = ar.alloc((TN,), F32); r_mean = k.res("mean")
        var = ar.alloc((TN,), F32); r_var = k.res("var")
        tcv = ar.alloc((TN,), F32); r_tcv = k.res("tcv")
        u2 = ar.alloc((4, TN), F32); r_u2 = k.res("u2")
        u2s = ar.alloc((4, TN), BF16); r_u2s = k.res("u2s")
        rs3 = ar.alloc((TN,), F32); r_rs3 = k.res("rs3")
        mcb = [ar.alloc((4, TN), BF16) for _ in range(2)]
        r_mcb = [k.res("mcb%d" % i, dma=True) for i in range(2)]

        def load_xq(i):
            b = i % 2
            for half in range(2):
                k.dma("sp", xq[b][:, half * 4:(half + 1) * 4, :], xqT_v[:, half * 4:(half + 1) * 4, i, :], r_xq[b],
                      writes=[r_xq[b]] if half == 0 else (), adds=() if half == 0 else [r_xq[b]])
        def A_sq(i):
            b = i % 2
            dve.run(lambda e, b=b: e.tensor_tensor(out=sq, in0=xq[b], in1=xq[b], op=ALU.mult), reads=[r_xq[b]], writes=[r_sq])

        def A_ss(i):
            def mm(e):
                last = None
                for kc in range(8):
                    last = e.matmul(PS(0), lhsT=onesb, rhs=sq[:, kc, 0:TN], start=(kc == 0), stop=(kc == 7))
                for kc in range(8):
                    last = e.matmul(PS(1)[:, 0:HALO], lhsT=onesb, rhs=sq[:, kc, TN:TNH], start=(kc == 0), stop=(kc == 7))
                return last
            pe.run(mm, reads=[r_sq, r_ones], writes=[psres[0], psres[1]])

        def A_rstd(i):
            act.run(lambda e: e.activation(out=rstd, in_=PS(0, 2)[:, 0:TNH], func=AF.Ln, bias=c_eps, scale=1.0 / D), reads=[psres[0], psres[1], r_cst], writes=[r_rstd])
            act.run(lambda e: e.activation(out=rstd, in_=rstd, func=AF.Exp, bias=c_zero, scale=-0.5), writes=[r_rstd])

        def A_u(i, kcs):
            b = i % 2
            for kc in kcs:
                tb = kc % 2
                dve.run(lambda e, kc=kc, tb=tb, b=b: e.tensor_tensor(out=tmpf[tb], in0=xq[b][:, kc, :], in1=rstd, op=ALU.mult),
                        reads=[r_xq[b], r_rstd], writes=[r_tmpf[tb]])
                act.run(lambda e, kc=kc, tb=tb, b=b: e.activation(out=hq[b][:, kc, :], in_=tmpf[tb], func=AF.Identity, bias=sh1[:, kc:kc + 1], scale=gm1[:, kc:kc + 1]),
                        reads=[r_tmpf[tb], r_mod], writes=[r_hq[b]] if kc == 0 else (), adds=() if kc == 0 else [r_hq[b]])
        load_xq(0)
        load_xq(1)
        A_sq(0); A_ss(0); A_rstd(0); A_u(0, range(8))
        for i in range(NOWN):
            b = i % 2
            nxt = i + 1 < NOWN
            if i + 2 < NOWN:
                load_xq(i + 2)
            if nxt:
                A_sq(i + 1)
            fgate(hq[b], r_hq[b], wfg, r_w1b, PS(6), psres[6], zt, r_zt, spt, r_spt, PS(7), psres[7], None, None, False, part=1)
            def q_mm(pr, b=b):
                pb = pr % 2
                psq = PS(2 + pb)

                def mmq(e, pr=pr, psq=psq, hqb=hq[b]):
                    last = None
                    for kc in range(8):
                        last = e.matmul(psq, lhsT=wq[:, kc, pr * 128:(pr + 1) * 128], rhs=hqb[:, kc, 0:TN], start=(kc == 0), stop=(kc == 7))
                    return last
                pe.run(mmq, reads=[r_hq[b], r_w1b], writes=[psres[2 + pb]])

            def q_norm(pr, steps, b=b):
                pb = pr % 2
                pair_norm(PS(2 + pb), psres[2 + pb], qg[:, 0:1], c_ln8, raw[pb], r_raw[pb], sqb[pb], r_sqb[pb], PS(4), psres[4], rs[pb], r_rs[pb],
                          qn[b][:, pr, :], r_qn[b], pr == 0, steps=steps)
            for p0 in (0, 2):
                q_mm(p0); q_mm(p0 + 1)
                q_norm(p0, ("sq",)); q_norm(p0 + 1, ("sq",))
                q_norm(p0, ("rest",)); q_norm(p0 + 1, ("rest",))
            for pr in range(4):
                for hh in range(2):
                    k.dma("sp", QA[2 * pr + hh, 0:64, i * TN:(i + 1) * TN], qn[b][hh * 64:(hh + 1) * 64, pr, :], r_qn[b], reads=[r_qn[b]], adds=[r_QA])
            if nxt:
                A_ss(i + 1)
            ps_gk = PS(7)
            fgate(hq[b], r_hq[b], wfg, r_w1b, PS(6), psres[6], zt, r_zt, spt, r_spt, ps_gk, psres[7], None, None, False, part=2)
            dve.run(lambda e, b=b: e.tensor_scalar(out=dhl[b][0:8, 0, :], in0=ps_gk[0:8, :], scalar1=-1.0, scalar2=None, op0=ALU.mult),
                    reads=[psres[7]], writes=[r_dhl[b]])
            dve.run(lambda e, b=b: e.scalar_tensor_tensor(out=dhl[b][0:8, 1, :], in0=ps_gk[0:8, :], scalar=-1.0, in1=dhl[b][0:8, 0, :],
                                                          op0=ALU.mult, op1=ALU.subtract),
                    reads=[psres[7]], writes=[r_dhl[b]])
            k.dma("sp", QA[:, 64:66, i * TN:(i + 1) * TN], dhl[b][0:8, :, :], r_dhl[b], reads=[r_dhl[b]], adds=[r_QA])
            if nxt:
                A_rstd(i + 1)
            for c in range(4):
                psl = PS(2, 2)
                psg = PS(4, 2)

                def mmc(e, c=c, psl=psl, psg=psg, hqb=hq[b]):
                    last = None
                    for (ps_, col0) in ((psl, c * 128), (psg, 512 + c * 128)):
                        for kc in range(8):
                            last = e.matmul(ps_[:, 0:TN], lhsT=wc[:, kc, col0:col0 + 128], rhs=hqb[:, kc, 0:TN], start=(kc == 0), stop=(kc == 7))
                        for kc in range(8):
                            last = e.matmul(ps_[:, TN:TNH], lhsT=wc[:, kc, col0:col0 + 128], rhs=hqb[:, kc, TN:TNH], start=(kc == 0), stop=(kc == 7))
                    return last
                pe.run(mmc, reads=[r_hq[b], r_w1b], writes=[psres[2], psres[3], psres[4], psres[5]])
                act.run(lambda e, psg=psg: e.activation(out=eg, in_=psg[:, 0:TNH], func=AF.Sigmoid, bias=c_zero, scale=1.0),
                        reads=[psres[4], psres[5], r_cst], writes=[r_eg])
                dve.run(lambda e, c=c, psl=psl: e.tensor_tensor(out=ubuf[:, c, HALO:TNH], in0=psl[:, 0:TN], in1=eg[:, 0:TN], op=ALU.mult),
                        reads=[psres[2], psres[3], r_eg], writes=[r_ub] if c == 0 else (), adds=() if c == 0 else [r_ub])
                dve.run(lambda e, c=c, psl=psl, i=i: e.scalar_tensor_tensor(out=ubuf[:, c, 0:HALO], in0=psl[:, TN:TNH], scalar=hsc[:, i:i + 1], in1=eg[:, TN:TNH],
                                                                            op0=ALU.mult, op1=ALU.mult),
                        reads=[psres[2], psres[3], r_eg, r_consts], adds=[r_ub])
            if nxt:
                A_u(i + 1, range(8))
            for c in range(4):
                psy = PS(2 + (c % 2))

                def mmy(e, c=c, psy=psy):
                    last = None
                    for kk in range(KW):
                        last = e.matmul(psy, lhsT=Dg[:, c, kk, :], rhs=ubuf[:, c, kk:kk + TN], start=(kk == 0), stop=(kk == KW - 1))
                    return last
                pe.run(mmy, reads=[r_ub, r_Dg], writes=[psres[2 + (c % 2)]])
                act.run(lambda e, c=c, psy=psy: e.activation(out=ybuf[:, c, :], in_=psy, func=AF.Identity, bias=cbc[:, c:c + 1], scale=1.0),
                        reads=[psres[2 + (c % 2)], r_consts], writes=[r_yb] if c == 0 else (), adds=() if c == 0 else [r_yb])
                dve.run(lambda e, c=c: e.tensor_copy(out=ybf[:, c, :], in_=ybuf[:, c, :]), reads=[r_yb], writes=[r_ybf] if c == 0 else (), adds=() if c == 0 else [r_ybf])
                act.run(lambda e, c=c: e.activation(out=ysq[:, c, :], in_=ybuf[:, c, :], func=AF.Square, bias=c_zero, scale=1.0),
                        reads=[r_yb, r_cst], writes=[r_ysq] if c == 0 else (), adds=() if c == 0 else [r_ysq])

            def mmln(e):
                last = None
                for c in range(4):
                    last = e.matmul(PS(4), lhsT=onesb, rhs=ybf[:, c, :], start=(c == 0), stop=(c == 3))
                for c in range(4):
                    last = e.matmul(PS(5), lhsT=onesb, rhs=ysq[:, c, :], start=(c == 0), stop=(c == 3))
                return last
            pe.run(mmln, reads=[r_ybf, r_ysq, r_ones], writes=[psres[4], psres[5]])
            dve.run(lambda e: e.tensor_scalar(out=mean, in0=PS(4), scalar1=1.0 / 512, scalar2=None, op0=ALU.mult), reads=[psres[4]], writes=[r_mean])
            dve.run(lambda e: e.tensor_tensor(out=tcv, in0=mean, in1=mean, op=ALU.mult), reads=[r_mean], writes=[r_tcv])
            dve.run(lambda e: e.scalar_tensor_tensor(out=var, in0=PS(5), scalar=1.0 / 512, in1=tcv, op0=ALU.mult, op1=ALU.subtract),
                    reads=[psres[5], r_tcv], writes=[r_var])
            act.run(lambda e: e.activation(out=var, in_=var, func=AF.Ln, bias=c_eps, scale=1.0), reads=[r_cst], writes=[r_var])
            act.run(lambda e: e.activation(out=var, in_=var, func=AF.Exp, bias=c_zero, scale=-0.5), writes=[r_var])
            for c in range(4):
                tb_, r_tb_ = (tcv, r_tcv) if c % 2 == 0 else (eg[:, 0:TN], r_eg)
                dve.run(lambda e, c=c, tb_=tb_: e.tensor_tensor(out=tb_, in0=ybuf[:, c, :], in1=mean, op=ALU.subtract), reads=[r_yb, r_mean], writes=[r_tb_])
                dve.run(lambda e, tb_=tb_: e.tensor_tensor(out=tb_, in0=tb_, in1=var, op=ALU.mult), reads=[r_var], writes=[r_tb_])
                act.run(lambda e, c=c, tb_=tb_: e.activation(out=u2[:, c, :], in_=tb_, func=AF.Silu, bias=clb[:, c:c + 1], scale=clg[:, c:c + 1]),
                        reads=[r_tb_, r_consts], writes=[r_u2] if c == 0 else (), adds=() if c == 0 else [r_u2])
                act.run(lambda e, c=c: e.activation(out=u2s[:, c, :], in_=u2[:, c, :], func=AF.Square, bias=c_zero, scale=1.0),
                        reads=[r_u2, r_cst], writes=[r_u2s] if c == 0 else (), adds=() if c == 0 else [r_u2s])

            def mmr(e):
                last = None
                for c in range(4):
                    last = e.matmul(PS(6), lhsT=onesb, rhs=u2s[:, c, :], start=(c == 0), stop=(c == 3))
                return last
            pe.run(mmr, reads=[r_u2s, r_ones], writes=[psres[6]])
            act.run(lambda e: e.activation(out=rs3, in_=PS(6), func=AF.Ln, bias=c_eps, scale=1.0 / 512), reads=[psres[6], r_cst], writes=[r_rs3])
            act.run(lambda e: e.activation(out=rs3, in_=rs3, func=AF.Exp, bias=c_zero, scale=-0.5), writes=[r_rs3])
            for c in range(4):
                dve.run(lambda e, c=c, b=b: e.scalar_tensor_tensor(out=mcb[b][:, c, :], in0=u2[:, c, :], scalar=bcc[:, c:c + 1], in1=rs3, op0=ALU.mult, op1=ALU.mult),
                        reads=[r_u2, r_rs3, r_consts], writes=[r_mcb[b]] if c == 0 else (), adds=() if c == 0 else [r_mcb[b]])
            k.dma("sp", MC.rearrange("(c p) t -> p c t", p=128)[:, :, i * TN:(i + 1) * TN], mcb[b], r_mcb[b], reads=[r_mcb[b]], adds=[r_MC])
        k.barrier()
        ar.reset(pmark)

    phase1b()
    def phase2():
        mtri = ar.alloc((4, TN), BF16)
        dsel = ar.alloc((32, 128), BF16)
        ohrep = ar.alloc((NOWN * H, NT), F32)
        r_c2 = k.res("c2", dma=True)
        k.dma("sp", mtri, mtri_d.rearrange("p (a b) -> p a b", a=4), r_c2, adds=[r_c2])
        k.dma("sp", dsel, dsel_d.rearrange("p (a b) -> p a b", a=32), r_c2, adds=[r_c2])
        k.dma("sp", ohrep, oh_d.rearrange("p (a b) -> p a b", b=NT), r_c2, adds=[r_c2])
        CB = ar.alloc((NOWN * H, NT), F32); r_CB = k.res("CB")
        seltmp = ar.alloc((H, NT), F32); r_seltmp = k.res("seltmp")
        sel = ar.alloc((H,), F32); r_sel = k.res("sel")
        GRv = GRt[:, 0:NT, :].rearrange("p t h -> p h t")
        for i in range(NOWN):
            dve.run(lambda e, i=i: e.tensor_tensor(out=seltmp, in0=GRv, in1=ohrep[:, i * H:(i + 1) * H, :], op=ALU.mult),
                    reads=[r_GR, r_c2], writes=[r_seltmp])
            dve.run(lambda e: e.tensor_reduce(out=sel, in_=seltmp, axis=AX.X, op=ALU.add), reads=[r_seltmp], writes=[r_sel])
            dve.run(lambda e: e.tensor_scalar(out=sel, in0=sel, scalar1=-1.0, scalar2=None, op0=ALU.mult), writes=[r_sel])
            for h in range(H):
                dve.run(lambda e, i=i, h=h: e.scalar_tensor_tensor(out=CB[:, i * H + h, :], in0=GRt[:, 0:NT, h], scalar=sel[:, h:h + 1], in1=cmask[:, i, :],
                                                                   op0=ALU.add, op1=ALU.add),
                        reads=[r_sel, r_GR, r_consts], writes=[r_CB] if (i == 0 and h == 0) else (), adds=() if (i == 0 and h == 0) else [r_CB])
        if debug:
            DBG = nc.dram_tensor("DBG", [128, NOWN * H * NT + (NT + 1) * H], F32, kind="ExternalOutput").ap()
            r_dbg = k.res("dbg", dma=True)
            k.dma("sp", DBG[:, 0:NOWN * H * NT].rearrange("p (a b) -> p a b", b=NT), CB, r_dbg, reads=[r_CB], adds=[r_dbg])
            k.dma("sp", DBG[:, NOWN * H * NT:].rearrange("p (a b) -> p a b", b=H), GRt, r_dbg, reads=[r_GR], adds=[r_dbg])
        kaug = [ar.alloc((S,), BF16) for _ in range(2)]
        r_kaug = [k.res("kaug%d" % i, dma=True) for i in range(2)]
        vh = [ar.alloc((128, 65), BF16) for _ in range(2)]
        r_vh = [k.res("vh%d" % i, dma=True) for i in range(2)]
        qaug = [ar.alloc((TN,), BF16) for _ in range(2)]
        r_qaug = [k.res("qaug%d" % i, dma=True) for i in range(2)]
        NPT = 4
        pT = [ar.alloc((1024,), BF16) for _ in range(NPT)]
        r_pT = [k.res("pT%d" % i) for i in range(NPT)]
        osb = [ar.alloc((TN,), F32) for _ in range(2)]
        r_osb = [k.res("osb%d" % i, dma=True) for i in range(2)]
        for b in range(2):
            pool.run(lambda e, b=b: e.memset(kaug[b][64:68, :], 1.0), writes=[r_kaug[b]])
            pool.run(lambda e, b=b: e.memset(qaug[b][64:68, :], 1.0), writes=[r_qaug[b]])

        def load_head(h, qn="sp"):
            hb = h % 2
            for ch in range(4):
                k.dma(qn, kaug[hb][0:66, ch * 4096:(ch + 1) * 4096], KA[h, :, ch * 4096:(ch + 1) * 4096], r_kaug[hb], reads=[r_KA],
                      writes=[r_kaug[hb]] if ch == 0 else (), adds=() if ch == 0 else [r_kaug[hb]])
            for ch in range(2):
                k.dma(qn, vh[hb][:, ch * 64:(ch + 1) * 64, :], VA[h, :, ch * 64:(ch + 1) * 64, :], r_vh[hb], reads=[r_VA],
                      writes=[r_vh[hb]] if ch == 0 else (), adds=() if ch == 0 else [r_vh[hb]])

        def load_head_part(h, part):
            hb = h % 2
            if part < 4:
                ch = part
                k.dma("pool", kaug[hb][0:66, ch * 4096:(ch + 1) * 4096], KA[h, :, ch * 4096:(ch + 1) * 4096], r_kaug[hb], reads=[r_KA],
                      writes=[r_kaug[hb]] if ch == 0 else (), adds=() if ch == 0 else [r_kaug[hb]])
            else:
                ch = part - 4
                k.dma("pool", vh[hb][:, ch * 64:(ch + 1) * 64, :], VA[h, :, ch * 64:(ch + 1) * 64, :], r_vh[hb], reads=[r_VA],
                      writes=[r_vh[hb]] if ch == 0 else (), adds=() if ch == 0 else [r_vh[hb]])

        def load_q(h, i, qb):
            k.dma("sp", qaug[qb][0:64, :], QA[h, 0:64, i * TN:(i + 1) * TN], r_qaug[qb], reads=[r_QA], writes=[r_qaug[qb]])
            k.dma("sp", qaug[qb][66:68, :], QA[h, 64:66, i * TN:(i + 1) * TN], r_qaug[qb], adds=[r_qaug[qb]])

        NST = 3
        units = [(h, i) for h in range(H) for i in range(NOWN)]
        load_head(0)
        load_q(0, 0, 0)
        act.relax = True
        stream = []
        for ui, (h, i) in enumerate(units):
            nonrag = [(T, hp, None) for T in range(rbase_of(i)) for hp in range(2)]
            rag = [(T, hp, i * 4 + (T - rbase_of(i))) for T in range(rbase_of(i), kmax_of(i) + 1) for hp in range(2)]
            order = []
            ni = ri = 0
            while ni < len(nonrag) or ri < len(rag):
                if ri < len(rag) and (ni >= len(nonrag) or ri * len(nonrag) <= ni * len(rag)):
                    order.append(rag[ri]); ri += 1
                else:
                    order.append(nonrag[ni]); ni += 1
            for pi, (T, hp, slot) in enumerate(order):
                stream.append(dict(ui=ui, h=h, i=i, T=T, hp=hp, slot=slot, first=(pi == 0), last=(pi == len(order) - 1)))

        def emit_pv(ent):
            ui, h, i = ent["ui"], ent["h"], ent["i"]
            hb, ob = h % 2, ui % 2
            ot = PS(6 + ob)
            r_ot = psres[6 + ob]
            first_, last_ = ent["first"], ent["last"]

            def mmpv(e, pb_=ent["pb"], T_=ent["T"], hp_=ent["hp"], first_=first_, last_=last_, ot=ot, hb=hb):
                last = None
                for x in range(2):
                    kb = T_ * 4 + hp_ * 2 + x
                    last = e.matmul(ot[0:65, :], lhsT=vh[hb][:, kb, :], rhs=pT[pb_][:, x * 512:(x + 1) * 512],
                                    start=(first_ and x == 0), stop=(last_ and x == 1))
                return last
            pe.run(mmpv, reads=[r_pT[ent["pb"]], r_vh[hb]], writes=[r_ot] if first_ else (), adds=() if first_ else [r_ot])
            if last_:
                dve.run(lambda e, ob=ob, ot=ot: e.tensor_copy(out=osb[ob][0:65, :], in_=ot[0:65, :]), reads=[r_ot], writes=[r_osb[ob]])
                k.dma("sp", AT[h, :, i * TN:(i + 1) * TN], osb[ob][0:65, :], r_osb[ob], reads=[r_osb[ob]], adds=[r_AT])

        for n, ent in enumerate(stream):
            ui, h, i, T, hp, slot = ent["ui"], ent["h"], ent["i"], ent["T"], ent["hp"], ent["slot"]
            hb, qb = h % 2, ui % 2
            sb = n % NST
            pb = n % NPT
            ent["pb"] = pb
            st = PS(2 * sb, 2)
            r_st = [psres[2 * sb], psres[2 * sb + 1]]

            def mmqk(e, T=T, hp=hp, slot=slot, st=st, hb=hb, qb=qb):
                last = None
                for x in range(2):
                    kb = T * 4 + hp * 2 + x
                    last = e.matmul(st[:, x * 512:(x + 1) * 512], lhsT=kaug[hb][0:68, kb * 128:(kb + 1) * 128], rhs=qaug[qb][0:68, :],
                                    start=True, stop=(slot is None))
                    if slot is not None:
                        last = e.matmul(st[:, x * 512:(x + 1) * 512], lhsT=dsel[:, slot, :], rhs=mtri[:, hp * 2 + x, :], start=False, stop=True)
                return last
            pe.run(mmqk, reads=[r_kaug[hb], r_qaug[qb], r_c2], writes=r_st)
            act.run(lambda e, st=st, pb=pb, i=i, h=h, T=T: e.activation(out=pT[pb], in_=st, func=AF.Exp, bias=CB[:, i * H + h, T:T + 1], scale=1.0),
                    reads=r_st + [r_CB], writes=[r_pT[pb]])
            if n > 0:
                emit_pv(stream[n - 1])
            if ent["first"]:
                if ui + 1 < len(units):
                    load_q(units[ui + 1][0], units[ui + 1][1], (ui + 1) % 2)
                if h + 1 < H and 1 <= i <= 6:
                    load_head_part(h + 1, i - 1)
        emit_pv(stream[-1])
        act.relax = False
        k.barrier()
        ar.reset(pmark)

    phase2()
    def phase3a():
        woa = ar.alloc((4, D), BF16)
        woc = ar.alloc((4, D), BF16)
        r_w3a = k.res("w3a", dma=True)
        k.dma("pool", woa, wo_d[0:512, :].rearrange("(c p) n -> p c n", p=128), r_w3a, adds=[r_w3a])
        k.dma("pool", woc, wo_d[512:1024, :].rearrange("(c p) n -> p c n", p=128), r_w3a, adds=[r_w3a])
        num = [ar.alloc((4, TN), F32) for _ in range(2)]
        r_num = [k.res("num%d" % i, dma=True) for i in range(2)]
        den1 = ar.alloc((4, TN), F32)
        den = [den1, den1]
        r_den1 = k.res("den", dma=True)
        r_den = [r_den1, r_den1]
        mcl = [ar.alloc((4, TN), BF16) for _ in range(2)]
        r_mcl = [k.res("mcl%d" % i, dma=True) for i in range(2)]
        xr = [ar.alloc((8, TN), F32) for _ in range(2)]
        r_xr = [k.res("xr%d" % i, dma=True) for i in range(2)]
        atsq = ar.alloc((4, TN), BF16); r_atsq = k.res("atsq")
        rsa = ar.alloc((TN,), F32); r_rsa = k.res("rsa")
        ma = [ar.alloc((4, TN), BF16) for _ in range(2)]
        r_ma = [k.res("ma%d" % i) for i in range(2)]
        x1 = [ar.alloc((8, TN), F32) for _ in range(2)]
        r_x1 = [k.res("x1%d" % i, dma=True) for i in range(2)]
        sq = ar.alloc((8, TN), BF16); r_sq = k.res("sq3")
        rstd = ar.alloc((TN,), F32); r_rstd = k.res("rstd3")
        tmpf = [ar.alloc((TN,), F32) for _ in range(2)]
        r_tmpf = [k.res("tmpf3%d" % i) for i in range(2)]
        h2 = [ar.alloc((8, TN), BF16) for _ in range(2)]
        r_h2 = [k.res("h2%d" % i, dma=True) for i in range(2)]
        def load_3a(i):
            b = i % 2
            cols = slice(i * TN, (i + 1) * TN)
            for hh in range(2):
                k.dma("sp", num[b][hh * 64:(hh + 1) * 64, :, :], AT_p[hh, 0:64, :, cols], r_num[b], reads=[r_AT],
                      writes=[r_num[b]] if hh == 0 else (), adds=() if hh == 0 else [r_num[b]])
            k.dma("sp", mcl[b], MC_v[:, :, cols], r_mcl[b], reads=[r_MC], writes=[r_mcl[b]])
            for half in range(2):
                k.dma("sp", xr[b][:, half * 4:(half + 1) * 4, :], xqT_v[:, half * 4:(half + 1) * 4, i, 0:TN], r_xr[b],
                      writes=[r_xr[b]] if half == 0 else (), adds=() if half == 0 else [r_xr[b]])
        def a_load_den(i):
            b = i % 2
            cols = slice(i * TN, (i + 1) * TN)
            for h in range(H):
                k.dma("sp", den[b][(h % 2) * 64:(h % 2 + 1) * 64, h // 2, :], AT[h, 64:65, cols].partition_broadcast(64), r_den[b], reads=[r_AT],
                      writes=[r_den[b]] if h == 0 else (), adds=() if h == 0 else [r_den[b]])

        def a_rec(i):
            b = i % 2
            dve.run(lambda e, b=b: e.reciprocal(out=den[b], in_=den[b]), writes=[r_den[b]])

        def a_mul(i):
            b = i % 2
            dve.run(lambda e, b=b: e.tensor_tensor(out=num[b], in0=num[b], in1=den[b], op=ALU.mult),
                    reads=[r_den[b]], writes=[r_num[b]])

        def a_sq(i):
            b = i % 2
            act.run(lambda e, b=b: e.activation(out=atsq, in_=num[b], func=AF.Square, bias=c_zero, scale=1.0),
                    reads=[r_num[b], r_cst], writes=[r_atsq])

        def a_mma(i):
            def mma(e):
                last = None
                for h in range(4):
                    last = e.matmul(PS(0), lhsT=onesb, rhs=atsq[:, h, :], start=(h == 0), stop=(h == 3))
                return last
            pe.run(mma, reads=[r_atsq, r_ones], writes=[psres[0]])

        def a_ln(i):
            act.run(lambda e: e.activation(out=rsa, in_=PS(0), func=AF.Ln, bias=c_eps, scale=1.0 / 512),
                    reads=[psres[0], r_cst], writes=[r_rsa])
            act.run(lambda e: e.activation(out=rsa, in_=rsa, func=AF.Exp, bias=c_zero, scale=-0.5), writes=[r_rsa])

        def a_ma(i, hs):
            b = i % 2
            for h in hs:
                dve.run(lambda e, h=h, b=b: e.scalar_tensor_tensor(out=ma[b][:, h, :], in0=num[b][:, h, :], scalar=bac[:, h:h + 1], in1=rsa,
                                                                   op0=ALU.mult, op1=ALU.mult),
                        reads=[r_num[b], r_rsa, r_consts], writes=[r_ma[b]] if h == 0 else (), adds=() if h == 0 else [r_ma[b]])

        def b_mm(i, oc):
            b = i % 2
            pso = PS(2 + (oc % 2))

            def mmo(e, oc=oc, pso=pso, b=b):
                last = None
                for h in range(4):
                    last = e.matmul(pso, lhsT=woa[:, h, oc * 128:(oc + 1) * 128], rhs=ma[b][:, h, :], start=(h == 0), stop=False)
                for c in range(4):
                    last = e.matmul(pso, lhsT=woc[:, c, oc * 128:(oc + 1) * 128], rhs=mcl[b][:, c, :], start=False, stop=(c == 3))
                return last
            pe.run(mmo, reads=[r_ma[b], r_mcl[b], r_w3a], writes=[psres[2 + (oc % 2)]])

        def b_x1(i, oc):
            b = i % 2
            pso = PS(2 + (oc % 2))
            dve.run(lambda e, oc=oc, pso=pso, b=b: e.scalar_tensor_tensor(out=x1[b][:, oc, :], in0=pso, scalar=g1c[:, oc:oc + 1], in1=xr[b][:, oc, :],
                                                                          op0=ALU.mult, op1=ALU.add),
                    reads=[psres[2 + (oc % 2)], r_xr[b], r_mod], writes=[r_x1[b]] if oc == 0 else (), adds=() if oc == 0 else [r_x1[b]])

        def c_sq(i):
            b = i % 2
            dve.run(lambda e, b=b: e.tensor_tensor(out=sq, in0=x1[b], in1=x1[b], op=ALU.mult), reads=[r_x1[b]], writes=[r_sq])

        def c_ss(i):
            def mm(e):
                last = None
                for kc in range(8):
                    last = e.matmul(PS(4), lhsT=onesb, rhs=sq[:, kc, :], start=(kc == 0), stop=(kc == 7))
                return last
            pe.run(mm, reads=[r_sq, r_ones], writes=[psres[4]])

        def c_rstd(i):
            act.run(lambda e: e.activation(out=rstd, in_=PS(4), func=AF.Ln, bias=c_eps, scale=1.0 / D), reads=[psres[4], r_cst], writes=[r_rstd])
            act.run(lambda e: e.activation(out=rstd, in_=rstd, func=AF.Exp, bias=c_zero, scale=-0.5), writes=[r_rstd])

        def c_u(i, kcs):
            b = i % 2
            for kc in kcs:
                tb = kc % 2
                dve.run(lambda e, kc=kc, tb=tb, b=b: e.tensor_tensor(out=tmpf[tb], in0=x1[b][:, kc, :], in1=rstd, op=ALU.mult),
                        reads=[r_x1[b], r_rstd], writes=[r_tmpf[tb]])
                act.run(lambda e, kc=kc, tb=tb, b=b: e.activation(out=h2[b][:, kc, :], in_=tmpf[tb], func=AF.Identity, bias=sh2[:, kc:kc + 1], scale=gm2[:, kc:kc + 1]),
                        reads=[r_tmpf[tb], r_mod], writes=[r_h2[b]] if kc == 0 else (), adds=() if kc == 0 else [r_h2[b]])

        load_3a(0)
        a_load_den(0); a_rec(0); a_mul(0); a_sq(0); a_mma(0); a_ln(0); a_ma(0, range(4))
        for i in range(NOWN):
            b = i % 2
            cols = slice(i * TN, (i + 1) * TN)
            n = i + 1
            nxt = n < NOWN
            if nxt:
                load_3a(n)
                a_load_den(n)
            b_mm(i, 0); b_mm(i, 1)
            if nxt:
                a_rec(n)
            b_x1(i, 0); b_mm(i, 2)
            if nxt:
                a_mul(n)
            b_x1(i, 1); b_mm(i, 3)
            if nxt:
                a_sq(n)
            b_x1(i, 2); b_mm(i, 4)
            if nxt:
                a_mma(n)
            b_x1(i, 3); b_mm(i, 5)
            if nxt:
                a_ln(n)
            b_x1(i, 4); b_mm(i, 6); b_x1(i, 5); b_mm(i, 7)
            if nxt:
                a_ma(n, range(0, 2))
            b_x1(i, 6); b_x1(i, 7)
            c_sq(i)
            if nxt:
                a_ma(n, range(2, 4))
            c_ss(i); c_rstd(i); c_u(i, range(8))
            k.dma("sp", X1_v[:, :, cols], x1[b], r_x1[b], reads=[r_x1[b]], adds=[r_X1])
            k.dma("sp", H2_v[:, :, cols], h2[b], r_h2[b], reads=[r_h2[b]], adds=[r_H2])
        k.barrier()
        ar.reset(pmark)

    phase3a()
    def phase3b():
        w1 = ar.alloc((8, DFF), BF16)
        w2 = ar.alloc((32, D), BF16)
        r_w3b = k.res("w3b", dma=True)
        w1_v = w1_d.rearrange("(kc p) n -> p kc n", p=128)
        w2_v = w2_d.rearrange("(fc p) n -> p fc n", p=128)
        for kc in range(8):
            k.dma("pool", w1[:, kc, :], w1_v[:, kc, :], r_w3b, adds=[r_w3b])
        for f4 in range(8):
            k.dma("pool", w2[:, f4 * 4:(f4 + 1) * 4, :], w2_v[:, f4 * 4:(f4 + 1) * 4, :], r_w3b, adds=[r_w3b])
        h2l = [ar.alloc((8, TN), BF16) for _ in range(2)]
        r_h2l = [k.res("h2l%d" % i, dma=True) for i in range(2)]
        aT = ar.alloc((32, TN), BF16)
        r_aT = [k.res("aT%d" % i) for i in range(32)]
        xo = ar.alloc((8, TN), F32); r_xo = k.res("xo", dma=True)
        rbuf = [ar.alloc((TN,), F32) for _ in range(4)]
        r_rbuf = [k.res("rbuf%d" % i) for i in range(4)]
        outT_v = outT.rearrange("(kc p) t -> p kc t", p=128)
        k.dma("sp", h2l[0], H2_v[:, :, 0:TN], r_h2l[0], reads=[r_H2], writes=[r_h2l[0]])
        for i in range(NOWN):
            b = i % 2
            cols = slice(i * TN, (i + 1) * TN)
            k.dma("sp", xo, X1_v[:, :, cols], r_xo, reads=[r_X1], writes=[r_xo])
            if i + 1 < NOWN:
                k.dma("sp", h2l[1 - b], H2_v[:, :, (i + 1) * TN:(i + 2) * TN], r_h2l[1 - b], reads=[r_H2], writes=[r_h2l[1 - b]])
            for fc in range(32):
                psf = PS(fc % 4)

                def mmf(e, fc=fc, psf=psf, b=b):
                    last = None
                    for kc in range(8):
                        last = e.matmul(psf, lhsT=w1[:, kc, fc * 128:(fc + 1) * 128], rhs=h2l[b][:, kc, :], start=(kc == 0), stop=(kc == 7))
                    return last
                pe.run(mmf, reads=[r_h2l[b], r_w3b], writes=[psres[fc % 4]])
                rb = fc % 4
                act.run(lambda e, psf=psf, rb=rb: e.activation(out=rbuf[rb], in_=psf, func=AF.Relu, bias=c_zero, scale=1.0),
                        reads=[psres[fc % 4], r_cst], writes=[r_rbuf[rb]])
                dve.run(lambda e, fc=fc, rb=rb: e.tensor_tensor(out=aT[:, fc, :], in0=rbuf[rb], in1=rbuf[rb], op=ALU.mult),
                        reads=[r_rbuf[rb]], writes=[r_aT[fc]])
            for oc in range(8):
                ps2 = PS(4 + (oc % 4))

                def mm2(e, oc=oc, ps2=ps2):
                    last = None
                    for fc in range(32):
                        last = e.matmul(ps2, lhsT=w2[:, fc, oc * 128:(oc + 1) * 128], rhs=aT[:, fc, :], start=(fc == 0), stop=(fc == 31))
                    return last
                pe.run(mm2, reads=r_aT + [r_w3b], writes=[psres[4 + (oc % 4)]])
                dve.run(lambda e, oc=oc, ps2=ps2: e.scalar_tensor_tensor(out=xo[:, oc, :], in0=ps2, scalar=g2c[:, oc:oc + 1], in1=xo[:, oc, :],
                                                                         op0=ALU.mult, op1=ALU.add),
                        reads=[psres[4 + (oc % 4)], r_mod], writes=[r_xo])
            k.dma("sp", outT_v[:, :, cols], xo, r_xo, reads=[r_xo])
    phase3b()
    k.barrier()

    block = E(nc.Block())

    @block.tensor
    def _(e):
        for f in k.pe.q:
            f(e)

    @block.scalar
    def _(e):
        for f in k.act.q:
            f(e)

    @block.vector
    def _(e):
        for f in k.dve.q:
            f(e)

    @block.gpsimd
    def _(e):
        for f in k.pool.q:
            f(e)

    @block.sync
    def _(e):
        for f in k.sp.q:
            f(e)

    es.close()
    return nc


def _col(v, p=128):
    v = np.asarray(v, np.float32).reshape(-1)
    return np.ascontiguousarray(v.reshape(-1, p).T)


def prep_inputs(x, c, w_ada, b_ada, norm1_g, w_in, q_norm_g, k_norm_g, b_f, conv_w, conv_b, conv_ln_g, conv_ln_b,
                beta_attn, beta_conv, w_out, norm2_g, w_ff1, w_ff2):
    f32 = np.float32
    x = np.asarray(x, f32)
    w_in0 = np.asarray(w_in, f32)[0]
    shared = {
        "wada": np.ascontiguousarray(np.asarray(w_ada, f32)[0]),
        "badacol": _col(np.asarray(b_ada)[0]),
        "n1gcol": _col(np.asarray(norm1_g)[0]),
        "n2gcol": _col(np.asarray(norm2_g)[0]),
        "wq": np.ascontiguousarray(w_in0[:, 0:512]),
        "wk": np.ascontiguousarray(w_in0[:, 512:1024]),
        "wv": np.ascontiguousarray(w_in0[:, 1024:1536]),
        "wfg": np.ascontiguousarray(w_in0[:, 1536:1544]),
        "wc": np.ascontiguousarray(w_in0[:, 1544:2568]),
        "qgcol": np.ascontiguousarray(np.tile(np.asarray(q_norm_g, f32)[0], 2).reshape(128, 1)),
        "kgcol": np.ascontiguousarray(np.tile(np.asarray(k_norm_g, f32)[0], 2).reshape(128, 1)),
        "bfrep": np.ascontiguousarray(np.tile(np.asarray(b_f, f32)[0].reshape(1, 8), (128, 4))),
        "cwcol": np.ascontiguousarray(np.asarray(conv_w, f32)[0].reshape(KW, 4, 128).transpose(2, 1, 0).reshape(128, 4 * KW)),
        "cbcol": _col(np.asarray(conv_b)[0]),
        "clgcol": _col(np.asarray(conv_ln_g)[0]),
        "clbcol": _col(np.asarray(conv_ln_b)[0]),
        "bccol": _col(np.asarray(beta_conv)[0]),
        "bapcol": _col(np.asarray(beta_attn)[0]),
        "wo": np.ascontiguousarray(np.asarray(w_out, f32)[0]),
        "w1": np.ascontiguousarray(np.asarray(w_ff1, f32)[0]),
        "w2": np.ascontiguousarray(np.asarray(w_ff2, f32)[0]),
        "ident": np.eye(128, dtype=f32),
        "tri": np.triu(np.ones((128, 128), f32)),
    }
    kk = np.arange(128)[:, None, None] + 128 * np.arange(4)[None, :, None]
    qq = np.arange(TN)[None, None, :]
    shared["mtri"] = np.where(kk > qq, NEG, 0.0).astype(f32).reshape(128, 4 * TN).astype(ml_dtypes.bfloat16)
    xTb = [np.ascontiguousarray(x[b].T) for b in range(2)]
    ccols = [_col(np.asarray(c, f32)[b]) for b in range(2)]
    in_maps = []
    for core in range(8):
        b, j = core // 4, core % 4
        tiles = own_tiles(j)
        m = dict(shared)
        m["xT"] = xTb[b]
        m["ccol"] = ccols[b]
        xq = np.zeros((D, NOWN, TNH), f32)
        dsel = np.zeros((128, 32, 128), f32)
        cmask = np.zeros((128, NOWN, NT), f32)
        oh = np.zeros((128, NOWN, H, NT), f32)
        hs = np.ones((128, NOWN), f32)
        for i, t in enumerate(tiles):
            xq[:, i, 0:TN] = xTb[b][:, t * TN:(t + 1) * TN]
            if t > 0:
                xq[:, i, TN:TNH] = xTb[b][:, t * TN - HALO:t * TN]
            else:
                hs[:, i] = 0.0
            cmask[:, i, t + 1:] = NEG
            oh[:, i, :, t] = 1.0
            for r in range(4):
                if rbase_of(i) + r == t:
                    dsel[:, i * 4 + r, :] = np.eye(128, dtype=f32)
        m["xqT"] = np.ascontiguousarray(xq.reshape(D, NOWN * TNH))
        m["diagsel"] = dsel.reshape(128, 32 * 128).astype(ml_dtypes.bfloat16)
        m["cmask"] = cmask.reshape(128, NOWN * NT)
        m["ohrep"] = oh.reshape(128, NOWN * H * NT)
        m["haloscale"] = hs
        in_maps.append(m)
    return in_maps


_NC_CACHE = {}


def kernel(**inputs):
    in_maps = prep_inputs(**inputs)
    if "nc" not in _NC_CACHE:
        _NC_CACHE["nc"] = build_nc()
    nc = _NC_CACHE["nc"]
    res = run_bass_kernel_spmd(nc, in_maps, core_ids=list(range(8)))
    out = np.zeros((2, S, D), np.float32)
    for core in range(8):
        b, j = core // 4, core % 4
        oT = np.asarray(res.results[core]["outT"])
        for i, t in enumerate(own_tiles(j)):
            out[b, t * TN:(t + 1) * TN, :] = oT[:, i * TN:(i + 1) * TN].T
    return out
```

```python
import numpy as np
import ml_dtypes
from contextlib import ExitStack
import concourse.bass as bass
import concourse.mybir as mybir
from concourse.bass_utils import run_bass_kernel_spmd

F32 = mybir.dt.float32
BF16 = mybir.dt.bfloat16
AF = mybir.ActivationFunctionType
ALU = mybir.AluOpType
AX = mybir.AxisListType

D = 1024
S = 16384
NT = 32
TN = 512
NOWN = 8
HALO = 30
TNH = TN + HALO
H = 8
DH = 64
DFF = 4096
EPS = 1e-6
NEG = -30000.0
KW = 31


def own_tiles(j):
    out = []
    for m in range(4):
        out += [8 * m + j, 8 * m + 7 - j]
    return out


def kmax_of(i):
    m = i // 2
    return 8 * m + 3 if i % 2 == 0 else 8 * m + 7


def rbase_of(i):
    m = i // 2
    return 8 * m if i % 2 == 0 else 8 * m + 4


class Tok:
    __slots__ = ("sem", "val")

    def __init__(self, sem, val):
        self.sem = sem
        self.val = val


class Res:
    def __init__(self, name):
        self.name = name
        self.w = []
        self.r = []
        self.dsem = None
        self.dcnt = 0

    @staticmethod
    def _add(lst, tok):
        for k, t in enumerate(lst):
            if t.sem is tok.sem:
                if tok.val > t.val:
                    lst[k] = tok
                return
        lst.append(tok)

    def add_reader(self, tok):
        Res._add(self.r, tok)

    def add_writer(self, tok):
        Res._add(self.w, tok)


class Eng:
    def __init__(self, name, sem, same_sync):
        self.name = name
        self.sem = sem
        self.same = same_sync
        self.cnt = 0
        self.q = []
        self.seen = {}
        self.relax = False

    def _wait(self, t):
        if t.sem is self.sem and (not self.same or (self.relax and self.cnt - t.val >= 3)):
            return
        key = id(t.sem)
        if self.seen.get(key, 0) >= t.val:
            return
        self.seen[key] = t.val
        self.q.append(lambda e, s=t.sem, v=t.val: e.wait_ge(s, v))

    def wait_deps(self, reads, writes, adds):
        for r in reads:
            for t in r.w:
                self._wait(t)
        for w in writes:
            for t in w.w:
                self._wait(t)
            for t in w.r:
                self._wait(t)
        for a in adds:
            for t in a.r:
                self._wait(t)

    def run(self, fn, reads=(), writes=(), adds=()):
        self.wait_deps(reads, writes, adds)
        self.cnt += 1
        sem = self.sem
        self.q.append(lambda e, fn=fn, sem=sem: fn(e).then_inc(sem, 1))
        tok = Tok(sem, self.cnt)
        for r in reads:
            r.add_reader(tok)
        for w in writes:
            w.w = [tok]
            w.r = []
        for a in adds:
            a.add_writer(tok)
        return tok


class KB:
    def __init__(self, nc, es):
        self.nc = nc
        self.es = es
        self.engs = {}
        for name, same in [("pe", False), ("act", True), ("dve", True), ("pool", True), ("sp", False)]:
            sem = es.enter_context(nc.semaphore("s_" + name))
            self.engs[name] = Eng(name, sem, same)
        self.pe = self.engs["pe"]
        self.act = self.engs["act"]
        self.dve = self.engs["dve"]
        self.pool = self.engs["pool"]
        self.sp = self.engs["sp"]
        self.dma_res = []
        self.nres = 0

    def res(self, name, dma=False):
        r = Res(name)
        if dma:
            r.dsem = self.es.enter_context(self.nc.semaphore("d%d_%s" % (self.nres, name)))
            self.dma_res.append(r)
        self.nres += 1
        return r

    def dma(self, qname, out, in_, sres, reads=(), writes=(), adds=()):
        q = self.engs[qname]
        q.wait_deps(reads, writes, adds)
        sres.dcnt += 16
        sem = sres.dsem
        tok = Tok(sem, sres.dcnt)
        q.q.append(lambda e, o=out, i=in_, sem=sem: e.dma_start(out=o, in_=i).then_inc(sem, 16))
        for r in reads:
            r.add_reader(tok)
        for w in writes:
            w.w = [tok]
            w.r = []
        for a in adds:
            a.add_writer(tok)
        return tok

    def barrier(self):
        toks = [Tok(e.sem, e.cnt) for e in self.engs.values() if e.cnt > 0]
        toks += [Tok(r.dsem, r.dcnt) for r in self.dma_res if r.dcnt > 0]
        for e in self.engs.values():
            for t in toks:
                if t.sem is e.sem:
                    continue
                e._wait(t)


class Arena:
    def __init__(self, ap, words):
        self.ap = ap
        self.words = words
        self.off = 0

    def mark(self):
        return self.off

    def reset(self, m):
        self.off = m

    def alloc(self, shape, dt):
        n = int(np.prod(shape))
        nw = n if dt == F32 else (n + 1) // 2
        assert self.off + nw <= self.words, ("arena overflow", self.off, nw, self.words)
        v = self.ap[:, self.off:self.off + nw]
        self.off += nw
        if dt != F32:
            v = v.bitcast(dt)
            if v.shape[1] != n:
                v = v[:, 0:n]
        if len(shape) == 2:
            return v.rearrange("p (a b) -> p a b", a=shape[0])
        if len(shape) == 3:
            return v.rearrange("p (a b c) -> p a b c", a=shape[0], b=shape[1])
        return v


def build_nc(debug=False):
    nc = bass.Bass("TRN2", target_bir_lowering=False)

    def DI(name, shape, dt=F32):
        return nc.dram_tensor(name, list(shape), dt, kind="ExternalInput").ap()

    def DS(name, shape, dt):
        return nc.dram_tensor(name, list(shape), dt, kind="ExternalOutput" if debug else "Internal").ap()

    xT = DI("xT", [D, S])
    xqT = DI("xqT", [D, NOWN * TNH])
    ccol_d = DI("ccol", [128, 8])
    wada_d = DI("wada", [D, 6 * D])
    bada_d = DI("badacol", [128, 48])
    n1g_d = DI("n1gcol", [128, 8])
    n2g_d = DI("n2gcol", [128, 8])
    wq_d = DI("wq", [D, 512])
    wk_d = DI("wk", [D, 512])
    wv_d = DI("wv", [D, 512])
    wfg_d = DI("wfg", [D, 8])
    wc_d = DI("wc", [D, 1024])
    qg_d = DI("qgcol", [128, 1])
    kg_d = DI("kgcol", [128, 1])
    bf_d = DI("bfrep", [128, 32])
    cw_d = DI("cwcol", [128, 4 * KW])
    cb_d = DI("cbcol", [128, 4])
    clg_d = DI("clgcol", [128, 4])
    clb_d = DI("clbcol", [128, 4])
    bc_d = DI("bccol", [128, 4])
    ba_d = DI("bapcol", [128, 4])
    wo_d = DI("wo", [D, D])
    w1_d = DI("w1", [D, DFF])
    w2_d = DI("w2", [DFF, D])
    ident_d = DI("ident", [128, 128])
    tri_d = DI("tri", [128, 128])
    mtri_d = DI("mtri", [128, 4 * TN], BF16)
    dsel_d = DI("diagsel", [128, 32 * 128], BF16)
    cmask_d = DI("cmask", [128, NOWN * NT])
    oh_d = DI("ohrep", [128, NOWN * H * NT])
    hs_d = DI("haloscale", [128, NOWN])
    outT = nc.dram_tensor("outT", [D, NOWN * TN], F32, kind="ExternalOutput").ap()

    KA = DS("KA", [H, 66, S], BF16)
    VA = DS("VA", [H, 128, 128, 65], BF16)
    QA = DS("QA", [H, 66, NOWN * TN], BF16)
    MC = DS("MC", [512, NOWN * TN], BF16)
    AT = DS("AT", [H, 65, NOWN * TN], F32)
    X1 = DS("X1", [D, NOWN * TN], F32)
    H2 = DS("H2", [D, NOWN * TN], BF16)

    es = ExitStack()
    E = es.enter_context
    AW = 52800
    arena_t = E(nc.sbuf_tensor("arena", [128, AW], F32))
    psum_t = E(nc.psum_tensor("psum", [128, 4096], F32))
    k = KB(nc, es)
    ar = Arena(arena_t[:, :], AW)

    def PS(bank, nb=1):
        return psum_t[:, bank * 512:(bank + nb) * 512]

    psres = [k.res("psb%d" % i) for i in range(8)]
    pe, act, dve, pool = k.pe, k.act, k.dve, k.pool

    cst = ar.alloc((8,), F32)
    r_cst = k.res("cst")
    pool.run(lambda e: e.memset(cst[:, 0:1], EPS), writes=[r_cst])
    pool.run(lambda e: e.memset(cst[:, 1:2], float(np.log(0.125))), adds=[r_cst])
    pool.run(lambda e: e.memset(cst[:, 2:3], 1.0), adds=[r_cst])
    pool.run(lambda e: e.memset(cst[:, 3:4], 0.0), adds=[r_cst])
    c_eps, c_ln8, c_one, c_zero = cst[:, 0:1], cst[:, 1:2], cst[:, 2:3], cst[:, 3:4]

    r_consts = k.res("consts", dma=True)

    def cload(shape, dt, src, q="sp"):
        v = ar.alloc(shape, dt)
        k.dma(q, v if len(shape) > 1 else v, src, r_consts, adds=[r_consts])
        return v

    ccol = ar.alloc((8,), F32); k.dma("sp", ccol, ccol_d, r_consts, adds=[r_consts])
    bada = ar.alloc((48,), F32); k.dma("sp", bada, bada_d, r_consts, adds=[r_consts])
    n1g = ar.alloc((8,), F32); k.dma("sp", n1g, n1g_d, r_consts, adds=[r_consts])
    n2g = ar.alloc((8,), F32); k.dma("sp", n2g, n2g_d, r_consts, adds=[r_consts])
    qg = ar.alloc((1,), F32); k.dma("sp", qg, qg_d, r_consts, adds=[r_consts])
    kg = ar.alloc((1,), F32); k.dma("sp", kg, kg_d, r_consts, adds=[r_consts])
    bfr = ar.alloc((32,), F32); k.dma("sp", bfr, bf_d, r_consts, adds=[r_consts])
    cwc = ar.alloc((4 * KW,), F32); k.dma("sp", cwc, cw_d, r_consts, adds=[r_consts])
    cbc = ar.alloc((4,), F32); k.dma("sp", cbc, cb_d, r_consts, adds=[r_consts])
    clg = ar.alloc((4,), F32); k.dma("sp", clg, clg_d, r_consts, adds=[r_consts])
    clb = ar.alloc((4,), F32); k.dma("sp", clb, clb_d, r_consts, adds=[r_consts])
    bcc = ar.alloc((4,), F32); k.dma("sp", bcc, bc_d, r_consts, adds=[r_consts])
    bac = ar.alloc((4,), F32); k.dma("sp", bac, ba_d, r_consts, adds=[r_consts])
    ident = ar.alloc((128,), F32); k.dma("sp", ident, ident_d, r_consts, adds=[r_consts])
    tri = ar.alloc((128,), F32); k.dma("sp", tri, tri_d, r_consts, adds=[r_consts])
    cmask = ar.alloc((NOWN, NT), F32); k.dma("sp", cmask, cmask_d.rearrange("p (a b) -> p a b", a=NOWN), r_consts, adds=[r_consts])
    hsc = ar.alloc((NOWN,), F32); k.dma("sp", hsc, hs_d, r_consts, adds=[r_consts])

    onesf = ar.alloc((128,), F32)
    onesb = ar.alloc((128,), BF16)
    blk1 = ar.alloc((128,), BF16)
    r_ones = k.res("ones")
    pool.run(lambda e: e.memset(onesf, 1.0), writes=[r_ones])
    pool.run(lambda e: e.memset(onesb, 1.0), adds=[r_ones])
    pool.run(lambda e: e.memset(blk1, 0.0), adds=[r_ones])
    pool.run(lambda e: e.memset(blk1[0:64, 0:64], 1.0), reads=[r_ones], adds=[r_ones])
    pool.run(lambda e: e.memset(blk1[64:128, 64:128], 1.0), adds=[r_ones])

    modc = ar.alloc((48,), F32)
    gm1 = ar.alloc((8,), F32)
    gm2 = ar.alloc((8,), F32)
    r_mod = k.res("mod")
    sh1, g1c, sh2, g2c = modc[:, 0:8], modc[:, 16:24], modc[:, 24:32], modc[:, 40:48]
    GRt = ar.alloc((NT + 1, H), F32)
    r_GR = k.res("GR")
    pool.run(lambda e: e.memset(GRt[:, 0, :], 0.0), writes=[r_GR])

    pmark = ar.mark()
    xqT_v = xqT.rearrange("(kc p) (i n) -> p kc i n", p=128, n=TNH)
    AT_p = AT.rearrange("(pr two) d t -> two d pr t", two=2)
    MC_v = MC.rearrange("(c p) t -> p c t", p=128)
    X1_v = X1.rearrange("(kc p) t -> p kc t", p=128)
    H2_v = H2.rearrange("(kc p) t -> p kc t", p=128)
    r_KA = k.res("KA"); r_VA = k.res("VA"); r_QA = k.res("QA"); r_MC = k.res("MC"); r_AT = k.res("AT"); r_X1 = k.res("X1"); r_H2 = k.res("H2")

    def phase0():
        sccol = ar.alloc((8,), F32)
        tmp8 = ar.alloc((8,), F32)
        r_sc = k.res("sc")
        act.run(lambda e: e.activation(out=tmp8, in_=ccol, func=AF.Exp, bias=c_zero, scale=-1.0), reads=[r_consts, r_cst], writes=[r_sc])
        dve.run(lambda e: e.tensor_scalar(out=tmp8, in0=tmp8, scalar1=1.0, scalar2=None, op0=ALU.add), writes=[r_sc])
        dve.run(lambda e: e.reciprocal(out=tmp8, in_=tmp8), writes=[r_sc])
        dve.run(lambda e: e.tensor_tensor(out=sccol, in0=ccol, in1=tmp8, op=ALU.mult), writes=[r_sc])
        wsec = [ar.alloc((8, 1024), BF16) for _ in range(2)]
        scb = ar.alloc((8,), BF16)
        dve.run(lambda e: e.tensor_copy(out=scb, in_=sccol), writes=[r_sc])
        r_wsec = [k.res("wsec%d" % i, dma=True) for i in range(2)]
        wada_v = wada_d.rearrange("(kc p) n -> p kc n", p=128)
        psm = PS(0)
        for sec in range(6):
            b = sec % 2
            for half in range(2):
                k.dma("pool", wsec[b][:, half * 4:(half + 1) * 4, :], wada_v[:, half * 4:(half + 1) * 4, sec * 1024:(sec + 1) * 1024], r_wsec[b],
                      writes=[r_wsec[b]] if half == 0 else (), adds=() if half == 0 else [r_wsec[b]])

            def mm_sec(e, sec=sec, b=b):
                last = None
                for oc in range(8):
                    col = sec * 8 + oc
                    for kc in range(8):
                        last = e.matmul(psm[:, col:col + 1], lhsT=wsec[b][:, kc, oc * 128:(oc + 1) * 128], rhs=scb[:, kc:kc + 1],
                                        start=(kc == 0), stop=(kc == 7))
                return last
            pe.run(mm_sec, reads=[r_wsec[b], r_sc], writes=[psres[0]] if sec == 0 else (), adds=() if sec == 0 else [psres[0]])
        dve.run(lambda e: e.tensor_tensor(out=modc, in0=psm[:, 0:48], in1=bada, op=ALU.add), reads=[psres[0], r_consts], writes=[r_mod])
        dve.run(lambda e: e.tensor_scalar(out=gm1, in0=modc[:, 8:16], scalar1=1.0, scalar2=None, op0=ALU.add), reads=[r_mod], writes=[r_sc])
        dve.run(lambda e: e.tensor_tensor(out=gm1, in0=gm1, in1=n1g, op=ALU.mult), writes=[r_sc])
        dve.run(lambda e: e.tensor_scalar(out=gm2, in0=modc[:, 32:40], scalar1=1.0, scalar2=None, op0=ALU.add), writes=[r_sc])
        dve.run(lambda e: e.tensor_tensor(out=gm2, in0=gm2, in1=n2g, op=ALU.mult), writes=[r_sc, r_mod])
        k.barrier()
        ar.reset(pmark)

    phase0()
    def norm_mod(xt, r_x, N, gm, sh, sq, r_sq, rstd, r_rstd, tmpf, r_tmpf, hT, r_hT, ps_bank):
        pss = PS(ps_bank, 2)
        r_ps = [psres[ps_bank], psres[ps_bank + 1]]
        dve.run(lambda e: e.tensor_tensor(out=sq, in0=xt, in1=xt, op=ALU.mult), reads=[r_x], writes=[r_sq])

        def mm(e):
            last = None
            for kc in range(8):
                last = e.matmul(pss[:, 0:min(N, 512)], lhsT=onesb, rhs=sq[:, kc, 0:min(N, 512)], start=(kc == 0), stop=(kc == 7))
            if N > 512:
                for kc in range(8):
                    last = e.matmul(pss[:, 512:N], lhsT=onesb, rhs=sq[:, kc, 512:N], start=(kc == 0), stop=(kc == 7))
            return last
        pe.run(mm, reads=[r_sq, r_ones], writes=r_ps)
        act.run(lambda e: e.activation(out=rstd, in_=pss[:, 0:N], func=AF.Ln, bias=c_eps, scale=1.0 / D), reads=r_ps, writes=[r_rstd])
        act.run(lambda e: e.activation(out=rstd, in_=rstd, func=AF.Exp, bias=c_zero, scale=-0.5), writes=[r_rstd])
        for kc in range(8):
            tb = kc % 2
            dve.run(lambda e, kc=kc, tb=tb: e.tensor_tensor(out=tmpf[tb], in0=xt[:, kc, :], in1=rstd, op=ALU.mult),
                    reads=[r_x, r_rstd], writes=[r_tmpf[tb]])
            act.run(lambda e, kc=kc, tb=tb: e.activation(out=hT[:, kc, :], in_=tmpf[tb], func=AF.Identity, bias=sh[:, kc:kc + 1], scale=gm[:, kc:kc + 1]),
                    reads=[r_tmpf[tb], r_mod], writes=[r_hT] if kc == 0 else (), adds=() if kc == 0 else [r_hT])

    def pair_norm(ps_raw, r_raw, gcol, lnbias, raw, r_rawsb, sqb, r_sqb, ps_ss, r_ss, rs, r_rs, out, r_out, first, steps=("sq", "rest")):
        if "sq" in steps:
            act.run(lambda e: e.activation(out=sqb, in_=ps_raw, func=AF.Square, bias=c_zero, scale=1.0), reads=[r_raw, r_cst], writes=[r_sqb])
        if "rest" in steps:
            pe.run(lambda e: e.matmul(ps_ss, lhsT=blk1, rhs=sqb, start=True, stop=True), reads=[r_sqb, r_ones], writes=[r_ss])
            act.run(lambda e: e.activation(out=rs, in_=ps_ss, func=AF.Ln, bias=c_eps, scale=1.0 / DH), reads=[r_ss, r_cst], writes=[r_rs])
            act.run(lambda e: e.activation(out=rs, in_=rs, func=AF.Exp, bias=lnbias, scale=-0.5), writes=[r_rs])
            dve.run(lambda e: e.scalar_tensor_tensor(out=out, in0=ps_raw, scalar=gcol, in1=rs, op0=ALU.mult, op1=ALU.mult),
                    reads=[r_raw, r_rs, r_consts], writes=[r_out] if first else (), adds=() if first else [r_out])

    def fgate(hT, r_hT, wfg, r_w, ps_fg, r_psfg, zt, r_zt, spt, r_spt, ps_gk, r_psgk, ps_tot, r_pstot, want_tot, part=0):
        def mm(e):
            last = None
            for b in range(4):
                for kc in range(8):
                    last = e.matmul(ps_fg[:, b * 8:(b + 1) * 8], lhsT=hT[:, kc, b * 128:(b + 1) * 128], rhs=wfg[:, kc, :],
                                    start=(kc == 0), stop=(kc == 7))
            return last
        if part in (0, 1):
            pe.run(mm, reads=[r_hT, r_w], writes=[r_psfg])
            dve.run(lambda e: e.tensor_tensor(out=zt, in0=ps_fg[:, 0:32], in1=bfr, op=ALU.add), reads=[r_psfg, r_consts], writes=[r_zt])
            act.run(lambda e: e.activation(out=zt, in_=zt, func=AF.Exp, bias=c_zero, scale=-1.0), reads=[r_cst], writes=[r_zt])
            act.run(lambda e: e.activation(out=spt, in_=zt, func=AF.Ln, bias=c_one, scale=1.0), reads=[r_zt], writes=[r_spt])
        if part == 1:
            return

        def mm2(e):
            last = None
            for b in range(4):
                last = e.matmul(ps_gk[0:8, b * 128:(b + 1) * 128], lhsT=spt[:, b * 8:(b + 1) * 8], rhs=tri, start=True, stop=(b == 0))
                for b2 in range(b):
                    last = e.matmul(ps_gk[0:8, b * 128:(b + 1) * 128], lhsT=spt[:, b2 * 8:(b2 + 1) * 8], rhs=onesf, start=False, stop=(b2 == b - 1))
            return last
        pe.run(mm2, reads=[r_spt, r_consts, r_ones], writes=[r_psgk])
        if want_tot:
            def mm3(e):
                last = None
                for b in range(4):
                    last = e.matmul(ps_tot[:, 0:8], lhsT=onesf, rhs=spt[:, b * 8:(b + 1) * 8], start=(b == 0), stop=(b == 3))
                return last
            pe.run(mm3, reads=[r_spt, r_ones], writes=[r_pstot])

    def phase1a():
        wk = ar.alloc((8, 512), BF16)
        wv = ar.alloc((8, 512), BF16)
        wfg = ar.alloc((8, 8), BF16)
        r_w1a = k.res("w1a", dma=True)
        k.dma("pool", wk, wk_d.rearrange("(kc p) n -> p kc n", p=128), r_w1a, adds=[r_w1a])
        k.dma("pool", wv, wv_d.rearrange("(kc p) n -> p kc n", p=128), r_w1a, adds=[r_w1a])
        k.dma("pool", wfg, wfg_d.rearrange("(kc p) n -> p kc n", p=128), r_w1a, adds=[r_w1a])
        xt = [ar.alloc((8, TN), F32) for _ in range(2)]
        r_xt = [k.res("xt%d" % i, dma=True) for i in range(2)]
        sq = ar.alloc((8, TN), BF16); r_sq = k.res("sq")
        rstd = ar.alloc((TN,), F32); r_rstd = k.res("rstd")
        tmpf = [ar.alloc((TN,), F32) for _ in range(2)]
        r_tmpf = [k.res("tmpf%d" % i) for i in range(2)]
        hT = [ar.alloc((8, TN), BF16) for _ in range(2)]
        r_hT = [k.res("hT%d" % i) for i in range(2)]
        raw = [ar.alloc((TN,), F32) for _ in range(2)]
        r_raw = [k.res("raw%d" % i) for i in range(2)]
        sqb = [ar.alloc((TN,), BF16) for _ in range(2)]
        r_sqb = [k.res("sqb%d" % i) for i in range(2)]
        rs = [ar.alloc((TN,), F32) for _ in range(2)]
        r_rs = [k.res("rs%d" % i) for i in range(2)]
        kn = [ar.alloc((4, TN), BF16) for _ in range(2)]
        r_kn = [k.res("kn%d" % i, dma=True) for i in range(2)]
        vaug = [ar.alloc((H, 4, 65), BF16) for _ in range(2)]
        r_vaug = [k.res("vaug%d" % i, dma=True) for i in range(2)]
        zt = ar.alloc((32,), F32); r_zt = k.res("zt")
        spt = ar.alloc((32,), F32); r_spt = k.res("spt")
        tot32 = ar.alloc((32,), F32); r_tot8 = k.res("tot8")
        ghl = [ar.alloc((2, TN), BF16) for _ in range(2)]
        r_ghl = [k.res("ghl%d" % i, dma=True) for i in range(2)]
        for b in range(2):
            pool.run(lambda e, b=b: e.memset(vaug[b], 1.0), writes=[r_vaug[b]])

        xT_v = xT.rearrange("(kc p) t -> p kc t", p=128)
        def load_xt(t):
            b = t % 2
            for half in range(2):
                k.dma("sp", xt[b][:, half * 4:(half + 1) * 4, :], xT_v[:, half * 4:(half + 1) * 4, t * TN:(t + 1) * TN], r_xt[b],
                      writes=[r_xt[b]] if half == 0 else (), adds=() if half == 0 else [r_xt[b]])
        def A_sq(t):
            b = t % 2
            dve.run(lambda e, b=b: e.tensor_tensor(out=sq, in0=xt[b], in1=xt[b], op=ALU.mult), reads=[r_xt[b]], writes=[r_sq])

        def A_ss(t):
            def mm(e):
                last = None
                for kc in range(8):
                    last = e.matmul(PS(0), lhsT=onesb, rhs=sq[:, kc, :], start=(kc == 0), stop=(kc == 7))
                return last
            pe.run(mm, reads=[r_sq, r_ones], writes=[psres[0]])

        def A_rstd(t):
            act.run(lambda e: e.activation(out=rstd, in_=PS(0), func=AF.Ln, bias=c_eps, scale=1.0 / D), reads=[psres[0], r_cst], writes=[r_rstd])
            act.run(lambda e: e.activation(out=rstd, in_=rstd, func=AF.Exp, bias=c_zero, scale=-0.5), writes=[r_rstd])

        def A_u(t, kcs):
            b = t % 2
            for kc in kcs:
                tb = kc % 2
                dve.run(lambda e, kc=kc, tb=tb, b=b: e.tensor_tensor(out=tmpf[tb], in0=xt[b][:, kc, :], in1=rstd, op=ALU.mult),
                        reads=[r_xt[b], r_rstd], writes=[r_tmpf[tb]])
                act.run(lambda e, kc=kc, tb=tb, b=b: e.activation(out=hT[b][:, kc, :], in_=tmpf[tb], func=AF.Identity, bias=sh1[:, kc:kc + 1], scale=gm1[:, kc:kc + 1]),
                        reads=[r_tmpf[tb], r_mod], writes=[r_hT[b]] if kc == 0 else (), adds=() if kc == 0 else [r_hT[b]])

        load_xt(0)
        load_xt(1)
        A_sq(0); A_ss(0); A_rstd(0); A_u(0, range(8))
        psvb = [5, 1]
        for t in range(NT):
            b = t % 2
            nxt = t + 1 < NT
            if t + 2 < NT:
                load_xt(t + 2)

            def mmk(pr, b=b):
                pb = pr % 2
                psk = PS(2 + pb)

                def f(e, pr=pr, psk=psk, b=b):
                    last = None
                    for kc in range(8):
                        last = e.matmul(psk, lhsT=wk[:, kc, pr * 128:(pr + 1) * 128], rhs=hT[b][:, kc, :], start=(kc == 0), stop=(kc == 7))
                    return last
                pe.run(f, reads=[r_hT[b], r_w1a], writes=[psres[2 + pb]])

            def ksq(pr):
                pb = pr % 2
                act.run(lambda e, pb=pb: e.activation(out=sqb[pb], in_=PS(2 + pb), func=AF.Square, bias=c_zero, scale=1.0),
                        reads=[psres[2 + pb], r_cst], writes=[r_sqb[pb]])

            def kss(pr):
                pb = pr % 2
                pe.run(lambda e, pb=pb: e.matmul(PS(4), lhsT=blk1, rhs=sqb[pb], start=True, stop=True), reads=[r_sqb[pb], r_ones], writes=[psres[4]])

            def klnexp(pr):
                pb = pr % 2
                act.run(lambda e, pb=pb: e.activation(out=rs[pb], in_=PS(4), func=AF.Ln, bias=c_eps, scale=1.0 / DH), reads=[psres[4], r_cst], writes=[r_rs[pb]])
                act.run(lambda e, pb=pb: e.activation(out=rs[pb], in_=rs[pb], func=AF.Exp, bias=c_zero, scale=-0.5), writes=[r_rs[pb]])

            def kn_(pr, b=b):
                pb = pr % 2
                dve.run(lambda e, pr=pr, pb=pb, b=b: e.scalar_tensor_tensor(out=kn[b][:, pr, :], in0=PS(2 + pb), scalar=kg[:, 0:1], in1=rs[pb], op0=ALU.mult, op1=ALU.mult),
                        reads=[psres[2 + pb], r_rs[pb], r_consts], writes=[r_kn[b]] if pr == 0 else (), adds=() if pr == 0 else [r_kn[b]])

            def mmv(blk, b=b):
                bank = psvb[blk % 2]
                psv = PS(bank)

                def f(e, blk=blk, b=b, psv=psv):
                    last = None
                    for kc in range(8):
                        last = e.matmul(psv, lhsT=hT[b][:, kc, blk * 128:(blk + 1) * 128], rhs=wv[:, kc, :], start=(kc == 0), stop=(kc == 7))
                    return last
                pe.run(f, reads=[r_hT[b], r_w1a], writes=[psres[bank]])

            def vcast(blk, b=b):
                bank = psvb[blk % 2]
                dve.run(lambda e, blk=blk, b=b, bank=bank: e.tensor_copy(out=vaug[b][:, :, blk, 0:64], in_=PS(bank).rearrange("p (h d) -> p h d", h=H)),
                        reads=[psres[bank]], writes=[r_vaug[b]] if blk == 0 else (), adds=() if blk == 0 else [r_vaug[b]])

            if nxt:
                A_sq(t + 1)
            ps_gk = PS(7)

            def fg_a(b=b):
                def mm(e, b=b):
                    last = None
                    for blk in range(4):
                        for kc in range(8):
                            last = e.matmul(PS(6)[:, blk * 8:(blk + 1) * 8], lhsT=hT[b][:, kc, blk * 128:(blk + 1) * 128], rhs=wfg[:, kc, :],
                                            start=(kc == 0), stop=(kc == 7))
                    return last
                pe.run(mm, reads=[r_hT[b], r_w1a], writes=[psres[6]])
                dve.run(lambda e: e.tensor_tensor(out=zt, in0=PS(6)[:, 0:32], in1=bfr, op=ALU.add), reads=[psres[6], r_consts], writes=[r_zt])
                act.run(lambda e: e.activation(out=zt, in_=zt, func=AF.Exp, bias=c_zero, scale=-1.0), reads=[r_cst], writes=[r_zt])
                act.run(lambda e: e.activation(out=spt, in_=zt, func=AF.Ln, bias=c_one, scale=1.0), reads=[r_zt], writes=[r_spt])

            def fg_b():
                def mm2(e):
                    last = None
                    for blk in range(4):
                        last = e.matmul(ps_gk[0:8, blk * 128:(blk + 1) * 128], lhsT=spt[:, blk * 8:(blk + 1) * 8], rhs=tri, start=True, stop=(blk == 0))
                        for b2 in range(blk):
                            last = e.matmul(ps_gk[0:8, blk * 128:(blk + 1) * 128], lhsT=spt[:, b2 * 8:(b2 + 1) * 8], rhs=onesf, start=False, stop=(b2 == blk - 1))
                    return last
                pe.run(mm2, reads=[r_spt, r_consts, r_ones], writes=[psres[7]])

            mmk(0); mmk(1); fg_a(); ksq(0); mmv(0); ksq(1); kss(0)
            if nxt:
                A_ss(t + 1)
            klnexp(0); vcast(0); mmv(1); kn_(0); kss(1); klnexp(1)
            if nxt:
                A_rstd(t + 1)
            vcast(1); mmk(2); kn_(1); mmk(3); ksq(2); mmv(2); ksq(3)
            if nxt:
                A_u(t + 1, range(0, 4))
            kss(2); klnexp(2); vcast(2); mmv(3); kn_(2); kss(3); klnexp(3); vcast(3); kn_(3)
            if nxt:
                A_u(t + 1, range(4, 8))
            for pr in range(4):
                for hh in range(2):
                    k.dma("sp", KA[2 * pr + hh, 0:64, t * TN:(t + 1) * TN], kn[b][hh * 64:(hh + 1) * 64, pr, :], r_kn[b], reads=[r_kn[b]], adds=[r_KA])
            k.dma("sp", VA[:, :, t * 4:(t + 1) * 4, :].rearrange("h p k e -> p h k e"), vaug[b], r_vaug[b], reads=[r_vaug[b]], adds=[r_VA])
            fg_b()
            pe.run(lambda e: e.matmul(PS(4)[:, 0:32], lhsT=onesf, rhs=spt[:, 0:32], start=True, stop=True), reads=[r_spt, r_ones], writes=[psres[4]])
            dve.run(lambda e: e.tensor_copy(out=tot32, in_=PS(4)[:, 0:32]), reads=[psres[4]], writes=[r_tot8])
            dve.run(lambda e: e.tensor_tensor(out=tot32[:, 0:16], in0=tot32[:, 0:16], in1=tot32[:, 16:32], op=ALU.add), writes=[r_tot8])
            dve.run(lambda e: e.tensor_tensor(out=tot32[:, 0:8], in0=tot32[:, 0:8], in1=tot32[:, 8:16], op=ALU.add), writes=[r_tot8])
            dve.run(lambda e, t=t: e.tensor_tensor(out=GRt[:, t + 1, :], in0=tot32[:, 0:8], in1=GRt[:, t, :], op=ALU.add),
                    reads=[r_tot8], writes=[r_GR])
            dve.run(lambda e, b=b: e.tensor_copy(out=ghl[b][0:8, 0, :], in_=ps_gk[0:8, :]), reads=[psres[7]], writes=[r_ghl[b]])
            dve.run(lambda e, b=b: e.tensor_tensor(out=ghl[b][0:8, 1, :], in0=ps_gk[0:8, :], in1=ghl[b][0:8, 0, :], op=ALU.subtract),
                    reads=[psres[7]], writes=[r_ghl[b]])
            k.dma("sp", KA[:, 64:66, t * TN:(t + 1) * TN], ghl[b][0:8, :, :], r_ghl[b], reads=[r_ghl[b]], adds=[r_KA])
        k.barrier()
        ar.reset(pmark)

    phase1a()
    def phase1b():
        wq = ar.alloc((8, 512), BF16)
        wc = ar.alloc((8, 1024), BF16)
        wfg = ar.alloc((8, 8), BF16)
        r_w1b = k.res("w1b", dma=True)
        k.dma("pool", wq, wq_d.rearrange("(kc p) n -> p kc n", p=128), r_w1b, adds=[r_w1b])
        k.dma("pool", wc, wc_d.rearrange("(kc p) n -> p kc n", p=128), r_w1b, adds=[r_w1b])
        k.dma("pool", wfg, wfg_d.rearrange("(kc p) n -> p kc n", p=128), r_w1b, adds=[r_w1b])
        Dg = ar.alloc((4, KW, 128), BF16)
        r_Dg = k.res("Dg")
        first = True
        for c in range(4):
            for kk in range(KW):
                eng = dve
                eng.run(lambda e, c=c, kk=kk: e.tensor_scalar(out=Dg[:, c, kk, :], in0=ident, scalar1=cwc[:, c * KW + kk:c * KW + kk + 1], scalar2=None, op0=ALU.mult),
                        reads=[r_consts], writes=[r_Dg] if first else (), adds=() if first else [r_Dg])
                first = False
        xq = [ar.alloc((8, TNH), F32) for _ in range(2)]
        r_xq = [k.res("xq%d" % i, dma=True) for i in range(2)]
        sq = ar.alloc((8, TNH), BF16); r_sq = k.res("sqb")
        rstd = ar.alloc((TNH,), F32); r_rstd = k.res("rstdb")
        tmpf = [ar.alloc((TNH,), F32) for _ in range(2)]
        r_tmpf = [k.res("tmpfb%d" % i) for i in range(2)]
        hq = [ar.alloc((8, TNH), BF16) for _ in range(2)]
        r_hq = [k.res("hq%d" % i) for i in range(2)]
        raw = [ar.alloc((TN,), F32) for _ in range(2)]
        r_raw = [k.res("rawb%d" % i) for i in range(2)]
        sqb = [ar.alloc((TN,), BF16) for _ in range(2)]
        r_sqb = [k.res("sqbb%d" % i) for i in range(2)]
        rs = [ar.alloc((TN,), F32) for _ in range(2)]
        r_rs = [k.res("rsb%d" % i) for i in range(2)]
        qn = [ar.alloc((4, TN), BF16) for _ in range(2)]
        r_qn = [k.res("qn%d" % i, dma=True) for i in range(2)]
        zt = ar.alloc((32,), F32); r_zt = k.res("ztb")
        spt = ar.alloc((32,), F32); r_spt = k.res("sptb")
        dhl = [ar.alloc((2, TN), BF16) for _ in range(2)]
        r_dhl = [k.res("dhl%d" % i, dma=True) for i in range(2)]
        eg = ar.alloc((TNH,), F32); r_eg = k.res("eg")
        ubuf = ar.alloc((4, TNH), BF16); r_ub = k.res("ubuf")
        ybuf = ar.alloc((4, TN), F32); r_yb = k.res("ybuf")
        ybf = ar.alloc((4, TN), BF16); r_ybf = k.res("ybf")
        ysq = ar.alloc((4, TN), BF16); r_ysq = k.res("ysq")
        mean = ar.alloc((TN,), F32); r_mean = k.res("mean")
        var = ar.alloc((TN,), F32); r_var = k.res("var")
        tcv = ar.alloc((TN,), F32); r_tcv = k.res("tcv")
        u2 = ar.alloc((4, TN), F32); r_u2 = k.res("u2")
        u2s = ar.alloc((4, TN), BF16); r_u2s = k.res("u2s")
        rs3 = ar.alloc((TN,), F32); r_rs3 = k.res("rs3")
        mcb = [ar.alloc((4, TN), BF16) for _ in range(2)]
        r_mcb = [k.res("mcb%d" % i, dma=True) for i in range(2)]

        def load_xq(i):
            b = i % 2
            for half in range(2):
                k.dma("sp", xq[b][:, half * 4:(half + 1) * 4, :], xqT_v[:, half * 4:(half + 1) * 4, i, :], r_xq[b],
                      writes=[r_xq[b]] if half == 0 else (), adds=() if half == 0 else [r_xq[b]])
        def A_sq(i):
            b = i % 2
            dve.run(lambda e, b=b: e.tensor_tensor(out=sq, in0=xq[b], in1=xq[b], op=ALU.mult), reads=[r_xq[b]], writes=[r_sq])

        def A_ss(i):
            def mm(e):
                last = None
                for kc in range(8):
                    last = e.matmul(PS(0), lhsT=onesb, rhs=sq[:, kc, 0:TN], start=(kc == 0), stop=(kc == 7))
                for kc in range(8):
                    last = e.matmul(PS(1)[:, 0:HALO], lhsT=onesb, rhs=sq[:, kc, TN:TNH], start=(kc == 0), stop=(kc == 7))
                return last
            pe.run(mm, reads=[r_sq, r_ones], writes=[psres[0], psres[1]])

        def A_rstd(i):
            act.run(lambda e: e.activation(out=rstd, in_=PS(0, 2)[:, 0:TNH], func=AF.Ln, bias=c_eps, scale=1.0 / D), reads=[psres[0], psres[1], r_cst], writes=[r_rstd])
            act.run(lambda e: e.activation(out=rstd, in_=rstd, func=AF.Exp, bias=c_zero, scale=-0.5), writes=[r_rstd])

        def A_u(i, kcs):
            b = i % 2
            for kc in kcs:
                tb = kc % 2
                dve.run(lambda e, kc=kc, tb=tb, b=b: e.tensor_tensor(out=tmpf[tb], in0=xq[b][:, kc, :], in1=rstd, op=ALU.mult),
                        reads=[r_xq[b], r_rstd], writes=[r_tmpf[tb]])
                act.run(lambda e, kc=kc, tb=tb, b=b: e.activation(out=hq[b][:, kc, :], in_=tmpf[tb], func=AF.Identity, bias=sh1[:, kc:kc + 1], scale=gm1[:, kc:kc + 1]),
                        reads=[r_tmpf[tb], r_mod], writes=[r_hq[b]] if kc == 0 else (), adds=() if kc == 0 else [r_hq[b]])
        load_xq(0)
        load_xq(1)
        A_sq(0); A_ss(0); A_rstd(0); A_u(0, range(8))
        for i in range(NOWN):
            b = i % 2
            nxt = i + 1 < NOWN
            if i + 2 < NOWN:
                load_xq(i + 2)
            if nxt:
                A_sq(i + 1)
            fgate(hq[b], r_hq[b], wfg, r_w1b, PS(6), psres[6], zt, r_zt, spt, r_spt, PS(7), psres[7], None, None, False, part=1)
            def q_mm(pr, b=b):
                pb = pr % 2
                psq = PS(2 + pb)

                def mmq(e, pr=pr, psq=psq, hqb=hq[b]):
                    last = None
                    for kc in range(8):
                        last = e.matmul(psq, lhsT=wq[:, kc, pr * 128:(pr + 1) * 128], rhs=hqb[:, kc, 0:TN], start=(kc == 0), stop=(kc == 7))
                    return last
                pe.run(mmq, reads=[r_hq[b], r_w1b], writes=[psres[2 + pb]])

            def q_norm(pr, steps, b=b):
                pb = pr % 2
                pair_norm(PS(2 + pb), psres[2 + pb], qg[:, 0:1], c_ln8, raw[pb], r_raw[pb], sqb[pb], r_sqb[pb], PS(4), psres[4], rs[pb], r_rs[pb],
                          qn[b][:, pr, :], r_qn[b], pr == 0, steps=steps)
            for p0 in (0, 2):
                q_mm(p0); q_mm(p0 + 1)
                q_norm(p0, ("sq",)); q_norm(p0 + 1, ("sq",))
                q_norm(p0, ("rest",)); q_norm(p0 + 1, ("rest",))
            for pr in range(4):
                for hh in range(2):
                    k.dma("sp", QA[2 * pr + hh, 0:64, i * TN:(i + 1) * TN], qn[b][hh * 64:(hh + 1) * 64, pr, :], r_qn[b], reads=[r_qn[b]], adds=[r_QA])
            if nxt:
                A_ss(i + 1)
            ps_gk = PS(7)
            fgate(hq[b], r_hq[b], wfg, r_w1b, PS(6), psres[6], zt, r_zt, spt, r_spt, ps_gk, psres[7], None, None, False, part=2)
            dve.run(lambda e, b=b: e.tensor_scalar(out=dhl[b][0:8, 0, :], in0=ps_gk[0:8, :], scalar1=-1.0, scalar2=None, op0=ALU.mult),
                    reads=[psres[7]], writes=[r_dhl[b]])
            dve.run(lambda e, b=b: e.scalar_tensor_tensor(out=dhl[b][0:8, 1, :], in0=ps_gk[0:8, :], scalar=-1.0, in1=dhl[b][0:8, 0, :],
                                                          op0=ALU.mult, op1=ALU.subtract),
                    reads=[psres[7]], writes=[r_dhl[b]])
            k.dma("sp", QA[:, 64:66, i * TN:(i + 1) * TN], dhl[b][0:8, :, :], r_dhl[b], reads=[r_dhl[b]], adds=[r_QA])
            if nxt:
                A_rstd(i + 1)
            for c in range(4):
                psl = PS(2, 2)
                psg = PS(4, 2)

                def mmc(e, c=c, psl=psl, psg=psg, hqb=hq[b]):
                    last = None
                    for (ps_, col0) in ((psl, c * 128), (psg, 512 + c * 128)):
                        for kc in range(8):
                            last = e.matmul(ps_[:, 0:TN], lhsT=wc[:, kc, col0:col0 + 128], rhs=hqb[:, kc, 0:TN], start=(kc == 0), stop=(kc == 7))
                        for kc in range(8):
                            last = e.matmul(ps_[:, TN:TNH], lhsT=wc[:, kc, col0:col0 + 128], rhs=hqb[:, kc, TN:TNH], start=(kc == 0), stop=(kc == 7))
                    return last
                pe.run(mmc, reads=[r_hq[b], r_w1b], writes=[psres[2], psres[3], psres[4], psres[5]])
                act.run(lambda e, psg=psg: e.activation(out=eg, in_=psg[:, 0:TNH], func=AF.Sigmoid, bias=c_zero, scale=1.0),
                        reads=[psres[4], psres[5], r_cst], writes=[r_eg])
                dve.run(lambda e, c=c, psl=psl: e.tensor_tensor(out=ubuf[:, c, HALO:TNH], in0=psl[:, 0:TN], in1=eg[:, 0:TN], op=ALU.mult),
                        reads=[psres[2], psres[3], r_eg], writes=[r_ub] if c == 0 else (), adds=() if c == 0 else [r_ub])
                dve.run(lambda e, c=c, psl=psl, i=i: e.scalar_tensor_tensor(out=ubuf[:, c, 0:HALO], in0=psl[:, TN:TNH], scalar=hsc[:, i:i + 1], in1=eg[:, TN:TNH],
                                                                            op0=ALU.mult, op1=ALU.mult),
                        reads=[psres[2], psres[3], r_eg, r_consts], adds=[r_ub])
            if nxt:
                A_u(i + 1, range(8))
            for c in range(4):
                psy = PS(2 + (c % 2))

                def mmy(e, c=c, psy=psy):
                    last = None
                    for kk in range(KW):
                        last = e.matmul(psy, lhsT=Dg[:, c, kk, :], rhs=ubuf[:, c, kk:kk + TN], start=(kk == 0), stop=(kk == KW - 1))
                    return last
                pe.run(mmy, reads=[r_ub, r_Dg], writes=[psres[2 + (c % 2)]])
                act.run(lambda e, c=c, psy=psy: e.activation(out=ybuf[:, c, :], in_=psy, func=AF.Identity, bias=cbc[:, c:c + 1], scale=1.0),
                        reads=[psres[2 + (c % 2)], r_consts], writes=[r_yb] if c == 0 else (), adds=() if c == 0 else [r_yb])
                dve.run(lambda e, c=c: e.tensor_copy(out=ybf[:, c, :], in_=ybuf[:, c, :]), reads=[r_yb], writes=[r_ybf] if c == 0 else (), adds=() if c == 0 else [r_ybf])
                act.run(lambda e, c=c: e.activation(out=ysq[:, c, :], in_=ybuf[:, c, :], func=AF.Square, bias=c_zero, scale=1.0),
                        reads=[r_yb, r_cst], writes=[r_ysq] if c == 0 else (), adds=() if c == 0 else [r_ysq])

            def mmln(e):
                last = None
                for c in range(4):
                    last = e.matmul(PS(4), lhsT=onesb, rhs=ybf[:, c, :], start=(c == 0), stop=(c == 3))
                for c in range(4):
                    last = e.matmul(PS(5), lhsT=onesb, rhs=ysq[:, c, :], start=(c == 0), stop=(c == 3))
                return last
            pe.run(mmln, reads=[r_ybf, r_ysq, r_ones], writes=[psres[4], psres[5]])
            dve.run(lambda e: e.tensor_scalar(out=mean, in0=PS(4), scalar1=1.0 / 512, scalar2=None, op0=ALU.mult), reads=[psres[4]], writes=[r_mean])
            dve.run(lambda e: e.tensor_tensor(out=tcv, in0=mean, in1=mean, op=ALU.mult), reads=[r_mean], writes=[r_tcv])
            dve.run(lambda e: e.scalar_tensor_tensor(out=var, in0=PS(5), scalar=1.0 / 512, in1=tcv, op0=ALU.mult, op1=ALU.subtract),
                    reads=[psres[5], r_tcv], writes=[r_var])
            act.run(lambda e: e.activation(out=var, in_=var, func=AF.Ln, bias=c_eps, scale=1.0), reads=[r_cst], writes=[r_var])
            act.run(lambda e: e.activation(out=var, in_=var, func=AF.Exp, bias=c_zero, scale=-0.5), writes=[r_var])
            for c in range(4):
                tb_, r_tb_ = (tcv, r_tcv) if c % 2 == 0 else (eg[:, 0:TN], r_eg)
                dve.run(lambda e, c=c, tb_=tb_: e.tensor_tensor(out=tb_, in0=ybuf[:, c, :], in1=mean, op=ALU.subtract), reads=[r_yb, r_mean], writes=[r_tb_])
                dve.run(lambda e, tb_=tb_: e.tensor_tensor(out=tb_, in0=tb_, in1=var, op=ALU.mult), reads=[r_var], writes=[r_tb_])
                act.run(lambda e, c=c, tb_=tb_: e.activation(out=u2[:, c, :], in_=tb_, func=AF.Silu, bias=clb[:, c:c + 1], scale=clg[:, c:c + 1]),
                        reads=[r_tb_, r_consts], writes=[r_u2] if c == 0 else (), adds=() if c == 0 else [r_u2])
                act.run(lambda e, c=c: e.activation(out=u2s[:, c, :], in_=u2[:, c, :], func=AF.Square, bias=c_zero, scale=1.0),
                        reads=[r_u2, r_cst], writes=[r_u2s] if c == 0 else (), adds=() if c == 0 else [r_u2s])

            def mmr(e):
                last = None
                for c in range(4):
                    last = e.matmul(PS(6), lhsT=onesb, rhs=u2s[:, c, :], start=(c == 0), stop=(c == 3))
                return last
            pe.run(mmr, reads=[r_u2s, r_ones], writes=[psres[6]])
            act.run(lambda e: e.activation(out=rs3, in_=PS(6), func=AF.Ln, bias=c_eps, scale=1.0 / 512), reads=[psres[6], r_cst], writes=[r_rs3])
            act.run(lambda e: e.activation(out=rs3, in_=rs3, func=AF.Exp, bias=c_zero, scale=-0.5), writes=[r_rs3])
            for c in range(4):
                dve.run(lambda e, c=c, b=b: e.scalar_tensor_tensor(out=mcb[b][:, c, :], in0=u2[:, c, :], scalar=bcc[:, c:c + 1], in1=rs3, op0=ALU.mult, op1=ALU.mult),
                        reads=[r_u2, r_rs3, r_consts], writes=[r_mcb[b]] if c == 0 else (), adds=() if c == 0 else [r_mcb[b]])
            k.dma("sp", MC.rearrange("(c p) t -> p c t", p=128)[:, :, i * TN:(i + 1) * TN], mcb[b], r_mcb[b], reads=[r_mcb[b]], adds=[r_MC])
        k.barrier()
        ar.reset(pmark)

    phase1b()
    def phase2():
        mtri = ar.alloc((4, TN), BF16)
        dsel = ar.alloc((32, 128), BF16)
        ohrep = ar.alloc((NOWN * H, NT), F32)
        r_c2 = k.res("c2", dma=True)
        k.dma("sp", mtri, mtri_d.rearrange("p (a b) -> p a b", a=4), r_c2, adds=[r_c2])
        k.dma("sp", dsel, dsel_d.rearrange("p (a b) -> p a b", a=32), r_c2, adds=[r_c2])
        k.dma("sp", ohrep, oh_d.rearrange("p (a b) -> p a b", b=NT), r_c2, adds=[r_c2])
        CB = ar.alloc((NOWN * H, NT), F32); r_CB = k.res("CB")
        seltmp = ar.alloc((H, NT), F32); r_seltmp = k.res("seltmp")
        sel = ar.alloc((H,), F32); r_sel = k.res("sel")
        GRv = GRt[:, 0:NT, :].rearrange("p t h -> p h t")
        for i in range(NOWN):
            dve.run(lambda e, i=i: e.tensor_tensor(out=seltmp, in0=GRv, in1=ohrep[:, i * H:(i + 1) * H, :], op=ALU.mult),
                    reads=[r_GR, r_c2], writes=[r_seltmp])
            dve.run(lambda e: e.tensor_reduce(out=sel, in_=seltmp, axis=AX.X, op=ALU.add), reads=[r_seltmp], writes=[r_sel])
            dve.run(lambda e: e.tensor_scalar(out=sel, in0=sel, scalar1=-1.0, scalar2=None, op0=ALU.mult), writes=[r_sel])
            for h in range(H):
                dve.run(lambda e, i=i, h=h: e.scalar_tensor_tensor(out=CB[:, i * H + h, :], in0=GRt[:, 0:NT, h], scalar=sel[:, h:h + 1], in1=cmask[:, i, :],
                                                                   op0=ALU.add, op1=ALU.add),
                        reads=[r_sel, r_GR, r_consts], writes=[r_CB] if (i == 0 and h == 0) else (), adds=() if (i == 0 and h == 0) else [r_CB])
        if debug:
            DBG = nc.dram_tensor("DBG", [128, NOWN * H * NT + (NT + 1) * H], F32, kind="ExternalOutput").ap()
            r_dbg = k.res("dbg", dma=True)
            k.dma("sp", DBG[:, 0:NOWN * H * NT].rearrange("p (a b) -> p a b", b=NT), CB, r_dbg, reads=[r_CB], adds=[r_dbg])
            k.dma("sp", DBG[:, NOWN * H * NT:].rearrange("p (a b) -> p a b", b=H), GRt, r_dbg, reads=[r_GR], adds=[r_dbg])
        kaug = [ar.alloc((S,), BF16) for _ in range(2)]
        r_kaug = [k.res("kaug%d" % i, dma=True) for i in range(2)]
        vh = [ar.alloc((128, 65), BF16) for _ in range(2)]
        r_vh = [k.res("vh%d" % i, dma=True) for i in range(2)]
        qaug = [ar.alloc((TN,), BF16) for _ in range(2)]
        r_qaug = [k.res("qaug%d" % i, dma=True) for i in range(2)]
        NPT = 4
        pT = [ar.alloc((1024,), BF16) for _ in range(NPT)]
        r_pT = [k.res("pT%d" % i) for i in range(NPT)]
        osb = [ar.alloc((TN,), F32) for _ in range(2)]
        r_osb = [k.res("osb%d" % i, dma=True) for i in range(2)]
        for b in range(2):
            pool.run(lambda e, b=b: e.memset(kaug[b][64:68, :], 1.0), writes=[r_kaug[b]])
            pool.run(lambda e, b=b: e.memset(qaug[b][64:68, :], 1.0), writes=[r_qaug[b]])

        def load_head(h, qn="sp"):
            hb = h % 2
            for ch in range(4):
                k.dma(qn, kaug[hb][0:66, ch * 4096:(ch + 1) * 4096], KA[h, :, ch * 4096:(ch + 1) * 4096], r_kaug[hb], reads=[r_KA],
                      writes=[r_kaug[hb]] if ch == 0 else (), adds=() if ch == 0 else [r_kaug[hb]])
            for ch in range(2):
                k.dma(qn, vh[hb][:, ch * 64:(ch + 1) * 64, :], VA[h, :, ch * 64:(ch + 1) * 64, :], r_vh[hb], reads=[r_VA],
                      writes=[r_vh[hb]] if ch == 0 else (), adds=() if ch == 0 else [r_vh[hb]])

        def load_head_part(h, part):
            hb = h % 2
            if part < 4:
                ch = part
                k.dma("pool", kaug[hb][0:66, ch * 4096:(ch + 1) * 4096], KA[h, :, ch * 4096:(ch + 1) * 4096], r_kaug[hb], reads=[r_KA],
                      writes=[r_kaug[hb]] if ch == 0 else (), adds=() if ch == 0 else [r_kaug[hb]])
            else:
                ch = part - 4
                k.dma("pool", vh[hb][:, ch * 64:(ch + 1) * 64, :], VA[h, :, ch * 64:(ch + 1) * 64, :], r_vh[hb], reads=[r_VA],
                      writes=[r_vh[hb]] if ch == 0 else (), adds=() if ch == 0 else [r_vh[hb]])

        def load_q(h, i, qb):
            k.dma("sp", qaug[qb][0:64, :], QA[h, 0:64, i * TN:(i + 1) * TN], r_qaug[qb], reads=[r_QA], writes=[r_qaug[qb]])
            k.dma("sp", qaug[qb][66:68, :], QA[h, 64:66, i * TN:(i + 1) * TN], r_qaug[qb], adds=[r_qaug[qb]])

        NST = 3
        units = [(h, i) for h in range(H) for i in range(NOWN)]
        load_head(0)
        load_q(0, 0, 0)
        act.relax = True
        stream = []
        for ui, (h, i) in enumerate(units):
            nonrag = [(T, hp, None) for T in range(rbase_of(i)) for hp in range(2)]
            rag = [(T, hp, i * 4 + (T - rbase_of(i))) for T in range(rbase_of(i), kmax_of(i) + 1) for hp in range(2)]
            order = []
            ni = ri = 0
            while ni < len(nonrag) or ri < len(rag):
                if ri < len(rag) and (ni >= len(nonrag) or ri * len(nonrag) <= ni * len(rag)):
                    order.append(rag[ri]); ri += 1
                else:
                    order.append(nonrag[ni]); ni += 1
            for pi, (T, hp, slot) in enumerate(order):
                stream.append(dict(ui=ui, h=h, i=i, T=T, hp=hp, slot=slot, first=(pi == 0), last=(pi == len(order) - 1)))

        def emit_pv(ent):
            ui, h, i = ent["ui"], ent["h"], ent["i"]
            hb, ob = h % 2, ui % 2
            ot = PS(6 + ob)
            r_ot = psres[6 + ob]
            first_, last_ = ent["first"], ent["last"]

            def mmpv(e, pb_=ent["pb"], T_=ent["T"], hp_=ent["hp"], first_=first_, last_=last_, ot=ot, hb=hb):
                last = None
                for x in range(2):
                    kb = T_ * 4 + hp_ * 2 + x
                    last = e.matmul(ot[0:65, :], lhsT=vh[hb][:, kb, :], rhs=pT[pb_][:, x * 512:(x + 1) * 512],
                                    start=(first_ and x == 0), stop=(last_ and x == 1))
                return last
            pe.run(mmpv, reads=[r_pT[ent["pb"]], r_vh[hb]], writes=[r_ot] if first_ else (), adds=() if first_ else [r_ot])
            if last_:
                dve.run(lambda e, ob=ob, ot=ot: e.tensor_copy(out=osb[ob][0:65, :], in_=ot[0:65, :]), reads=[r_ot], writes=[r_osb[ob]])
                k.dma("sp", AT[h, :, i * TN:(i + 1) * TN], osb[ob][0:65, :], r_osb[ob], reads=[r_osb[ob]], adds=[r_AT])

        for n, ent in enumerate(stream):
            ui, h, i, T, hp, slot = ent["ui"], ent["h"], ent["i"], ent["T"], ent["hp"], ent["slot"]
            hb, qb = h % 2, ui % 2
            sb = n % NST
            pb = n % NPT
            ent["pb"] = pb
            st = PS(2 * sb, 2)
            r_st = [psres[2 * sb], psres[2 * sb + 1]]

            def mmqk(e, T=T, hp=hp, slot=slot, st=st, hb=hb, qb=qb):
                last = None
                for x in range(2):
                    kb = T * 4 + hp * 2 + x
                    last = e.matmul(st[:, x * 512:(x + 1) * 512], lhsT=kaug[hb][0:68, kb * 128:(kb + 1) * 128], rhs=qaug[qb][0:68, :],
                                    start=True, stop=(slot is None))
                    if slot is not None:
                        last = e.matmul(st[:, x * 512:(x + 1) * 512], lhsT=dsel[:, slot, :], rhs=mtri[:, hp * 2 + x, :], start=False, stop=True)
                return last
            pe.run(mmqk, reads=[r_kaug[hb], r_qaug[qb], r_c2], writes=r_st)
            act.run(lambda e, st=st, pb=pb, i=i, h=h, T=T: e.activation(out=pT[pb], in_=st, func=AF.Exp, bias=CB[:, i * H + h, T:T + 1], scale=1.0),
                    reads=r_st + [r_CB], writes=[r_pT[pb]])
            if n > 1:
                emit_pv(stream[n - 2])
            if ent["first"]:
                if ui + 1 < len(units):
                    load_q(units[ui + 1][0], units[ui + 1][1], (ui + 1) % 2)
                if h + 1 < H and 1 <= i <= 6:
                    load_head_part(h + 1, i - 1)
        emit_pv(stream[-2])
        emit_pv(stream[-1])
        act.relax = False
        k.barrier()
        ar.reset(pmark)

    phase2()
    def phase3a():
        woa = ar.alloc((4, D), BF16)
        woc = ar.alloc((4, D), BF16)
        r_w3a = k.res("w3a", dma=True)
        k.dma("pool", woa, wo_d[0:512, :].rearrange("(c p) n -> p c n", p=128), r_w3a, adds=[r_w3a])
        k.dma("pool", woc, wo_d[512:1024, :].rearrange("(c p) n -> p c n", p=128), r_w3a, adds=[r_w3a])
        num = [ar.alloc((4, TN), F32) for _ in range(2)]
        r_num = [k.res("num%d" % i, dma=True) for i in range(2)]
        den1 = ar.alloc((4, TN), F32)
        den = [den1, den1]
        r_den1 = k.res("den", dma=True)
        r_den = [r_den1, r_den1]
        mcl = [ar.alloc((4, TN), BF16) for _ in range(2)]
        r_mcl = [k.res("mcl%d" % i, dma=True) for i in range(2)]
        xr = [ar.alloc((8, TN), F32) for _ in range(2)]
        r_xr = [k.res("xr%d" % i, dma=True) for i in range(2)]
        atsq = ar.alloc((4, TN), BF16); r_atsq = k.res("atsq")
        rsa = ar.alloc((TN,), F32); r_rsa = k.res("rsa")
        ma = [ar.alloc((4, TN), BF16) for _ in range(2)]
        r_ma = [k.res("ma%d" % i) for i in range(2)]
        x1 = [ar.alloc((8, TN), F32) for _ in range(2)]
        r_x1 = [k.res("x1%d" % i, dma=True) for i in range(2)]
        sq = ar.alloc((8, TN), BF16); r_sq = k.res("sq3")
        rstd = ar.alloc((TN,), F32); r_rstd = k.res("rstd3")
        tmpf = [ar.alloc((TN,), F32) for _ in range(2)]
        r_tmpf = [k.res("tmpf3%d" % i) for i in range(2)]
        h2 = [ar.alloc((8, TN), BF16) for _ in range(2)]
        r_h2 = [k.res("h2%d" % i, dma=True) for i in range(2)]
        def load_3a(i):
            b = i % 2
            cols = slice(i * TN, (i + 1) * TN)
            for hh in range(2):
                k.dma("sp", num[b][hh * 64:(hh + 1) * 64, :, :], AT_p[hh, 0:64, :, cols], r_num[b], reads=[r_AT],
                      writes=[r_num[b]] if hh == 0 else (), adds=() if hh == 0 else [r_num[b]])
            k.dma("sp", mcl[b], MC_v[:, :, cols], r_mcl[b], reads=[r_MC], writes=[r_mcl[b]])
            for half in range(2):
                k.dma("sp", xr[b][:, half * 4:(half + 1) * 4, :], xqT_v[:, half * 4:(half + 1) * 4, i, 0:TN], r_xr[b],
                      writes=[r_xr[b]] if half == 0 else (), adds=() if half == 0 else [r_xr[b]])
        def a_load_den(i):
            b = i % 2
            cols = slice(i * TN, (i + 1) * TN)
            for h in range(H):
                k.dma("sp", den[b][(h % 2) * 64:(h % 2 + 1) * 64, h // 2, :], AT[h, 64:65, cols].partition_broadcast(64), r_den[b], reads=[r_AT],
                      writes=[r_den[b]] if h == 0 else (), adds=() if h == 0 else [r_den[b]])

        def a_rec(i):
            b = i % 2
            dve.run(lambda e, b=b: e.reciprocal(out=den[b], in_=den[b]), writes=[r_den[b]])

        def a_mul(i):
            b = i % 2
            dve.run(lambda e, b=b: e.tensor_tensor(out=num[b], in0=num[b], in1=den[b], op=ALU.mult),
                    reads=[r_den[b]], writes=[r_num[b]])

        def a_sq(i):
            b = i % 2
            act.run(lambda e, b=b: e.activation(out=atsq, in_=num[b], func=AF.Square, bias=c_zero, scale=1.0),
                    reads=[r_num[b], r_cst], writes=[r_atsq])

        def a_mma(i):
            def mma(e):
                last = None
                for h in range(4):
                    last = e.matmul(PS(0), lhsT=onesb, rhs=atsq[:, h, :], start=(h == 0), stop=(h == 3))
                return last
            pe.run(mma, reads=[r_atsq, r_ones], writes=[psres[0]])

        def a_ln(i):
            act.run(lambda e: e.activation(out=rsa, in_=PS(0), func=AF.Ln, bias=c_eps, scale=1.0 / 512),
                    reads=[psres[0], r_cst], writes=[r_rsa])
            act.run(lambda e: e.activation(out=rsa, in_=rsa, func=AF.Exp, bias=c_zero, scale=-0.5), writes=[r_rsa])

        def a_ma(i, hs):
            b = i % 2
            for h in hs:
                dve.run(lambda e, h=h, b=b: e.scalar_tensor_tensor(out=ma[b][:, h, :], in0=num[b][:, h, :], scalar=bac[:, h:h + 1], in1=rsa,
                                                                   op0=ALU.mult, op1=ALU.mult),
                        reads=[r_num[b], r_rsa, r_consts], writes=[r_ma[b]] if h == 0 else (), adds=() if h == 0 else [r_ma[b]])

        def b_mm(i, oc):
            b = i % 2
            pso = PS(2 + (oc % 2))

            def mmo(e, oc=oc, pso=pso, b=b):
                last = None
                for h in range(4):
                    last = e.matmul(pso, lhsT=woa[:, h, oc * 128:(oc + 1) * 128], rhs=ma[b][:, h, :], start=(h == 0), stop=False)
                for c in range(4):
                    last = e.matmul(pso, lhsT=woc[:, c, oc * 128:(oc + 1) * 128], rhs=mcl[b][:, c, :], start=False, stop=(c == 3))
                return last
            pe.run(mmo, reads=[r_ma[b], r_mcl[b], r_w3a], writes=[psres[2 + (oc % 2)]])

        def b_x1(i, oc):
            b = i % 2
            pso = PS(2 + (oc % 2))
            dve.run(lambda e, oc=oc, pso=pso, b=b: e.scalar_tensor_tensor(out=x1[b][:, oc, :], in0=pso, scalar=g1c[:, oc:oc + 1], in1=xr[b][:, oc, :],
                                                                          op0=ALU.mult, op1=ALU.add),
                    reads=[psres[2 + (oc % 2)], r_xr[b], r_mod], writes=[r_x1[b]] if oc == 0 else (), adds=() if oc == 0 else [r_x1[b]])

        def c_sq(i):
            b = i % 2
            dve.run(lambda e, b=b: e.tensor_tensor(out=sq, in0=x1[b], in1=x1[b], op=ALU.mult), reads=[r_x1[b]], writes=[r_sq])

        def c_ss(i):
            def mm(e):
                last = None
                for kc in range(8):
                    last = e.matmul(PS(4), lhsT=onesb, rhs=sq[:, kc, :], start=(kc == 0), stop=(kc == 7))
                return last
            pe.run(mm, reads=[r_sq, r_ones], writes=[psres[4]])

        def c_rstd(i):
            act.run(lambda e: e.activation(out=rstd, in_=PS(4), func=AF.Ln, bias=c_eps, scale=1.0 / D), reads=[psres[4], r_cst], writes=[r_rstd])
            act.run(lambda e: e.activation(out=rstd, in_=rstd, func=AF.Exp, bias=c_zero, scale=-0.5), writes=[r_rstd])

        def c_u(i, kcs):
            b = i % 2
            for kc in kcs:
                tb = kc % 2
                dve.run(lambda e, kc=kc, tb=tb, b=b: e.tensor_tensor(out=tmpf[tb], in0=x1[b][:, kc, :], in1=rstd, op=ALU.mult),
                        reads=[r_x1[b], r_rstd], writes=[r_tmpf[tb]])
                act.run(lambda e, kc=kc, tb=tb, b=b: e.activation(out=h2[b][:, kc, :], in_=tmpf[tb], func=AF.Identity, bias=sh2[:, kc:kc + 1], scale=gm2[:, kc:kc + 1]),
                        reads=[r_tmpf[tb], r_mod], writes=[r_h2[b]] if kc == 0 else (), adds=() if kc == 0 else [r_h2[b]])

        load_3a(0)
        a_load_den(0); a_rec(0); a_mul(0); a_sq(0); a_mma(0); a_ln(0); a_ma(0, range(4))
        for i in range(NOWN):
            b = i % 2
            cols = slice(i * TN, (i + 1) * TN)
            n = i + 1
            nxt = n < NOWN
            if nxt:
                load_3a(n)
                a_load_den(n)
            b_mm(i, 0); b_mm(i, 1)
            if nxt:
                a_rec(n)
            b_x1(i, 0); b_mm(i, 2)
            if nxt:
                a_mul(n)
            b_x1(i, 1); b_mm(i, 3)
            if nxt:
                a_sq(n)
            b_x1(i, 2); b_mm(i, 4)
            if nxt:
                a_mma(n)
            b_x1(i, 3); b_mm(i, 5)
            if nxt:
                a_ln(n)
            b_x1(i, 4); b_mm(i, 6); b_x1(i, 5); b_mm(i, 7)
            if nxt:
                a_ma(n, range(0, 2))
            b_x1(i, 6); b_x1(i, 7)
            c_sq(i)
            if nxt:
                a_ma(n, range(2, 4))
            c_ss(i); c_rstd(i); c_u(i, range(8))
            k.dma("sp", X1_v[:, :, cols], x1[b], r_x1[b], reads=[r_x1[b]], adds=[r_X1])
            k.dma("sp", H2_v[:, :, cols], h2[b], r_h2[b], reads=[r_h2[b]], adds=[r_H2])
        k.barrier()
        ar.reset(pmark)

    phase3a()
    def phase3b():
        w1 = ar.alloc((8, DFF), BF16)
        w2 = ar.alloc((32, D), BF16)
        r_w3b = k.res("w3b", dma=True)
        w1_v = w1_d.rearrange("(kc p) n -> p kc n", p=128)
        w2_v = w2_d.rearrange("(fc p) n -> p fc n", p=128)
        for kc in range(8):
            k.dma("pool", w1[:, kc, :], w1_v[:, kc, :], r_w3b, adds=[r_w3b])
        for f4 in range(8):
            k.dma("pool", w2[:, f4 * 4:(f4 + 1) * 4, :], w2_v[:, f4 * 4:(f4 + 1) * 4, :], r_w3b, adds=[r_w3b])
        h2l = [ar.alloc((8, TN), BF16) for _ in range(2)]
        r_h2l = [k.res("h2l%d" % i, dma=True) for i in range(2)]
        aT = ar.alloc((32, TN), BF16)
        r_aT = [k.res("aT%d" % i) for i in range(32)]
        xo = ar.alloc((8, TN), F32); r_xo = k.res("xo", dma=True)
        rbuf = [ar.alloc((TN,), F32) for _ in range(4)]
        r_rbuf = [k.res("rbuf%d" % i) for i in range(4)]
        outT_v = outT.rearrange("(kc p) t -> p kc t", p=128)
        k.dma("sp", h2l[0], H2_v[:, :, 0:TN], r_h2l[0], reads=[r_H2], writes=[r_h2l[0]])
        for i in range(NOWN):
            b = i % 2
            cols = slice(i * TN, (i + 1) * TN)
            k.dma("sp", xo, X1_v[:, :, cols], r_xo, reads=[r_X1], writes=[r_xo])
            if i + 1 < NOWN:
                k.dma("sp", h2l[1 - b], H2_v[:, :, (i + 1) * TN:(i + 2) * TN], r_h2l[1 - b], reads=[r_H2], writes=[r_h2l[1 - b]])
            for fc in range(32):
                psf = PS(fc % 4)

                def mmf(e, fc=fc, psf=psf, b=b):
                    last = None
                    for kc in range(8):
                        last = e.matmul(psf, lhsT=w1[:, kc, fc * 128:(fc + 1) * 128], rhs=h2l[b][:, kc, :], start=(kc == 0), stop=(kc == 7))
                    return last
                pe.run(mmf, reads=[r_h2l[b], r_w3b], writes=[psres[fc % 4]])
                rb = fc % 4
                act.run(lambda e, psf=psf, rb=rb: e.activation(out=rbuf[rb], in_=psf, func=AF.Relu, bias=c_zero, scale=1.0),
                        reads=[psres[fc % 4], r_cst], writes=[r_rbuf[rb]])
                dve.run(lambda e, fc=fc, rb=rb: e.tensor_tensor(out=aT[:, fc, :], in0=rbuf[rb], in1=rbuf[rb], op=ALU.mult),
                        reads=[r_rbuf[rb]], writes=[r_aT[fc]])
            for oc in range(8):
                ps2 = PS(4 + (oc % 4))

                def mm2(e, oc=oc, ps2=ps2):
                    last = None
                    for fc in range(32):
                        last = e.matmul(ps2, lhsT=w2[:, fc, oc * 128:(oc + 1) * 128], rhs=aT[:, fc, :], start=(fc == 0), stop=(fc == 31))
                    return last
                pe.run(mm2, reads=r_aT + [r_w3b], writes=[psres[4 + (oc % 4)]])
                dve.run(lambda e, oc=oc, ps2=ps2: e.scalar_tensor_tensor(out=xo[:, oc, :], in0=ps2, scalar=g2c[:, oc:oc + 1], in1=xo[:, oc, :],
                                                                         op0=ALU.mult, op1=ALU.add),
                        reads=[psres[4 + (oc % 4)], r_mod], writes=[r_xo])
            k.dma("sp", outT_v[:, :, cols], xo, r_xo, reads=[r_xo])
    phase3b()
    k.barrier()

    block = E(nc.Block())

    @block.tensor
    def _(e):
        for f in k.pe.q:
            f(e)

    @block.scalar
    def _(e):
        for f in k.act.q:
            f(e)

    @block.vector
    def _(e):
        for f in k.dve.q:
            f(e)

    @block.gpsimd
    def _(e):
        for f in k.pool.q:
            f(e)

    @block.sync
    def _(e):
        for f in k.sp.q:
            f(e)

    es.close()
    return nc


def _col(v, p=128):
    v = np.asarray(v, np.float32).reshape(-1)
    return np.ascontiguousarray(v.reshape(-1, p).T)


def prep_inputs(x, c, w_ada, b_ada, norm1_g, w_in, q_norm_g, k_norm_g, b_f, conv_w, conv_b, conv_ln_g, conv_ln_b,
                beta_attn, beta_conv, w_out, norm2_g, w_ff1, w_ff2):
    f32 = np.float32
    x = np.asarray(x, f32)
    w_in0 = np.asarray(w_in, f32)[0]
    shared = {
        "wada": np.ascontiguousarray(np.asarray(w_ada, f32)[0]),
        "badacol": _col(np.asarray(b_ada)[0]),
        "n1gcol": _col(np.asarray(norm1_g)[0]),
        "n2gcol": _col(np.asarray(norm2_g)[0]),
        "wq": np.ascontiguousarray(w_in0[:, 0:512]),
        "wk": np.ascontiguousarray(w_in0[:, 512:1024]),
        "wv": np.ascontiguousarray(w_in0[:, 1024:1536]),
        "wfg": np.ascontiguousarray(w_in0[:, 1536:1544]),
        "wc": np.ascontiguousarray(w_in0[:, 1544:2568]),
        "qgcol": np.ascontiguousarray(np.tile(np.asarray(q_norm_g, f32)[0], 2).reshape(128, 1)),
        "kgcol": np.ascontiguousarray(np.tile(np.asarray(k_norm_g, f32)[0], 2).reshape(128, 1)),
        "bfrep": np.ascontiguousarray(np.tile(np.asarray(b_f, f32)[0].reshape(1, 8), (128, 4))),
        "cwcol": np.ascontiguousarray(np.asarray(conv_w, f32)[0].reshape(KW, 4, 128).transpose(2, 1, 0).reshape(128, 4 * KW)),
        "cbcol": _col(np.asarray(conv_b)[0]),
        "clgcol": _col(np.asarray(conv_ln_g)[0]),
        "clbcol": _col(np.asarray(conv_ln_b)[0]),
        "bccol": _col(np.asarray(beta_conv)[0]),
        "bapcol": _col(np.asarray(beta_attn)[0]),
        "wo": np.ascontiguousarray(np.asarray(w_out, f32)[0]),
        "w1": np.ascontiguousarray(np.asarray(w_ff1, f32)[0]),
        "w2": np.ascontiguousarray(np.asarray(w_ff2, f32)[0]),
        "ident": np.eye(128, dtype=f32),
        "tri": np.triu(np.ones((128, 128), f32)),
    }
    kk = np.arange(128)[:, None, None] + 128 * np.arange(4)[None, :, None]
    qq = np.arange(TN)[None, None, :]
    shared["mtri"] = np.where(kk > qq, NEG, 0.0).astype(f32).reshape(128, 4 * TN).astype(ml_dtypes.bfloat16)
    xTb = [np.ascontiguousarray(x[b].T) for b in range(2)]
    ccols = [_col(np.asarray(c, f32)[b]) for b in range(2)]
    in_maps = []
    for core in range(8):
        b, j = core // 4, core % 4
        tiles = own_tiles(j)
        m = dict(shared)
        m["xT"] = xTb[b]
        m["ccol"] = ccols[b]
        xq = np.zeros((D, NOWN, TNH), f32)
        dsel = np.zeros((128, 32, 128), f32)
        cmask = np.zeros((128, NOWN, NT), f32)
        oh = np.zeros((128, NOWN, H, NT), f32)
        hs = np.ones((128, NOWN), f32)
        for i, t in enumerate(tiles):
            xq[:, i, 0:TN] = xTb[b][:, t * TN:(t + 1) * TN]
            if t > 0:
                xq[:, i, TN:TNH] = xTb[b][:, t * TN - HALO:t * TN]
            else:
                hs[:, i] = 0.0
            cmask[:, i, t + 1:] = NEG
            oh[:, i, :, t] = 1.0
            for r in range(4):
                if rbase_of(i) + r == t:
                    dsel[:, i * 4 + r, :] = np.eye(128, dtype=f32)
        m["xqT"] = np.ascontiguousarray(xq.reshape(D, NOWN * TNH))
        m["diagsel"] = dsel.reshape(128, 32 * 128).astype(ml_dtypes.bfloat16)
        m["cmask"] = cmask.reshape(128, NOWN * NT)
        m["ohrep"] = oh.reshape(128, NOWN * H * NT)
        m["haloscale"] = hs
        in_maps.append(m)
    return in_maps


_NC_CACHE = {}


def kernel(**inputs):
    in_maps = prep_inputs(**inputs)
    if "nc" not in _NC_CACHE:
        _NC_CACHE["nc"] = build_nc()
    nc = _NC_CACHE["nc"]
    res = run_bass_kernel_spmd(nc, in_maps, core_ids=list(range(8)))
    out = np.zeros((2, S, D), np.float32)
    for core in range(8):
        b, j = core // 4, core % 4
        oT = np.asarray(res.results[core]["outT"])
        for i, t in enumerate(own_tiles(j)):
            out[b, t * TN:(t + 1) * TN, :] = oT[:, i * TN:(i + 1) * TN].T
    return out
```
